# Optimizing a Trainium2 kernel written in Bass

```python
import math
import jax, jax.numpy as jnp
from jax import lax
import numpy as np

D_MODEL = 1024
BATCH = 4
SEQ = 4096
DEPTH = 1

PLE_DIM = 256
D_FF = 2816
C_CONV = D_MODEL
CONV_WIDTH = 31
HEAD_DIM = 128
HEADS_PER_GROUP = 4
GROUPS = ((128, 1), (512, 4), (2048, 16))
N_GROUPS = len(GROUPS)
N_HEADS = HEADS_PER_GROUP * N_GROUPS
ATTN_W = N_HEADS * HEAD_DIM
ATTN_OUT_W = HEADS_PER_GROUP * HEAD_DIM
Q_BLOCK = 128
ROPE_THETA = 10000.0
EPS = 1e-6
NEG = -1e30
IN_SPLITS = (2 * C_CONV, ATTN_W, ATTN_W, ATTN_W, 2 * D_MODEL)
IN_W = sum(IN_SPLITS)

kernel_name = "hybrid_gated_conv_dilated_attn_block"


def rms_norm(x, g):
    xf = x.astype(jnp.float32)
    y = xf * lax.rsqrt(jnp.mean(xf * xf, axis=-1, keepdims=True) + EPS)
    return (y * g.astype(jnp.float32)).astype(x.dtype)


def layer_norm(x, g, b):
    xf = x.astype(jnp.float32)
    mu = jnp.mean(xf, axis=-1, keepdims=True)
    var = jnp.mean(jnp.square(xf - mu), axis=-1, keepdims=True)
    y = (xf - mu) * lax.rsqrt(var + EPS)
    return (y * g.astype(jnp.float32) + b.astype(jnp.float32)).astype(x.dtype)


def swiglu(x, w_gate, w_up, w_down):
    return (jax.nn.silu(x @ w_gate) * (x @ w_up)) @ w_down


def rotary(t, cos, sin):
    t1, t2 = jnp.split(t, 2, axis=-1)
    return jnp.concatenate([t1 * cos - t2 * sin, t2 * cos + t1 * sin], axis=-1).astype(t.dtype)


def dilated_window_attention(q, k, v, dilation, steps):
    b, s, h, hd = q.shape
    L = s // dilation
    nb = -(-L // Q_BLOCK)
    Lp = nb * Q_BLOCK

    def to_strided(t):
        t = t.reshape(b, L, dilation, h, hd).transpose(0, 2, 3, 1, 4)
        t = jnp.pad(t, ((0, 0), (0, 0), (0, 0), (0, Lp - L), (0, 0)))
        return t.reshape(b, dilation, h, nb, Q_BLOCK, hd)

    def with_prev(t):
        prev = jnp.concatenate([jnp.zeros_like(t[:, :, :, :1]), t[:, :, :, :-1]], axis=3)
        return jnp.concatenate([prev, t], axis=4)

    qb = to_strided(q)
    kw = with_prev(to_strided(k))
    vw = with_prev(to_strided(v))
    scores = jnp.einsum('bdhnqc,bdhnkc->bdhnqk', qb, kw).astype(jnp.float32) * (hd ** -0.5)
    qi = jnp.arange(Q_BLOCK)[:, None]
    ki = jnp.arange(2 * Q_BLOCK)[None, :]
    dist = qi + Q_BLOCK - ki
    k_abs = jnp.arange(nb)[:, None, None] * Q_BLOCK - Q_BLOCK + ki[None]
    valid = (dist >= 0)[None] & (dist <= steps)[None] & (k_abs >= 0)
    scores = jnp.where(valid, scores, NEG)
    m = jnp.max(scores, axis=-1, keepdims=True)
    e = jnp.exp(scores - m)
    den = jnp.sum(e, axis=-1)
    o = jnp.einsum('bdhnqk,bdhnkc->bdhnqc', e, vw.astype(jnp.float32)) / den[..., None]
    lse = m[..., 0] + jnp.log(den)
    o = o.reshape(b, dilation, h, Lp, hd)[:, :, :, :L].transpose(0, 3, 1, 2, 4).reshape(b, s, h, hd)
    lse = lse.reshape(b, dilation, h, Lp)[..., :L].transpose(0, 3, 1, 2).reshape(b, s, h)
    return o.astype(q.dtype), lse


def conformer_conv(a_glu, w_dw, b_dw, g_ln, b_ln, w_out):
    a, gate = jnp.split(a_glu, 2, axis=-1)
    a = a * jax.nn.sigmoid(gate)
    a = lax.conv_general_dilated(
        a, w_dw[:, None, :].astype(a.dtype), window_strides=(1,),
        padding=[(CONV_WIDTH - 1, 0)],
        dimension_numbers=('NWC', 'WIO', 'NWC'),
        feature_group_count=C_CONV) + b_dw
    a = jax.nn.silu(layer_norm(a, g_ln, b_ln))
    return a @ w_out


def setup_inputs(seed: int = 0) -> dict:
    key = jax.random.key(seed)
    ks = iter(jax.random.split(key, 32))
    f32 = jnp.float32

    def w(shape, fan_in, scale=1.0):
        return jax.random.normal(next(ks), shape, f32) * (scale * fan_in ** -0.5)

    def gain(shape):
        return 1.0 + 0.02 * jax.random.normal(next(ks), shape, f32)

    def bias(shape):
        return 0.02 * jax.random.normal(next(ks), shape, f32)

    x = jax.random.normal(next(ks), (BATCH, SEQ, D_MODEL), f32)
    p = jax.random.normal(next(ks), (DEPTH, BATCH, SEQ, PLE_DIM), f32)
    positions = jnp.broadcast_to(jnp.arange(SEQ, dtype=jnp.int32)[None], (BATCH, SEQ))
    L = DEPTH
    return {
        "x": x, "p": p, "positions": positions,
        "g_ffn1": gain((L, D_MODEL)),
        "w_ffn1_gate": w((L, D_MODEL, D_FF), D_MODEL),
        "w_ffn1_up": w((L, D_MODEL, D_FF), D_MODEL),
        "w_ffn1_down": w((L, D_FF, D_MODEL), D_FF),
        "g_mix": gain((L, D_MODEL)),
        "w_in": w((L, D_MODEL, IN_W), D_MODEL),
        "w_dw": w((L, CONV_WIDTH, C_CONV), CONV_WIDTH),
        "b_dw": bias((L, C_CONV)),
        "g_conv_ln": gain((L, C_CONV)),
        "b_conv_ln": bias((L, C_CONV)),
        "w_conv_out": w((L, C_CONV, D_MODEL), C_CONV),
        "w_attn_out": w((L, ATTN_OUT_W, D_MODEL), ATTN_OUT_W),
        "w_mix_out": w((L, D_MODEL, D_MODEL), D_MODEL),
        "g_ffn2": gain((L, D_MODEL)),
        "w_ffn2_gate": w((L, D_MODEL, D_FF), D_MODEL),
        "w_ffn2_up": w((L, D_MODEL, D_FF), D_MODEL),
        "w_ffn2_down": w((L, D_FF, D_MODEL), D_FF),
        "g_ple": gain((L, D_MODEL)),
        "w_ple_gate": w((L, D_MODEL, D_MODEL), D_MODEL),
        "w_ple_proj": w((L, PLE_DIM, D_MODEL), PLE_DIM),
        "g_final": gain((D_MODEL,)),
    }


def reference(x, p, positions, g_ffn1, w_ffn1_gate, w_ffn1_up, w_ffn1_down, g_mix, w_in,
              w_dw, b_dw, g_conv_ln, b_conv_ln, w_conv_out, w_attn_out, w_mix_out,
              g_ffn2, w_ffn2_gate, w_ffn2_up, w_ffn2_down, g_ple, w_ple_gate, w_ple_proj,
              g_final):
    b, s, _ = x.shape
    inv_freq = ROPE_THETA ** (-jnp.arange(0, HEAD_DIM, 2, dtype=jnp.float32) / HEAD_DIM)
    ang = positions.astype(jnp.float32)[..., None] * inv_freq
    cos = jnp.cos(ang)[:, :, None, :].astype(x.dtype)
    sin = jnp.sin(ang)[:, :, None, :].astype(x.dtype)
    split_pts = list(np.cumsum(IN_SPLITS)[:-1])

    h = x
    for i in range(DEPTH):
        h = h + 0.5 * swiglu(rms_norm(h, g_ffn1[i]), w_ffn1_gate[i], w_ffn1_up[i], w_ffn1_down[i])

        u = rms_norm(h, g_mix[i])
        z = u @ w_in[i]
        a_glu, q, k, v, gate_logits = jnp.split(z, split_pts, axis=-1)

        y_conv = conformer_conv(a_glu, w_dw[i], b_dw[i], g_conv_ln[i], b_conv_ln[i], w_conv_out[i])

        q = rotary(q.reshape(b, s, N_HEADS, HEAD_DIM), cos, sin)
        k = rotary(k.reshape(b, s, N_HEADS, HEAD_DIM), cos, sin)
        v = v.reshape(b, s, N_HEADS, HEAD_DIM)
        outs, lses = [], []
        for gi, (window, dilation) in enumerate(GROUPS):
            hs = slice(gi * HEADS_PER_GROUP, (gi + 1) * HEADS_PER_GROUP)
            o_g, lse_g = dilated_window_attention(q[:, :, hs], k[:, :, hs], v[:, :, hs],
                                                  dilation, window // dilation)
            outs.append(o_g)
            lses.append(lse_g)
        alpha = jax.nn.softmax(jnp.stack(lses, axis=0), axis=0)
        o = jnp.sum(alpha[..., None].astype(q.dtype) * jnp.stack(outs, axis=0), axis=0)
        y_attn = o.reshape(b, s, ATTN_OUT_W) @ w_attn_out[i]

        g_a, g_b = jnp.split(jax.nn.sigmoid(gate_logits), 2, axis=-1)
        h = h + (g_a * y_conv + g_b * y_attn) @ w_mix_out[i]

        h = h + 0.5 * swiglu(rms_norm(h, g_ffn2[i]), w_ffn2_gate[i], w_ffn2_up[i], w_ffn2_down[i])

        gate = jax.nn.sigmoid(rms_norm(h, g_ple[i]) @ w_ple_gate[i])
        h = h + gate * (p[i] @ w_ple_proj[i])

    return rms_norm(h, g_final)
```

```python
import math
from contextlib import ExitStack

import numpy as np
import concourse.bass as bass
import concourse.mybir as mybir
from concourse.bass_utils import run_bass_kernel_spmd

F32 = mybir.dt.float32
BF16 = mybir.dt.bfloat16
I32 = mybir.dt.int32
ALU = mybir.AluOpType
AF = mybir.ActivationFunctionType

D = 1024
NT = 2048
NB = 4
BL = 512
DFF = 2816
NJ = DFF // 128
INW = 8704
EPS = 1e-6
NEGM = -30000.0
SCALE = 128 ** -0.5
GROUPS = ((1, 16), (4, 4), (16, 1))
TWO_PI = 2.0 * math.pi


class Op:
    __slots__ = ("eng", "fn", "deps", "is_dma", "grp", "val", "need_inc", "count")


class Sched:
    ENGS = ("pe", "act", "dve", "pool", "sp")

    def __init__(self, nc):
        self.nc = nc
        self.ops = {e: [] for e in self.ENGS}
        self.state = {}
        self.groups = {}
        self.pending_dma = []
        self.pending_bar = {e: set() for e in self.ENGS}

    def op(self, eng, fn, reads=(), writes=(), dma=None):
        o = Op()
        o.eng, o.fn, o.deps, o.is_dma, o.grp = eng, fn, set(), dma is not None, dma
        o.need_inc, o.count, o.val = False, 0, 0
        st = self.state
        ps_reads = [k for k in reads if isinstance(k, tuple) and k[0] == "ps"]
        if ps_reads:
            reads = [k for k in reads if k not in ps_reads]
            writes = list(writes) + ps_reads
        for k in reads:
            s = st.get(k)
            if s and s[0] is not None:
                o.deps.add(s[0])
        for k in writes:
            s = st.get(k)
            if s:
                if s[0] is not None:
                    o.deps.add(s[0])
                o.deps.update(s[1])
        for k in reads:
            rl = st.setdefault(k, [None, []])[1]
            if not o.is_dma:
                rl[:] = [r for r in rl if r.is_dma or r.eng != eng]
            rl.append(o)
        for k in writes:
            st[k] = [o, []]
        if self.pending_bar[eng]:
            o.deps.update(self.pending_bar[eng])
            self.pending_bar[eng] = set()
        if dma is not None:
            g = self.groups.setdefault(dma, [None, 0])
            g[1] += 1
            o.val = 16 * g[1]
            self.pending_dma.append(o)
        self.ops[eng].append(o)
        return o

    def barrier(self):
        tails = list(self.pending_dma)
        for e in self.ENGS:
            for o in reversed(self.ops[e]):
                if not o.is_dma:
                    tails.append(o)
                    break
        for e in self.ENGS:
            self.pending_bar[e].update(tails)
        self.pending_dma = []
        self.state = {}

    def finish(self):
        self.barrier()
        self.op("sp", lambda eng: eng.nop())

    def _needs(self, o, d):
        if o.is_dma or d.is_dma or d.eng != o.eng:
            return True
        return o.eng != "pe"

    def emit(self):
        nc = self.nc
        for e in self.ENGS:
            for o in self.ops[e]:
                for d in o.deps:
                    if not d.is_dma and self._needs(o, d):
                        d.need_inc = True
        for e in self.ENGS:
            c = 0
            for o in self.ops[e]:
                if not o.is_dma and o.need_inc:
                    c += 1
                    o.count = c
        with ExitStack() as es:
            sems = {e: es.enter_context(nc.semaphore("s_" + e)) for e in self.ENGS}
            for g in self.groups:
                self.groups[g][0] = es.enter_context(nc.semaphore("d_" + g))
            block = es.enter_context(nc.Block())

            def make(e):
                def body(eng):
                    waited = {}
                    for o in self.ops[e]:
                        ws = {}
                        for d in o.deps:
                            if not self._needs(o, d):
                                continue
                            if d.is_dma:
                                key, sem, val = "d_" + d.grp, self.groups[d.grp][0], d.val
                            else:
                                key, sem, val = "s_" + d.eng, sems[d.eng], d.count
                            if waited.get(key, 0) >= val:
                                continue
                            if key not in ws or ws[key][1] < val:
                                ws[key] = (sem, val)
                        for key, (sem, val) in ws.items():
                            eng.wait_ge(sem, val)
                            waited[key] = val
                        inst = o.fn(eng)
                        if o.is_dma:
                            inst.then_inc(self.groups[o.grp][0], 16)
                        elif o.need_inc:
                            inst.then_inc(sems[e], 1)
                return body

            block.tensor(make("pe"))
            block.scalar(make("act"))
            block.vector(make("dve"))
            block.gpsimd(make("pool"))
            block.sync(make("sp"))


class Arena:
    def __init__(self, nc, limit):
        self.nc, self.off, self.limit, self.n = nc, 16640, limit, 0

    def alloc(self, shape, dt):
        esz = 2 if dt == BF16 else 4
        size = esz
        for s in shape[1:]:
            size *= s
        size = (size + 63) // 64 * 64
        t = self.nc.alloc_sbuf_tensor_at("t%d" % self.n, list(shape), dt, offset=self.off)
        self.n += 1
        self.off += size
        assert self.off <= self.limit, ("SBUF overflow", self.off)
        return t


class Ring:
    def __init__(self, arena, name, n, shape, dt):
        self.t = [arena.alloc(shape, dt) for _ in range(n)]
        self.name, self.i = name, 0

    def next(self):
        i = self.i % len(self.t)
        self.i += 1
        return self.t[i], (self.name, i)


def build_program(dbg=None):
    nc = bass.Bass("TRN2", target_bir_lowering=False)
    try:
        _build_body(nc, dbg)
    except Exception as ex:
        if type(ex).__name__ != "_Stop":
            raise
    return nc


def _build_body(nc, dbg):
    S = None

    def din(name, shape, dt=F32):
        return nc.dram_tensor(name, list(shape), dt, kind="ExternalInput").ap()

    xo = din("xo", [D, NT])
    xh = din("xh", [D, NT])
    pT = din("pT", [256, NT])
    pos = din("pos", [128, 2 * NT], I32)
    w1g, w1u, w1d = din("w1g", [D, DFF]), din("w1u", [D, DFF]), din("w1d", [DFF, D])
    w2g, w2u, w2d = din("w2g", [D, DFF]), din("w2u", [D, DFF]), din("w2d", [DFF, D])
    w_in = din("w_in", [D, INW])
    w_co, w_ao, w_mo = din("w_co", [D, D]), din("w_ao", [512, D]), din("w_mo", [D, D])
    w_pg, w_pp = din("w_pg", [D, D]), din("w_pp", [256, D])
    gains_d = din("gains", [128, 8, 8])
    wdw_d = din("wdw", [128, 8, 31])
    cst_d = din("cst", [128, 4, 128])
    msk_d = din("msk", [128, 2, 256])
    sm_d = din("sm", [128, 4])
    yT = nc.dram_tensor("yT", [D, NT], F32, kind="ExternalOutput").ap()
    khd = nc.dram_tensor("khd", [12, 128, NT], BF16, kind="Internal").ap()
    vhd = nc.dram_tensor("vhd", [12, 128, 16, 128], BF16, kind="Internal").ap()
    dbg_out = None
    if dbg:
        dbg_out = nc.dram_tensor("dbg", [D, NT], F32, kind="ExternalOutput").ap()

    S = Sched(nc)
    A = Arena(nc, 229376)
    PS = [nc.alloc_psum_tensor("ps%d" % i, [128, 512], F32) for i in range(8)]
    ps_pool = {"banks": list(range(8)), "i": 0}

    def set_pool(banks):
        ps_pool["banks"] = list(banks)
        ps_pool["i"] = 0

    def ps_next():
        bk = ps_pool["banks"]
        i = bk[ps_pool["i"] % len(bk)]
        ps_pool["i"] += 1
        return PS[i], ("ps", i)

    gains = A.alloc([128, 8, 8], F32)
    wdw = A.alloc([128, 8, 31], F32)
    sm = A.alloc([128, 4], F32)
    ident = A.alloc([128, 128], BF16)
    perm = A.alloc([128, 128], BF16)
    ones = A.alloc([128, 128], BF16)
    mask2 = A.alloc([128, 256], BF16)
    maskH = A.alloc([128, 256], BF16)
    ahalo = A.alloc([128, 8, 32], F32)
    H = A.alloc([128, 8, NT], F32)
    U = A.alloc([128, 8, NT], BF16)
    base_mark = A.off

    G_FFN1, G_MIX, G_CLN, B_CLN, B_DW, G_FFN2, G_PLE, G_FIN = range(8)
    invf, sgn, epsc = sm[:, 0:1], sm[:, 1:2], sm[:, 2:3]

    def sp_load(dst, src, key, grp):
        S.op("sp", lambda e: e.dma_start(out=dst, in_=src), writes=[key], dma=grp)

    def pool_load(dst, src, key, grp):
        S.op("pool", lambda e: e.dma_start(out=dst, in_=src), writes=[key], dma=grp)

    sp_load(gains[:], gains_d, "gains", "c0")
    sp_load(wdw[:], wdw_d, "wdw", "c0")
    sp_load(sm[:], sm_d, "sm", "c0")
    pool_load(ident[:], cst_d[:, 0, :], "ident", "c1")
    pool_load(perm[:], cst_d[:, 1, :], "perm", "c1")
    pool_load(ones[:], cst_d[:, 2, :], "ones", "c1")
    pool_load(mask2[:], msk_d[:, 0, :], "mask2", "c1")
    pool_load(maskH[:], msk_d[:, 1, :], "maskH", "c1")
    S.barrier()

    def blk(b):
        return slice(b * BL, (b + 1) * BL)

    import os
    STOP = os.environ.get("KSTOP", "")

    class _Stop(Exception):
        pass

    def checkpoint(name):
        if STOP == name:
            yv_ = yT.rearrange("(c p) t -> p c t", p=128)
            for c in range(8):
                S.op("sp", lambda e, c=c: e.dma_start(out=yv_[:, c, :], in_=H[:, c, :]),
                     reads=[("H", c, b) for b in range(NB)], dma="yout")
            S.finish()
            S.emit()
            raise _Stop()

    def mm(out, lhsT, rhs, start, stop, reads, writes):
        S.op("pe", lambda e: e.matmul(out, lhsT=lhsT, rhs=rhs, start=start, stop=stop), reads, writes)

    def act(out, in_, func, reads, writes, bias=None, scale=None):
        kw = {}
        if bias is not None:
            kw["bias"] = bias
        if scale is not None:
            kw["scale"] = scale
        S.op("act", lambda e: e.activation(out=out, in_=in_, func=func, **kw), reads, writes)

    def tt(eng, out, in0, in1, op, reads, writes):
        S.op(eng, lambda e: e.tensor_tensor(out=out, in0=in0, in1=in1, op=op), reads, writes)

    def stt(eng, out, in0, scalar, in1, op0, op1, reads, writes):
        S.op(eng, lambda e: e.scalar_tensor_tensor(out=out, in0=in0, scalar=scalar, in1=in1, op0=op0, op1=op1), reads, writes)

    def ts(eng, out, in0, s1, s2, op0, op1, reads, writes):
        if s2 is None:
            S.op(eng, lambda e: e.tensor_scalar(out=out, in0=in0, scalar1=s1, scalar2=None, op0=op0), reads, writes)
        else:
            S.op(eng, lambda e: e.tensor_scalar(out=out, in0=in0, scalar1=s1, scalar2=s2, op0=op0, op1=op1), reads, writes)

    def cp(eng, out, in_, reads, writes):
        if eng == "act":
            S.op(eng, lambda e: e.copy(out=out, in_=in_), reads, writes)
        else:
            S.op(eng, lambda e: e.tensor_copy(out=out, in_=in_), reads, writes)

    def load_xT(src):
        v = src.rearrange("(c p) t -> p c t", p=128)
        for c in range(8):
            S.op("sp", lambda e, c=c: e.dma_start(out=H[:, c, :], in_=v[:, c, :]),
                 writes=[("H", c, b) for b in range(NB)], dma="xin%d" % c)

    def rmsnorm_to_U(which, sq_ring, rt_ring):
        for b in range(NB):
            ps, psk = ps_next()
            for c in range(8):
                sq, sqk = sq_ring.next()
                act(sq[:], H[:, c, blk(b)], AF.Square, [("H", c, b)], [sqk])
                mm(ps[:], ones[:], sq[:], c == 0, c == 7, [sqk, "ones"], [psk])
            rt, rtk = rt_ring.next()
            act(rt[:], ps[:], AF.Sqrt, [psk, "sm"], [rtk], bias=epsc, scale=1.0 / D)
            S.op("dve", lambda e, rt=rt: e.reciprocal(out=rt[:], in_=rt[:]), [rtk], [rtk])
            for c in range(8):
                stt("dve", U[:, c, blk(b)], H[:, c, blk(b)], gains[:, which, c:c + 1], rt[:],
                    ALU.mult, ALU.mult, [("H", c, b), rtk, "gains"], [("U", c, b)])

    def ffn(wg_d, wu_d, wd_d, gain_idx):
        mark = A.off
        sq_ring = Ring(A, "sq", 2, [128, BL], BF16)
        rt_ring = Ring(A, "rt", 2, [128, BL], F32)
        st_ring = Ring(A, "st", 2, [128, BL], F32)
        ACTB = A.alloc([128, 8, NT], BF16)
        wring = Ring(A, "w", 6, [128, 4096], BF16)
        rmsnorm_to_U(gain_idx, sq_ring, rt_ring)
        wgv = wg_d.rearrange("(c p) n -> p c n", p=128)
        wuv = wu_d.rearrange("(c p) n -> p c n", p=128)
        wdv = wd_d.rearrange("(j p) n -> p j n", p=128)
        for (j0, j1) in ((0, 8), (8, 16), (16, 22)):
            nj = j1 - j0
            for gj in range(j0, j1, 4):
                n4 = min(4, j1 - gj)
                wgt, wgk = wring.next()
                wgt = wgt[:, 0:8 * n4 * 128].rearrange("p (c n) -> p c n", c=8)
                pool_load(wgt, wgv[:, :, gj * 128:(gj + n4) * 128], wgk, "%s%d" % wgk)
                wut, wuk = wring.next()
                wut = wut[:, 0:8 * n4 * 128].rearrange("p (c n) -> p c n", c=8)
                pool_load(wut, wuv[:, :, gj * 128:(gj + n4) * 128], wuk, "%s%d" % wuk)
                for jl in range(n4):
                    jj = gj + jl - j0
                    cs = slice(jl * 128, (jl + 1) * 128)
                    for b in range(NB):
                        pg, pgk = ps_next()
                        pu, puk = ps_next()
                        for c in range(8):
                            mm(pg[:], wgt[:, c, cs], U[:, c, blk(b)], c == 0, c == 7, [wgk, ("U", c, b)], [pgk])
                        for c in range(8):
                            mm(pu[:], wut[:, c, cs], U[:, c, blk(b)], c == 0, c == 7, [wuk, ("U", c, b)], [puk])
                        st, stk = st_ring.next()
                        act(st[:], pg[:], AF.Silu, [pgk], [stk])
                        tt("dve", ACTB[:, jj, blk(b)], st[:], pu[:], ALU.mult, [stk, puk], [("A", jj, b)])
            wds = []
            for gj in range(j0, j1, 4):
                n4 = min(4, j1 - gj)
                wdt, wdk = wring.next()
                wdt = wdt[:, 0:n4 * 1024].rearrange("p (j n) -> p j n", j=n4)
                pool_load(wdt, wdv[:, gj:gj + n4, :], wdk, "%s%d" % wdk)
                wds.append((wdt, wdk))
            for c in range(8):
                for b in range(NB):
                    ps, psk = ps_next()
                    for jj in range(nj):
                        wdt, wdk = wds[jj // 4]
                        mm(ps[:], wdt[:, jj % 4, c * 128:(c + 1) * 128], ACTB[:, jj, blk(b)],
                           jj == 0, jj == nj - 1, [wdk, ("A", jj, b)], [psk])
                    stt("dve", H[:, c, blk(b)], ps[:], 0.5, H[:, c, blk(b)], ALU.mult, ALU.add,
                        [psk, ("H", c, b)], [("H", c, b)])
        S.barrier()
        A.off = mark

    def build_tables(cos2, sinS, pcol0, tmp_f, tmp_i, tmp_k):
        sp_load(tmp_i[:], pos[:, pcol0:pcol0 + NT], "tmp_i", "tab")
        for tab, key, shift in ((sinS, "sinS", 0.0), (cos2, "cos2", math.pi / 2)):
            cp("dve", tab[:], tmp_i[:], ["tmp_i"], [key])
            ts("dve", tab[:], tab[:], invf, shift, ALU.mult, ALU.add, [key, "sm"], [key])
            ts("dve", tmp_f[:], tab[:], 1.0 / TWO_PI, None, ALU.mult, None, [key], ["tmp_f"])
            cp("dve", tmp_k[:], tmp_f[:], ["tmp_f"], ["tmp_k"])
            cp("dve", tmp_f[:], tmp_k[:], ["tmp_k"], ["tmp_f"])
            stt("dve", tab[:], tmp_f[:], -TWO_PI, tab[:], ALU.mult, ALU.add, ["tmp_f", key], [key])
            ts("dve", tmp_f[:], tab[:], math.pi, -TWO_PI, ALU.is_gt, ALU.mult, [key], ["tmp_f"])
            tt("dve", tab[:], tab[:], tmp_f[:], ALU.add, [key, "tmp_f"], [key])
            ts("dve", tmp_f[:], tab[:], -math.pi, TWO_PI, ALU.is_lt, ALU.mult, [key], ["tmp_f"])
            tt("dve", tab[:], tab[:], tmp_f[:], ALU.add, [key, "tmp_f"], [key])
            act(tab[:], tab[:], AF.Sin, [key], [key])
        ts("dve", sinS[:], sinS[:], sgn, None, ALU.mult, None, ["sinS", "sm"], ["sinS"])

    def proj_rot(wt, wk, dst, dst_key, t0, ntok, cos2, sinS, qb_ring, t_ring):
        for o0 in range(0, ntok, BL):
            n = min(BL, ntok - o0)
            b = (t0 + o0) // BL
            tsl = slice(t0 + o0, t0 + o0 + n)
            pz, pzk = ps_next()
            for c in range(8):
                mm(pz[:, 0:n], wt[:, c, :], U[:, c, tsl], c == 0, c == 7, [wk, ("U", c, b)], [pzk])
            qb, qbk = qb_ring.next()
            act(qb[:, 0:n], pz[:, 0:n], AF.Copy, [pzk], [qbk])
            pw, pwk = ps_next()
            mm(pw[:, 0:n], perm[:], qb[:, 0:n], True, True, [qbk, "perm"], [pwk])
            t1, t1k = t_ring.next()
            tt("dve", t1[:, 0:n], pz[:, 0:n], cos2[:, tsl], ALU.mult, [pzk, "cos2"], [t1k])
            t2, t2k = t_ring.next()
            tt("dve", t2[:, 0:n], pw[:, 0:n], sinS[:, tsl], ALU.mult, [pwk, "sinS"], [t2k])
            tt("dve", dst[:, o0:o0 + n], t1[:, 0:n], t2[:, 0:n], ALU.add, [t1k, t2k], [(dst_key, o0 // BL)])

    def proj_v(wt, wk, dst, dst_key, tile_starts, d, vring_keys=None):
        for i, st0 in enumerate(tile_starts):
            pv, pvk = ps_next()
            bs = sorted({(st0 + d * q) // BL for q in (0, 127)})
            bs = list(range(bs[0], bs[-1] + 1))
            for c in range(8):
                mm(pv[:, 0:128], U[:, c, st0:st0 + 127 * d + 1:d], wt[:, c, :],
                   c == 0, c == 7, [wk] + [("U", c, b) for b in bs], [pvk])
            cp("act" if i % 2 else "dve", dst[:, i, :], pv[:, 0:128], [pvk], [(dst_key, i)])

    w_in_v = w_in.rearrange("(c p) n -> p c n", p=128)

    def load_w_chunk(ring, col0, ncol=128):
        wt, wk = ring.next()
        wt = wt[:, 0:8 * ncol].rearrange("p (c n) -> p c n", c=8)
        pool_load(wt, w_in_v[:, :, col0:col0 + ncol], wk, "%s%d" % wk)
        return wt, wk

    QOFF, KOFF, VOFF, GOFF = 2048, 2048 + 1536, 2048 + 3072, 2048 + 4608

    checkpoint("init")
    load_xT(xh)
    checkpoint("load")
    ffn(w1g, w1u, w1d, G_FFN1)
    checkpoint("ffn0")
    mark = A.off
    sq_ring = Ring(A, "sq", 2, [128, BL], BF16)
    rt_ring = Ring(A, "rt", 2, [128, BL], F32)
    rmsnorm_to_U(G_MIX, sq_ring, rt_ring)
    checkpoint("p0a")
    cos2 = A.alloc([128, NT], F32)
    sinS = A.alloc([128, NT], F32)
    tmp_f = A.alloc([128, NT], F32)
    tmp_i = A.alloc([128, NT], I32)
    tmp_k = A.alloc([128, NT], I32)
    build_tables(cos2, sinS, 0, tmp_f, tmp_i, tmp_k)
    checkpoint("p0b")
    wring = Ring(A, "w", int(os.environ.get("KW", "4")), [128, 8 * 128], BF16)
    qb_ring = Ring(A, "qb", 2, [128, BL], BF16)
    t_ring = Ring(A, "t", 4, [128, BL], F32)
    kst_ring = Ring(A, "kst", 2, [128, NT], BF16)
    vst_ring = Ring(A, "vst", 2, [128, 16, 128], BF16)
    for g, (d, nbl) in enumerate(GROUPS):
        nh = 128 * d
        for j in range(4):
            head = 4 * g + j
            if head < int(os.environ.get("KSKIP", "0")):
                continue
            wt, wk = load_w_chunk(wring, KOFF + head * 128)
            kst, kstk = kst_ring.next()
            proj_rot(wt, wk, kst, kstk, NT - nh, nh, cos2, sinS, qb_ring, t_ring)
            if not os.environ.get("KNOSTORE"):
              S.op("sp", lambda e, kst=kst, head=head, nh=nh: e.dma_start(out=khd[head, :, 0:nh], in_=kst[:, 0:nh]),
                 reads=[(kstk, i) for i in range((nh + BL - 1) // BL)], writes=[("khd", head)], dma="hstk%d" % kstk[1])
            if head == 0:
                checkpoint("p0c")
            wt, wk = load_w_chunk(wring, VOFF + head * 128)
            vst, vstk = vst_ring.next()
            starts = [NT - nh + r for r in range(d)]
            proj_v(wt, wk, vst, vstk, starts, d)
            if not os.environ.get("KNOSTORE"):
              S.op("sp", lambda e, vst=vst, head=head, d=d: e.dma_start(out=vhd[head, :, 0:d, :], in_=vst[:, 0:d, :]),
                 reads=[(vstk, i) for i in range(d)], writes=[("vhd", head)], dma="hstv%d" % vstk[1])
            if head == 0:
                checkpoint("p0d")
            if head == 11:
                checkpoint("p0e")
            checkpoint("p0h%d" % head)
    sg_ring = Ring(A, "sg", 2, [128, BL], F32)
    for c in range(8):
        wa, wak = load_w_chunk(wring, c * 128)
        wg_, wgk = load_w_chunk(wring, 1024 + c * 128)
        pa, pak = ps_next()
        pg, pgk = ps_next()
        tsl = slice(NT - 128, NT)
        for k in range(8):
            mm(pa[:, 0:128], wa[:, k, :], U[:, k, tsl], k == 0, k == 7, [wak, ("U", k, 3)], [pak])
        for k in range(8):
            mm(pg[:, 0:128], wg_[:, k, :], U[:, k, tsl], k == 0, k == 7, [wgk, ("U", k, 3)], [pgk])
        sg, sgk = sg_ring.next()
        act(sg[:, 0:128], pg[:, 0:128], AF.Sigmoid, [pgk], [sgk])
        tt("dve", ahalo[:, c, :], sg[:, 96:128], pa[:, 96:128], ALU.mult, [sgk, pak], [("ahalo", c)])
    S.barrier()
    A.off = mark

    checkpoint("p0")
    load_xT(xo)
    ffn(w1g, w1u, w1d, G_FFN1)
    checkpoint("p1")

    mark = A.off
    sq_ring = Ring(A, "sq", 2, [128, BL], BF16)
    rt_ring = Ring(A, "rt", 2, [128, BL], F32)
    rmsnorm_to_U(G_MIX, sq_ring, rt_ring)
    S.barrier()
    A.off = mark
    OT = A.alloc([128, 4, NT], BF16)
    mark2 = A.off

    cos2 = A.alloc([128, NT], F32)
    sinS = A.alloc([128, NT], F32)
    mark3 = A.off
    tmp_f = A.alloc([128, NT], F32)
    tmp_i = A.alloc([128, NT], I32)
    tmp_k = A.alloc([128, NT], I32)
    build_tables(cos2, sinS, NT, tmp_f, tmp_i, tmp_k)
    S.barrier()
    A.off = mark3
    set_pool([0, 1, 2, 3])
    wring = Ring(A, "w", 3, [128, 8 * 128], BF16)
    qb_ring = Ring(A, "qb", 2, [128, BL], BF16)
    t_ring = Ring(A, "t", 4, [128, BL], F32)
    pt_ring = Ring(A, "pt", 3, [128, 256], BF16)
    q_ring = Ring(A, "QT", 2, [128, NT], BF16)
    k_ring = Ring(A, "KT", 2, [128, NT], BF16)
    v_ring = Ring(A, "VT", 2, [128, 16, 128], BF16)
    KHb = A.alloc([128, NT], BF16)
    VHb = A.alloc([128, 16, 128], BF16)
    OACC = A.alloc([128, NT], F32)
    DEN = A.alloc([128, NT], F32)
    ps_half = {"s": 0, "o": 0}

    def half_next(kind):
        i = ps_half[kind] % 2
        ps_half[kind] += 1
        bank = (4 if kind == "s" else 6) + i
        return PS[bank], 0, ("ps", bank)

    for j in range(4):
        for g, (d, nbl) in enumerate(GROUPS):
            head = 4 * g + j
            QTt, qkey = q_ring.next()
            KTt, kkey = k_ring.next()
            VTt, vkey = v_ring.next()
            wt, wk = load_w_chunk(wring, QOFF + head * 128)
            proj_rot(wt, wk, QTt, qkey, 0, NT, cos2, sinS, qb_ring, t_ring)
            wt, wk = load_w_chunk(wring, KOFF + head * 128)
            proj_rot(wt, wk, KTt, kkey, 0, NT, cos2, sinS, qb_ring, t_ring)
            wt, wk = load_w_chunk(wring, VOFF + head * 128)
            starts = [128 * n * d + r for r in range(d) for n in range(nbl)]
            proj_v(wt, wk, VTt, vkey, starts, d)
            S.op("sp", lambda e, head=head, d=d: e.dma_start(out=KHb[:, 0:128 * d], in_=khd[head, :, 0:128 * d]),
                 reads=[("khd", head)], writes=["KH"], dma="hldk")
            S.op("sp", lambda e, head=head, d=d: e.dma_start(out=VHb[:, 0:d, :], in_=vhd[head, :, 0:d, :]),
                 reads=[("vhd", head)], writes=["VH"], dma="hldv")
            blocks = [(r, n) for r in range(d) for n in range(nbl)]

            def emit_scores(r, n):
                s0 = 128 * n * d + r
                cols = slice(s0, s0 + 127 * d + 1, d)
                bs = list(range(s0 // BL, (s0 + 127 * d) // BL + 1))
                qk = [(qkey, b) for b in bs]
                kk = [(kkey, b) for b in bs]
                if n == 0:
                    pcols = slice(r, r + 127 * d + 1, d)
                    kprev, kpk = KHb[:, pcols], ["KH"]
                    msk, mkk = maskH, "maskH"
                else:
                    p0 = 128 * (n - 1) * d + r
                    pcols = slice(p0, p0 + 127 * d + 1, d)
                    pbs = list(range(p0 // BL, (p0 + 127 * d) // BL + 1))
                    kprev, kpk = KTt[:, pcols], [(kkey, b) for b in pbs]
                    msk, mkk = mask2, "mask2"
                pS, so, psk = half_next("s")
                mm(pS[:, so:so + 256], ident[:], msk[:], True, False, ["ident", mkk], [psk])
                mm(pS[:, so:so + 128], kprev, QTt[:, cols], False, False, kpk + qk, [psk])
                mm(pS[:, so + 128:so + 256], KTt[:, cols], QTt[:, cols], False, True, kk + qk, [psk])
                pt, ptk = pt_ring.next()
                act(pt[:], pS[:, so:so + 256], AF.Exp, [psk], [ptk], scale=SCALE)
                return pt, ptk, cols

            def emit_pv(r, n, pt, ptk, cols):
                if n == 0:
                    vprev, vpk = VHb[:, r, :], ["VH"]
                else:
                    vprev, vpk = VTt[:, r * nbl + n - 1, :], [(vkey, r * nbl + n - 1)]
                vcur, vck = VTt[:, r * nbl + n, :], [(vkey, r * nbl + n)]
                pO, oo, pok = half_next("o")
                mm(pO[:, oo:oo + 128], vprev, pt[:, 0:128], True, False, vpk + [ptk], [pok])
                mm(pO[:, oo:oo + 128], vcur, pt[:, 128:256], False, True, vck + [ptk], [pok])
                mm(pO[:, oo + 128:oo + 256], ones[:], pt[:, 0:128], True, False, ["ones", ptk], [pok])
                mm(pO[:, oo + 128:oo + 256], ones[:], pt[:, 128:256], False, True, ["ones", ptk], [pok])
                if g == 0:
                    cp("dve", OACC[:, cols], pO[:, oo:oo + 128], [pok], ["OACC"])
                    cp("dve", DEN[:, cols], pO[:, oo + 128:oo + 256], [pok], ["DEN"])
                else:
                    tt("dve", OACC[:, cols], OACC[:, cols], pO[:, oo:oo + 128], ALU.add, [pok, "OACC"], ["OACC"])
                    tt("dve", DEN[:, cols], DEN[:, cols], pO[:, oo + 128:oo + 256], ALU.add, [pok, "DEN"], ["DEN"])

            pend = emit_scores(*blocks[0])
            for bi in range(len(blocks)):
                nxt = emit_scores(*blocks[bi + 1]) if bi + 1 < len(blocks) else None
                emit_pv(blocks[bi][0], blocks[bi][1], *pend)
                pend = nxt
        S.op("dve", lambda e: e.reciprocal(out=DEN[:], in_=DEN[:]), ["DEN"], ["DEN"])
        tt("dve", OT[:, j, :], OACC[:], DEN[:], ALU.mult, ["OACC", "DEN"], [("OT", j)])
    S.barrier()
    A.off = mark2
    CT = A.alloc([128, 8, NT], BF16)
    mark2 = A.off

    checkpoint("attn")
    HT = 1024
    wring = Ring(A, "w", 4, [128, 8 * 128], BF16)
    sg_ring = Ring(A, "sg", 2, [128, BL], F32)
    sqf_ring = Ring(A, "sqf", 2, [128, BL], BF16)
    CONV = A.alloc([128, 8, HT], F32)
    AEXT = A.alloc([128, 32 + HT], F32)
    st1 = A.alloc([128, HT], F32)
    st2 = A.alloc([128, HT], F32)
    set_pool([4, 5, 6, 7])
    for hf in range(2):
        t0 = hf * HT
        ps1 = [(PS[0], ("ps", 0)), (PS[1], ("ps", 1))]
        ps2 = [(PS[2], ("ps", 2)), (PS[3], ("ps", 3))]
        for c in range(8):
            wa, wak = load_w_chunk(wring, c * 128)
            wg_, wgk = load_w_chunk(wring, 1024 + c * 128)
            cp("act", AEXT[:, 0:32], ahalo[:, c, :], [("ahalo", c)], ["aext_h"])
            for bb in range(2):
                b = hf * 2 + bb
                tsl = slice(t0 + bb * BL, t0 + (bb + 1) * BL)
                pa, pak = ps_next()
                pg, pgk = ps_next()
                for k in range(8):
                    mm(pa[:], wa[:, k, :], U[:, k, tsl], k == 0, k == 7, [wak, ("U", k, b)], [pak])
                for k in range(8):
                    mm(pg[:], wg_[:, k, :], U[:, k, tsl], k == 0, k == 7, [wgk, ("U", k, b)], [pgk])
                sg, sgk = sg_ring.next()
                act(sg[:], pg[:], AF.Sigmoid, [pgk], [sgk])
                tt("dve", AEXT[:, 32 + bb * BL:32 + (bb + 1) * BL], sg[:], pa[:], ALU.mult, [sgk, pak], [("aext", bb)])
            ts("dve", CONV[:, c, :], AEXT[:, 2:2 + HT], wdw[:, c, 0:1], gains[:, B_DW, c:c + 1], ALU.mult, ALU.add,
               ["aext_h", ("aext", 0), ("aext", 1), "wdw", "gains"], [("conv", c)])
            for jt in range(1, 31):
                stt("dve", CONV[:, c, :], AEXT[:, 2 + jt:2 + jt + HT], wdw[:, c, jt:jt + 1], CONV[:, c, :], ALU.mult, ALU.add,
                    [("aext", 0), ("aext", 1), "aext_h", ("conv", c)], [("conv", c)])
            cp("act", ahalo[:, c, :], AEXT[:, HT:HT + 32], [("aext", 1)], [("ahalo", c)])
            for bb in range(2):
                cb, cbk = sqf_ring.next()
                cp("act", cb[:], CONV[:, c, bb * BL:(bb + 1) * BL], [("conv", c)], [cbk])
                mm(ps1[bb][0][:], ones[:], cb[:], c == 0, c == 7, [cbk, "ones"], [ps1[bb][1]])
                sq, sqk = sqf_ring.next()
                act(sq[:], CONV[:, c, bb * BL:(bb + 1) * BL], AF.Square, [("conv", c)], [sqk])
                mm(ps2[bb][0][:], ones[:], sq[:], c == 0, c == 7, [sqk, "ones"], [ps2[bb][1]])
        for bb in range(2):
            bs = slice(bb * BL, (bb + 1) * BL)
            ts("dve", st1[:, bs], ps1[bb][0][:], 1.0 / D, None, ALU.mult, None, [ps1[bb][1]], [("st1", bb)])
            tt("dve", st2[:, bs], st1[:, bs], st1[:, bs], ALU.mult, [("st1", bb)], [("st2", bb)])
            stt("dve", st2[:, bs], ps2[bb][0][:], 1.0 / D, st2[:, bs], ALU.mult, ALU.subtract, [ps2[bb][1], ("st2", bb)], [("st2", bb)])
            act(st2[:, bs], st2[:, bs], AF.Sqrt, [("st2", bb), "sm"], [("st2", bb)], bias=epsc, scale=1.0)
            S.op("dve", lambda e, bs=bs: e.reciprocal(out=st2[:, bs], in_=st2[:, bs]), [("st2", bb)], [("st2", bb)])
        for c in range(8):
            tt("dve", CONV[:, c, :], CONV[:, c, :], st1[:], ALU.subtract, [("conv", c), ("st1", 0), ("st1", 1)], [("conv", c)])
            tt("dve", CONV[:, c, :], CONV[:, c, :], st2[:], ALU.mult, [("conv", c), ("st2", 0), ("st2", 1)], [("conv", c)])
            act(CT[:, c, t0:t0 + HT], CONV[:, c, :], AF.Silu, [("conv", c), "gains"], [("CT", c, hf)],
                bias=gains[:, B_CLN, c:c + 1], scale=gains[:, G_CLN, c:c + 1])
    S.barrier()
    A.off = mark2
    set_pool(range(8))

    checkpoint("conv")
    wring = Ring(A, "w", 2, [128, 8 * 128], BF16)
    wring2 = Ring(A, "w2", 2, [128, 8 * 128], BF16)
    wring3 = Ring(A, "w3", 2, [128, 8 * 128], BF16)
    wring4 = Ring(A, "w4", 2, [128, 4 * 128], BF16)
    sg_ring = Ring(A, "sg", 4, [128, BL], F32)
    M = A.alloc([128, 8, NT], BF16)
    w_co_v = w_co.rearrange("(c p) n -> p c n", p=128)
    w_ao_v = w_ao.rearrange("(c p) n -> p c n", p=128)
    w_mo_v = w_mo.rearrange("(c p) n -> p c n", p=128)
    for c in range(8):
        wga, wgak = load_w_chunk(wring, GOFF + c * 128)
        wgb, wgbk = load_w_chunk(wring2, GOFF + 1024 + c * 128)
        wc, wck = wring3.next()
        wc = wc[:].rearrange("p (c n) -> p c n", c=8)
        pool_load(wc, w_co_v[:, :, c * 128:(c + 1) * 128], wck, "%s%d" % wck)
        wa, wak = wring4.next()
        wa = wa[:].rearrange("p (c n) -> p c n", c=4)
        pool_load(wa, w_ao_v[:, :, c * 128:(c + 1) * 128], wak, "%s%d" % wak)
        for b in range(NB):
            pga, pgak = ps_next()
            pyc, pyck = ps_next()
            pgb, pgbk = ps_next()
            pya, pyak = ps_next()
            for k in range(8):
                mm(pga[:], wga[:, k, :], U[:, k, blk(b)], k == 0, k == 7, [wgak, ("U", k, b)], [pgak])
            for k in range(8):
                mm(pyc[:], wc[:, k, :], CT[:, k, blk(b)], k == 0, k == 7, [wck, ("CT", k, b // 2)], [pyck])
            for k in range(8):
                mm(pgb[:], wgb[:, k, :], U[:, k, blk(b)], k == 0, k == 7, [wgbk, ("U", k, b)], [pgbk])
            for k in range(4):
                mm(pya[:], wa[:, k, :], OT[:, k, blk(b)], k == 0, k == 3, [wak, ("OT", k)], [pyak])
            sa, sak = sg_ring.next()
            sb, sbk = sg_ring.next()
            act(sa[:], pga[:], AF.Sigmoid, [pgak], [sak])
            act(sb[:], pgb[:], AF.Sigmoid, [pgbk], [sbk])
            tt("dve", sa[:], sa[:], pyc[:], ALU.mult, [sak, pyck], [sak])
            tt("dve", sb[:], sb[:], pya[:], ALU.mult, [sbk, pyak], [sbk])
            tt("dve", M[:, c, blk(b)], sa[:], sb[:], ALU.add, [sak, sbk], [("M", c, b)])
    for c in range(8):
        wm, wmk = wring.next()
        wm = wm[:].rearrange("p (c n) -> p c n", c=8)
        pool_load(wm, w_mo_v[:, :, c * 128:(c + 1) * 128], wmk, "%s%d" % wmk)
        for b in range(NB):
            ps, psk = ps_next()
            for k in range(8):
                mm(ps[:], wm[:, k, :], M[:, k, blk(b)], k == 0, k == 7, [wmk, ("M", k, b)], [psk])
            tt("dve", H[:, c, blk(b)], H[:, c, blk(b)], ps[:], ALU.add, [psk, ("H", c, b)], [("H", c, b)])
    S.barrier()
    A.off = mark

    checkpoint("mix")
    ffn(w2g, w2u, w2d, G_FFN2)
    checkpoint("ffn2")

    mark = A.off
    sq_ring = Ring(A, "sq", 2, [128, BL], BF16)
    rt_ring = Ring(A, "rt", 2, [128, BL], F32)
    rmsnorm_to_U(G_PLE, sq_ring, rt_ring)
    PB = A.alloc([128, 2, NT], BF16)
    pool_load(PB[:], pT.rearrange("(c p) t -> p c t", p=128), "PB", "pb")
    wring = Ring(A, "w", 3, [128, 8 * 128], BF16)
    wring4 = Ring(A, "w4", 3, [128, 2 * 128], BF16)
    sg_ring = Ring(A, "sg", 3, [128, BL], F32)
    w_pg_v = w_pg.rearrange("(c p) n -> p c n", p=128)
    w_pp_v = w_pp.rearrange("(c p) n -> p c n", p=128)
    for c in range(8):
        wg_, wgk = wring.next()
        wg_ = wg_[:].rearrange("p (c n) -> p c n", c=8)
        pool_load(wg_, w_pg_v[:, :, c * 128:(c + 1) * 128], wgk, "%s%d" % wgk)
        wp, wpk = wring4.next()
        wp = wp[:].rearrange("p (c n) -> p c n", c=2)
        pool_load(wp, w_pp_v[:, :, c * 128:(c + 1) * 128], wpk, "%s%d" % wpk)
        for b in range(NB):
            pg, pgk = ps_next()
            pp, ppk = ps_next()
            for k in range(8):
                mm(pg[:], wg_[:, k, :], U[:, k, blk(b)], k == 0, k == 7, [wgk, ("U", k, b)], [pgk])
            for k in range(2):
                mm(pp[:], wp[:, k, :], PB[:, k, blk(b)], k == 0, k == 1, [wpk, "PB"], [ppk])
            sg, sgk = sg_ring.next()
            act(sg[:], pg[:], AF.Sigmoid, [pgk], [sgk])
            tt("dve", sg[:], sg[:], pp[:], ALU.mult, [sgk, ppk], [sgk])
            tt("dve", H[:, c, blk(b)], H[:, c, blk(b)], sg[:], ALU.add, [sgk, ("H", c, b)], [("H", c, b)])

    yv = yT.rearrange("(c p) t -> p c t", p=128)
    sqf_ring = Ring(A, "sqf", 2, [128, BL], BF16)
    rtf_ring = Ring(A, "rtf", 2, [128, BL], F32)
    outs = []
    for b in range(NB):
        ps, psk = ps_next()
        for c in range(8):
            sq, sqk = sqf_ring.next()
            act(sq[:], H[:, c, blk(b)], AF.Square, [("H", c, b)], [sqk])
            mm(ps[:], ones[:], sq[:], c == 0, c == 7, [sqk, "ones"], [psk])
        rt, rtk = rtf_ring.next()
        act(rt[:], ps[:], AF.Sqrt, [psk, "sm"], [rtk], bias=epsc, scale=1.0 / D)
        S.op("dve", lambda e, rt=rt: e.reciprocal(out=rt[:], in_=rt[:]), [rtk], [rtk])
        for c in range(8):
            stt("dve", H[:, c, blk(b)], H[:, c, blk(b)], gains[:, G_FIN, c:c + 1], rt[:],
                ALU.mult, ALU.mult, [("H", c, b), rtk, "gains"], [("H", c, b)])
            outs.append(S.op("sp", lambda e, c=c, b=b: e.dma_start(out=yv[:, c, blk(b)], in_=H[:, c, blk(b)]),
                             reads=[("H", c, b)], dma="yout"))
    S.finish()
    S.emit()


_CACHE = {}


def _consts():
    ident = np.eye(128, dtype=np.float32)
    perm = np.zeros((128, 128), np.float32)
    for m in range(128):
        perm[(m + 64) % 128, m] = 1.0
    ones = np.ones((128, 128), np.float32)
    cst = np.stack([ident, perm, ones, ones], axis=1)
    k = np.arange(128)[:, None]
    q = np.arange(128)[None, :]
    mprev = np.where(k >= q, 0.0, NEGM).astype(np.float32)
    mcur = np.where(k <= q, 0.0, NEGM).astype(np.float32)
    mask2 = np.concatenate([mprev, mcur], axis=1)
    invf = (10000.0 ** (-np.arange(0, 128, 2, dtype=np.float32) / 128)).astype(np.float32)
    invf2 = np.concatenate([invf, invf])
    sgn = np.concatenate([-np.ones(64, np.float32), np.ones(64, np.float32)])
    sm = np.stack([invf2, sgn, np.full(128, EPS, np.float32), np.zeros(128, np.float32)], axis=1)
    return cst, mask2, sm.astype(np.float32)


def _pc(v):
    return np.ascontiguousarray(np.asarray(v, np.float32).reshape(8, 128).T)


def kernel(x, p, positions, g_ffn1, w_ffn1_gate, w_ffn1_up, w_ffn1_down, g_mix, w_in,
           w_dw, b_dw, g_conv_ln, b_conv_ln, w_conv_out, w_attn_out, w_mix_out,
           g_ffn2, w_ffn2_gate, w_ffn2_up, w_ffn2_down, g_ple, w_ple_gate, w_ple_proj,
           g_final):
    x = np.asarray(x, np.float32)
    p = np.asarray(p, np.float32)
    positions = np.asarray(positions, np.int32)
    if "nc" not in _CACHE:
        _CACHE["nc"] = build_program()
    nc = _CACHE["nc"]
    cst, mask2, sm = _consts()
    gains = np.stack([_pc(g_ffn1[0]), _pc(g_mix[0]), _pc(g_conv_ln[0]), _pc(b_conv_ln[0]), _pc(b_dw[0]),
                      _pc(g_ffn2[0]), _pc(g_ple[0]), _pc(g_final)], axis=1)
    wdw = np.ascontiguousarray(np.asarray(w_dw[0], np.float32).reshape(31, 8, 128).transpose(2, 1, 0))
    f32 = lambda a: np.ascontiguousarray(np.asarray(a, np.float32))
    shared = {
        "w1g": f32(w_ffn1_gate[0]), "w1u": f32(w_ffn1_up[0]), "w1d": f32(w_ffn1_down[0]),
        "w2g": f32(w_ffn2_gate[0]), "w2u": f32(w_ffn2_up[0]), "w2d": f32(w_ffn2_down[0]),
        "w_in": f32(w_in[0]), "w_co": f32(w_conv_out[0]), "w_ao": f32(w_attn_out[0]), "w_mo": f32(w_mix_out[0]),
        "w_pg": f32(w_ple_gate[0]), "w_pp": f32(w_ple_proj[0]),
        "gains": f32(gains), "wdw": f32(wdw), "cst": f32(cst), "sm": f32(sm),
    }
    in_maps = []
    for core in range(8):
        b, half = core // 2, core % 2
        t0 = half * NT
        xo = np.ascontiguousarray(x[b, t0:t0 + NT, :].T)
        if half == 1:
            xh = np.ascontiguousarray(x[b, 0:NT, :].T)
            ph = positions[b, 0:NT]
        else:
            xh = np.zeros((D, NT), np.float32)
            ph = np.zeros(NT, np.int32)
        pos = np.concatenate([ph, positions[b, t0:t0 + NT]]).astype(np.int32)
        pos = np.ascontiguousarray(np.broadcast_to(pos[None, :], (128, 2 * NT)))
        maskH = mask2.copy()
        if half == 0:
            maskH[:, 0:128] = NEGM
        msk = np.ascontiguousarray(np.stack([mask2, maskH], axis=1))
        m = dict(shared)
        m.update({"xo": xo, "xh": xh, "pT": np.ascontiguousarray(p[0, b, t0:t0 + NT, :].T), "pos": pos, "msk": msk})
        in_maps.append(m)
    import os
    if os.environ.get("KONE"):
        k1 = int(os.environ["KONE"])
        r1 = run_bass_kernel_spmd(nc, [in_maps[k1]], core_ids=[0], trace=bool(os.environ.get("KTRACE")))
        if os.environ.get("KTRACE"):
            print("EXEC_TIME_NS", r1.exec_time_ns)

        class _R:
            results = [r1.results[0] if c == k1 else {"yT": np.zeros((D, NT), np.float32)} for c in range(8)]
        res = _R()
    else:
        res = run_bass_kernel_spmd(nc, in_maps, core_ids=list(range(8)))
    out = np.empty((4, 4096, D), np.float32)
    for core in range(8):
        b, half = core // 2, core % 2
        out[b, half * NT:(half + 1) * NT, :] = res.results[core]["yT"].T
    return out
```

```python
import math
from contextlib import ExitStack

import numpy as np
import concourse.bass as bass
import concourse.mybir as mybir
from concourse.bass_utils import run_bass_kernel_spmd

F32 = mybir.dt.float32
BF16 = mybir.dt.bfloat16
I32 = mybir.dt.int32
ALU = mybir.AluOpType
AF = mybir.ActivationFunctionType

D = 1024
NT = 2048
NB = 4
BL = 512
DFF = 2816
NJ = DFF // 128
INW = 8704
EPS = 1e-6
NEGM = -30000.0
SCALE = 128 ** -0.5
GROUPS = ((1, 16), (4, 4), (16, 1))
TWO_PI = 2.0 * math.pi


class Op:
    __slots__ = ("eng", "fn", "deps", "is_dma", "grp", "val", "need_inc", "count")


class Sched:
    ENGS = ("pe", "act", "dve", "pool", "sp")

    def __init__(self, nc):
        self.nc = nc
        self.ops = {e: [] for e in self.ENGS}
        self.state = {}
        self.groups = {}
        self.pending_dma = []
        self.pending_bar = {e: set() for e in self.ENGS}

    def op(self, eng, fn, reads=(), writes=(), dma=None):
        o = Op()
        o.eng, o.fn, o.deps, o.is_dma, o.grp = eng, fn, set(), dma is not None, dma
        o.need_inc, o.count, o.val = False, 0, 0
        st = self.state
        ps_reads = [k for k in reads if isinstance(k, tuple) and k[0] == "ps"]
        if ps_reads:
            reads = [k for k in reads if k not in ps_reads]
            writes = list(writes) + ps_reads
        for k in reads:
            s = st.get(k)
            if s and s[0] is not None:
                o.deps.add(s[0])
        for k in writes:
            s = st.get(k)
            if s:
                if s[0] is not None:
                    o.deps.add(s[0])
                o.deps.update(s[1])
        for k in reads:
            rl = st.setdefault(k, [None, []])[1]
            if not o.is_dma:
                rl[:] = [r for r in rl if r.is_dma or r.eng != eng]
            rl.append(o)
        for k in writes:
            st[k] = [o, []]
        if self.pending_bar[eng]:
            o.deps.update(self.pending_bar[eng])
            self.pending_bar[eng] = set()
        if dma is not None:
            g = self.groups.setdefault(dma, [None, 0])
            g[1] += 1
            o.val = 16 * g[1]
            self.pending_dma.append(o)
        self.ops[eng].append(o)
        return o

    def barrier(self):
        tails = list(self.pending_dma)
        for e in self.ENGS:
            for o in reversed(self.ops[e]):
                if not o.is_dma:
                    tails.append(o)
                    break
        for e in self.ENGS:
            self.pending_bar[e].update(tails)
        self.pending_dma = []
        self.state = {}

    def finish(self):
        self.barrier()
        self.op("sp", lambda eng: eng.nop())

    def _needs(self, o, d):
        if o.is_dma or d.is_dma or d.eng != o.eng:
            return True
        return o.eng != "pe"

    def emit(self):
        nc = self.nc
        for e in self.ENGS:
            for o in self.ops[e]:
                for d in o.deps:
                    if not d.is_dma and self._needs(o, d):
                        d.need_inc = True
        for e in self.ENGS:
            c = 0
            for o in self.ops[e]:
                if not o.is_dma and o.need_inc:
                    c += 1
                    o.count = c
        with ExitStack() as es:
            sems = {e: es.enter_context(nc.semaphore("s_" + e)) for e in self.ENGS}
            for g in self.groups:
                self.groups[g][0] = es.enter_context(nc.semaphore("d_" + g))
            block = es.enter_context(nc.Block())

            def make(e):
                def body(eng):
                    waited = {}
                    for o in self.ops[e]:
                        ws = {}
                        for d in o.deps:
                            if not self._needs(o, d):
                                continue
                            if d.is_dma:
                                key, sem, val = "d_" + d.grp, self.groups[d.grp][0], d.val
                            else:
                                key, sem, val = "s_" + d.eng, sems[d.eng], d.count
                            if waited.get(key, 0) >= val:
                                continue
                            if key not in ws or ws[key][1] < val:
                                ws[key] = (sem, val)
                        for key, (sem, val) in ws.items():
                            eng.wait_ge(sem, val)
                            waited[key] = val
                        inst = o.fn(eng)
                        if o.is_dma:
                            inst.then_inc(self.groups[o.grp][0], 16)
                        elif o.need_inc:
                            inst.then_inc(sems[e], 1)
                return body

            block.tensor(make("pe"))
            block.scalar(make("act"))
            block.vector(make("dve"))
            block.gpsimd(make("pool"))
            block.sync(make("sp"))


class Arena:
    def __init__(self, nc, limit):
        self.nc, self.off, self.limit, self.n = nc, 16640, limit, 0

    def alloc(self, shape, dt):
        esz = 2 if dt == BF16 else 4
        size = esz
        for s in shape[1:]:
            size *= s
        size = (size + 63) // 64 * 64
        t = self.nc.alloc_sbuf_tensor_at("t%d" % self.n, list(shape), dt, offset=self.off)
        self.n += 1
        self.off += size
        assert self.off <= self.limit, ("SBUF overflow", self.off)
        return t


class Ring:
    def __init__(self, arena, name, n, shape, dt):
        self.t = [arena.alloc(shape, dt) for _ in range(n)]
        self.name, self.i = name, 0

    def next(self):
        i = self.i % len(self.t)
        self.i += 1
        return self.t[i], (self.name, i)


def build_program(dbg=None):
    nc = bass.Bass("TRN2", target_bir_lowering=False)
    try:
        _build_body(nc, dbg)
    except Exception as ex:
        if type(ex).__name__ != "_Stop":
            raise
    return nc


def _build_body(nc, dbg):
    S = None

    def din(name, shape, dt=F32):
        return nc.dram_tensor(name, list(shape), dt, kind="ExternalInput").ap()

    xo = din("xo", [D, NT])
    xh = din("xh", [D, NT])
    pT = din("pT", [256, NT])
    pos = din("pos", [128, 2 * NT], I32)
    w1g, w1u, w1d = din("w1g", [D, DFF]), din("w1u", [D, DFF]), din("w1d", [DFF, D])
    w2g, w2u, w2d = din("w2g", [D, DFF]), din("w2u", [D, DFF]), din("w2d", [DFF, D])
    w_in = din("w_in", [D, INW])
    w_co, w_ao, w_mo = din("w_co", [D, D]), din("w_ao", [512, D]), din("w_mo", [D, D])
    w_pg, w_pp = din("w_pg", [D, D]), din("w_pp", [256, D])
    gains_d = din("gains", [128, 8, 8])
    wdw_d = din("wdw", [128, 8, 31])
    cst_d = din("cst", [128, 4, 128])
    msk_d = din("msk", [128, 2, 256])
    sm_d = din("sm", [128, 4])
    yT = nc.dram_tensor("yT", [D, NT], F32, kind="ExternalOutput").ap()
    khd = nc.dram_tensor("khd", [12, 128, NT], BF16, kind="Internal").ap()
    vhd = nc.dram_tensor("vhd", [12, 128, 16, 128], BF16, kind="Internal").ap()
    dbg_out = None
    if dbg:
        dbg_out = nc.dram_tensor("dbg", [D, NT], F32, kind="ExternalOutput").ap()

    S = Sched(nc)
    A = Arena(nc, 229376)
    PS = [nc.alloc_psum_tensor("ps%d" % i, [128, 512], F32) for i in range(8)]
    ps_pool = {"banks": list(range(8)), "i": 0}

    def set_pool(banks):
        ps_pool["banks"] = list(banks)
        ps_pool["i"] = 0

    def ps_next():
        bk = ps_pool["banks"]
        i = bk[ps_pool["i"] % len(bk)]
        ps_pool["i"] += 1
        return PS[i], ("ps", i)

    gains = A.alloc([128, 8, 8], F32)
    wdw = A.alloc([128, 8, 31], F32)
    sm = A.alloc([128, 4], F32)
    ident = A.alloc([128, 128], BF16)
    perm = A.alloc([128, 128], BF16)
    ones = A.alloc([128, 128], BF16)
    mask2 = A.alloc([128, 256], BF16)
    maskH = A.alloc([128, 256], BF16)
    ahalo = A.alloc([128, 8, 32], F32)
    H = A.alloc([128, 8, NT], F32)
    U = A.alloc([128, 8, NT], BF16)
    base_mark = A.off

    G_FFN1, G_MIX, G_CLN, B_CLN, B_DW, G_FFN2, G_PLE, G_FIN = range(8)
    invf, sgn, epsc = sm[:, 0:1], sm[:, 1:2], sm[:, 2:3]

    def sp_load(dst, src, key, grp):
        S.op("sp", lambda e: e.dma_start(out=dst, in_=src), writes=[key], dma=grp)

    def pool_load(dst, src, key, grp):
        S.op("pool", lambda e: e.dma_start(out=dst, in_=src), writes=[key], dma=grp)

    sp_load(gains[:], gains_d, "gains", "c0")
    sp_load(wdw[:], wdw_d, "wdw", "c0")
    sp_load(sm[:], sm_d, "sm", "c0")
    pool_load(ident[:], cst_d[:, 0, :], "ident", "c1")
    pool_load(perm[:], cst_d[:, 1, :], "perm", "c1")
    pool_load(ones[:], cst_d[:, 2, :], "ones", "c1")
    pool_load(mask2[:], msk_d[:, 0, :], "mask2", "c1")
    pool_load(maskH[:], msk_d[:, 1, :], "maskH", "c1")
    S.barrier()

    def blk(b):
        return slice(b * BL, (b + 1) * BL)

    import os
    STOP = os.environ.get("KSTOP", "")

    class _Stop(Exception):
        pass

    def checkpoint(name):
        if STOP == name:
            yv_ = yT.rearrange("(c p) t -> p c t", p=128)
            for c in range(8):
                S.op("sp", lambda e, c=c: e.dma_start(out=yv_[:, c, :], in_=H[:, c, :]),
                     reads=[("H", c, b) for b in range(NB)], dma="yout")
            S.finish()
            S.emit()
            raise _Stop()

    def mm(out, lhsT, rhs, start, stop, reads, writes):
        S.op("pe", lambda e: e.matmul(out, lhsT=lhsT, rhs=rhs, start=start, stop=stop), reads, writes)

    def act(out, in_, func, reads, writes, bias=None, scale=None):
        kw = {}
        if bias is not None:
            kw["bias"] = bias
        if scale is not None:
            kw["scale"] = scale
        S.op("act", lambda e: e.activation(out=out, in_=in_, func=func, **kw), reads, writes)

    def tt(eng, out, in0, in1, op, reads, writes):
        S.op(eng, lambda e: e.tensor_tensor(out=out, in0=in0, in1=in1, op=op), reads, writes)

    def stt(eng, out, in0, scalar, in1, op0, op1, reads, writes):
        S.op(eng, lambda e: e.scalar_tensor_tensor(out=out, in0=in0, scalar=scalar, in1=in1, op0=op0, op1=op1), reads, writes)

    def ts(eng, out, in0, s1, s2, op0, op1, reads, writes):
        if s2 is None:
            S.op(eng, lambda e: e.tensor_scalar(out=out, in0=in0, scalar1=s1, scalar2=None, op0=op0), reads, writes)
        else:
            S.op(eng, lambda e: e.tensor_scalar(out=out, in0=in0, scalar1=s1, scalar2=s2, op0=op0, op1=op1), reads, writes)

    def cp(eng, out, in_, reads, writes):
        if eng == "act":
            S.op(eng, lambda e: e.copy(out=out, in_=in_), reads, writes)
        else:
            S.op(eng, lambda e: e.tensor_copy(out=out, in_=in_), reads, writes)

    def load_xT(src):
        v = src.rearrange("(c p) t -> p c t", p=128)
        for c in range(8):
            S.op("sp", lambda e, c=c: e.dma_start(out=H[:, c, :], in_=v[:, c, :]),
                 writes=[("H", c, b) for b in range(NB)], dma="xin%d" % c)

    def rmsnorm_to_U(which, sq_ring, rt_ring):
        for b in range(NB):
            ps, psk = ps_next()
            for c in range(8):
                sq, sqk = sq_ring.next()
                act(sq[:], H[:, c, blk(b)], AF.Square, [("H", c, b)], [sqk])
                mm(ps[:], ones[:], sq[:], c == 0, c == 7, [sqk, "ones"], [psk])
            rt, rtk = rt_ring.next()
            act(rt[:], ps[:], AF.Sqrt, [psk, "sm"], [rtk], bias=epsc, scale=1.0 / D)
            S.op("dve", lambda e, rt=rt: e.reciprocal(out=rt[:], in_=rt[:]), [rtk], [rtk])
            for c in range(8):
                stt("dve", U[:, c, blk(b)], H[:, c, blk(b)], gains[:, which, c:c + 1], rt[:],
                    ALU.mult, ALU.mult, [("H", c, b), rtk, "gains"], [("U", c, b)])

    def ffn(wg_d, wu_d, wd_d, gain_idx):
        mark = A.off
        sq_ring = Ring(A, "sq", 2, [128, BL], BF16)
        rt_ring = Ring(A, "rt", 2, [128, BL], F32)
        st_ring = Ring(A, "st", 2, [128, BL], F32)
        ACTB = A.alloc([128, 8, NT], BF16)
        wring = Ring(A, "w", 6, [128, 4096], BF16)
        rmsnorm_to_U(gain_idx, sq_ring, rt_ring)
        wgv = wg_d.rearrange("(c p) n -> p c n", p=128)
        wuv = wu_d.rearrange("(c p) n -> p c n", p=128)
        wdv = wd_d.rearrange("(j p) n -> p j n", p=128)
        for (j0, j1) in ((0, 8), (8, 16), (16, 22)):
            nj = j1 - j0
            for gj in range(j0, j1, 4):
                n4 = min(4, j1 - gj)
                wgt, wgk = wring.next()
                wgt = wgt[:, 0:8 * n4 * 128].rearrange("p (c n) -> p c n", c=8)
                pool_load(wgt, wgv[:, :, gj * 128:(gj + n4) * 128], wgk, "%s%d" % wgk)
                wut, wuk = wring.next()
                wut = wut[:, 0:8 * n4 * 128].rearrange("p (c n) -> p c n", c=8)
                pool_load(wut, wuv[:, :, gj * 128:(gj + n4) * 128], wuk, "%s%d" % wuk)
                for jl in range(n4):
                    jj = gj + jl - j0
                    cs = slice(jl * 128, (jl + 1) * 128)
                    for b in range(NB):
                        pg, pgk = ps_next()
                        pu, puk = ps_next()
                        for c in range(8):
                            mm(pg[:], wgt[:, c, cs], U[:, c, blk(b)], c == 0, c == 7, [wgk, ("U", c, b)], [pgk])
                        for c in range(8):
                            mm(pu[:], wut[:, c, cs], U[:, c, blk(b)], c == 0, c == 7, [wuk, ("U", c, b)], [puk])
                        st, stk = st_ring.next()
                        act(st[:], pg[:], AF.Silu, [pgk], [stk])
                        tt("dve", ACTB[:, jj, blk(b)], st[:], pu[:], ALU.mult, [stk, puk], [("A", jj, b)])
            wds = []
            for gj in range(j0, j1, 4):
                n4 = min(4, j1 - gj)
                wdt, wdk = wring.next()
                wdt = wdt[:, 0:n4 * 1024].rearrange("p (j n) -> p j n", j=n4)
                pool_load(wdt, wdv[:, gj:gj + n4, :], wdk, "%s%d" % wdk)
                wds.append((wdt, wdk))
            for c in range(8):
                for b in range(NB):
                    ps, psk = ps_next()
                    for jj in range(nj):
                        wdt, wdk = wds[jj // 4]
                        mm(ps[:], wdt[:, jj % 4, c * 128:(c + 1) * 128], ACTB[:, jj, blk(b)],
                           jj == 0, jj == nj - 1, [wdk, ("A", jj, b)], [psk])
                    stt("dve", H[:, c, blk(b)], ps[:], 0.5, H[:, c, blk(b)], ALU.mult, ALU.add,
                        [psk, ("H", c, b)], [("H", c, b)])
        S.barrier()
        A.off = mark

    def build_tables(cos2, sinS, pcol0, tmp_f, tmp_i, tmp_k):
        sp_load(tmp_i[:], pos[:, pcol0:pcol0 + NT], "tmp_i", "tab")
        for tab, key, shift in ((sinS, "sinS", 0.0), (cos2, "cos2", math.pi / 2)):
            cp("dve", tab[:], tmp_i[:], ["tmp_i"], [key])
            ts("dve", tab[:], tab[:], invf, shift, ALU.mult, ALU.add, [key, "sm"], [key])
            ts("dve", tmp_f[:], tab[:], 1.0 / TWO_PI, None, ALU.mult, None, [key], ["tmp_f"])
            cp("dve", tmp_k[:], tmp_f[:], ["tmp_f"], ["tmp_k"])
            cp("dve", tmp_f[:], tmp_k[:], ["tmp_k"], ["tmp_f"])
            stt("dve", tab[:], tmp_f[:], -TWO_PI, tab[:], ALU.mult, ALU.add, ["tmp_f", key], [key])
            ts("dve", tmp_f[:], tab[:], math.pi, -TWO_PI, ALU.is_gt, ALU.mult, [key], ["tmp_f"])
            tt("dve", tab[:], tab[:], tmp_f[:], ALU.add, [key, "tmp_f"], [key])
            ts("dve", tmp_f[:], tab[:], -math.pi, TWO_PI, ALU.is_lt, ALU.mult, [key], ["tmp_f"])
            tt("dve", tab[:], tab[:], tmp_f[:], ALU.add, [key, "tmp_f"], [key])
            act(tab[:], tab[:], AF.Sin, [key], [key])
        ts("dve", sinS[:], sinS[:], sgn, None, ALU.mult, None, ["sinS", "sm"], ["sinS"])

    def proj_rot(wt, wk, dst, dst_key, t0, ntok, cos2, sinS, qb_ring, t_ring):
        for o0 in range(0, ntok, BL):
            n = min(BL, ntok - o0)
            b = (t0 + o0) // BL
            tsl = slice(t0 + o0, t0 + o0 + n)
            pz, pzk = ps_next()
            for c in range(8):
                mm(pz[:, 0:n], wt[:, c, :], U[:, c, tsl], c == 0, c == 7, [wk, ("U", c, b)], [pzk])
            qb, qbk = qb_ring.next()
            act(qb[:, 0:n], pz[:, 0:n], AF.Copy, [pzk], [qbk])
            pw, pwk = ps_next()
            mm(pw[:, 0:n], perm[:], qb[:, 0:n], True, True, [qbk, "perm"], [pwk])
            t1, t1k = t_ring.next()
            tt("dve", t1[:, 0:n], pz[:, 0:n], cos2[:, tsl], ALU.mult, [pzk, "cos2"], [t1k])
            t2, t2k = t_ring.next()
            tt("dve", t2[:, 0:n], pw[:, 0:n], sinS[:, tsl], ALU.mult, [pwk, "sinS"], [t2k])
            tt("dve", dst[:, o0:o0 + n], t1[:, 0:n], t2[:, 0:n], ALU.add, [t1k, t2k], [(dst_key, o0 // BL)])

    def proj_v(wt, wk, dst, dst_key, tile_starts, d, vring_keys=None):
        for i, st0 in enumerate(tile_starts):
            pv, pvk = ps_next()
            bs = sorted({(st0 + d * q) // BL for q in (0, 127)})
            bs = list(range(bs[0], bs[-1] + 1))
            for c in range(8):
                mm(pv[:, 0:128], U[:, c, st0:st0 + 127 * d + 1:d], wt[:, c, :],
                   c == 0, c == 7, [wk] + [("U", c, b) for b in bs], [pvk])
            cp("act" if i % 2 else "dve", dst[:, i, :], pv[:, 0:128], [pvk], [(dst_key, i)])

    w_in_v = w_in.rearrange("(c p) n -> p c n", p=128)

    def load_w_chunk(ring, col0, ncol=128):
        wt, wk = ring.next()
        wt = wt[:, 0:8 * ncol].rearrange("p (c n) -> p c n", c=8)
        pool_load(wt, w_in_v[:, :, col0:col0 + ncol], wk, "%s%d" % wk)
        return wt, wk

    QOFF, KOFF, VOFF, GOFF = 2048, 2048 + 1536, 2048 + 3072, 2048 + 4608

    checkpoint("init")
    load_xT(xh)
    checkpoint("load")
    ffn(w1g, w1u, w1d, G_FFN1)
    checkpoint("ffn0")
    mark = A.off
    sq_ring = Ring(A, "sq", 2, [128, BL], BF16)
    rt_ring = Ring(A, "rt", 2, [128, BL], F32)
    rmsnorm_to_U(G_MIX, sq_ring, rt_ring)
    checkpoint("p0a")
    cos2 = A.alloc([128, NT], F32)
    sinS = A.alloc([128, NT], F32)
    tmp_f = A.alloc([128, NT], F32)
    tmp_i = A.alloc([128, NT], I32)
    tmp_k = A.alloc([128, NT], I32)
    build_tables(cos2, sinS, 0, tmp_f, tmp_i, tmp_k)
    checkpoint("p0b")
    wring = Ring(A, "w", int(os.environ.get("KW", "4")), [128, 8 * 128], BF16)
    qb_ring = Ring(A, "qb", 2, [128, BL], BF16)
    t_ring = Ring(A, "t", 4, [128, BL], F32)
    kst_ring = Ring(A, "kst", 2, [128, NT], BF16)
    vst_ring = Ring(A, "vst", 2, [128, 16, 128], BF16)
    for g, (d, nbl) in enumerate(GROUPS):
        nh = 128 * d
        for j in range(4):
            head = 4 * g + j
            if head < int(os.environ.get("KSKIP", "0")):
                continue
            wt, wk = load_w_chunk(wring, KOFF + head * 128)
            kst, kstk = kst_ring.next()
            proj_rot(wt, wk, kst, kstk, NT - nh, nh, cos2, sinS, qb_ring, t_ring)
            if not os.environ.get("KNOSTORE"):
              S.op("sp", lambda e, kst=kst, head=head, nh=nh: e.dma_start(out=khd[head, :, 0:nh], in_=kst[:, 0:nh]),
                 reads=[(kstk, i) for i in range((nh + BL - 1) // BL)], writes=[("khd", head)], dma="hstk%d" % kstk[1])
            if head == 0:
                checkpoint("p0c")
            wt, wk = load_w_chunk(wring, VOFF + head * 128)
            vst, vstk = vst_ring.next()
            starts = [NT - nh + r for r in range(d)]
            proj_v(wt, wk, vst, vstk, starts, d)
            if not os.environ.get("KNOSTORE"):
              S.op("sp", lambda e, vst=vst, head=head, d=d: e.dma_start(out=vhd[head, :, 0:d, :], in_=vst[:, 0:d, :]),
                 reads=[(vstk, i) for i in range(d)], writes=[("vhd", head)], dma="hstv%d" % vstk[1])
            if head == 0:
                checkpoint("p0d")
            if head == 11:
                checkpoint("p0e")
            checkpoint("p0h%d" % head)
    sg_ring = Ring(A, "sg", 2, [128, BL], F32)
    for c in range(8):
        wa, wak = load_w_chunk(wring, c * 128)
        wg_, wgk = load_w_chunk(wring, 1024 + c * 128)
        pa, pak = ps_next()
        pg, pgk = ps_next()
        tsl = slice(NT - 128, NT)
        for k in range(8):
            mm(pa[:, 0:128], wa[:, k, :], U[:, k, tsl], k == 0, k == 7, [wak, ("U", k, 3)], [pak])
        for k in range(8):
            mm(pg[:, 0:128], wg_[:, k, :], U[:, k, tsl], k == 0, k == 7, [wgk, ("U", k, 3)], [pgk])
        sg, sgk = sg_ring.next()
        act(sg[:, 0:128], pg[:, 0:128], AF.Sigmoid, [pgk], [sgk])
        tt("dve", ahalo[:, c, :], sg[:, 96:128], pa[:, 96:128], ALU.mult, [sgk, pak], [("ahalo", c)])
    S.barrier()
    A.off = mark

    checkpoint("p0")
    load_xT(xo)
    ffn(w1g, w1u, w1d, G_FFN1)
    checkpoint("p1")

    mark = A.off
    sq_ring = Ring(A, "sq", 2, [128, BL], BF16)
    rt_ring = Ring(A, "rt", 2, [128, BL], F32)
    rmsnorm_to_U(G_MIX, sq_ring, rt_ring)
    S.barrier()
    A.off = mark
    OT = A.alloc([128, 4, NT], BF16)
    mark2 = A.off

    cos2 = A.alloc([128, NT], F32)
    sinS = A.alloc([128, NT], F32)
    mark3 = A.off
    tmp_f = A.alloc([128, NT], F32)
    tmp_i = A.alloc([128, NT], I32)
    tmp_k = A.alloc([128, NT], I32)
    build_tables(cos2, sinS, NT, tmp_f, tmp_i, tmp_k)
    S.barrier()
    A.off = mark3
    set_pool([0, 1, 2, 3])
    wring = Ring(A, "w", 3, [128, 8 * 128], BF16)
    qb_ring = Ring(A, "qb", 2, [128, BL], BF16)
    t_ring = Ring(A, "t", 4, [128, BL], F32)
    pt_ring = Ring(A, "pt", 3, [128, 256], BF16)
    q_ring = Ring(A, "QT", 2, [128, NT], BF16)
    k_ring = Ring(A, "KT", 2, [128, NT], BF16)
    v_ring = Ring(A, "VT", 2, [128, 16, 128], BF16)
    KHb = A.alloc([128, NT], BF16)
    VHb = A.alloc([128, 16, 128], BF16)
    OACC = A.alloc([128, NT], F32)
    DEN = A.alloc([128, NT], F32)
    ps_half = {"s": 0, "o": 0}

    def half_next(kind):
        i = ps_half[kind] % 2
        ps_half[kind] += 1
        bank = (4 if kind == "s" else 6) + i
        return PS[bank], 0, ("ps", bank)

    for j in range(4):
        for g, (d, nbl) in enumerate(GROUPS):
            head = 4 * g + j
            QTt, qkey = q_ring.next()
            KTt, kkey = k_ring.next()
            VTt, vkey = v_ring.next()
            wt, wk = load_w_chunk(wring, QOFF + head * 128)
            proj_rot(wt, wk, QTt, qkey, 0, NT, cos2, sinS, qb_ring, t_ring)
            wt, wk = load_w_chunk(wring, KOFF + head * 128)
            proj_rot(wt, wk, KTt, kkey, 0, NT, cos2, sinS, qb_ring, t_ring)
            wt, wk = load_w_chunk(wring, VOFF + head * 128)
            starts = [128 * n * d + r for r in range(d) for n in range(nbl)]
            proj_v(wt, wk, VTt, vkey, starts, d)
            S.op("sp", lambda e, head=head, d=d: e.dma_start(out=KHb[:, 0:128 * d], in_=khd[head, :, 0:128 * d]),
                 reads=[("khd", head)], writes=["KH"], dma="hldk")
            S.op("sp", lambda e, head=head, d=d: e.dma_start(out=VHb[:, 0:d, :], in_=vhd[head, :, 0:d, :]),
                 reads=[("vhd", head)], writes=["VH"], dma="hldv")
            blocks = [(r, n) for r in range(d) for n in range(nbl)]

            def emit_scores(r, n):
                s0 = 128 * n * d + r
                cols = slice(s0, s0 + 127 * d + 1, d)
                bs = list(range(s0 // BL, (s0 + 127 * d) // BL + 1))
                qk = [(qkey, b) for b in bs]
                kk = [(kkey, b) for b in bs]
                if n == 0:
                    pcols = slice(r, r + 127 * d + 1, d)
                    kprev, kpk = KHb[:, pcols], ["KH"]
                    msk, mkk = maskH, "maskH"
                else:
                    p0 = 128 * (n - 1) * d + r
                    pcols = slice(p0, p0 + 127 * d + 1, d)
                    pbs = list(range(p0 // BL, (p0 + 127 * d) // BL + 1))
                    kprev, kpk = KTt[:, pcols], [(kkey, b) for b in pbs]
                    msk, mkk = mask2, "mask2"
                pS, so, psk = half_next("s")
                mm(pS[:, so:so + 256], ident[:], msk[:], True, False, ["ident", mkk], [psk])
                mm(pS[:, so:so + 128], kprev, QTt[:, cols], False, False, kpk + qk, [psk])
                mm(pS[:, so + 128:so + 256], KTt[:, cols], QTt[:, cols], False, True, kk + qk, [psk])
                pt, ptk = pt_ring.next()
                act(pt[:], pS[:, so:so + 256], AF.Exp, [psk], [ptk], scale=SCALE)
                return pt, ptk, cols

            def emit_pv(r, n, pt, ptk, cols):
                if n == 0:
                    vprev, vpk = VHb[:, r, :], ["VH"]
                else:
                    vprev, vpk = VTt[:, r * nbl + n - 1, :], [(vkey, r * nbl + n - 1)]
                vcur, vck = VTt[:, r * nbl + n, :], [(vkey, r * nbl + n)]
                pO, oo, pok = half_next("o")
                mm(pO[:, oo:oo + 128], vprev, pt[:, 0:128], True, False, vpk + [ptk], [pok])
                mm(pO[:, oo:oo + 128], vcur, pt[:, 128:256], False, True, vck + [ptk], [pok])
                mm(pO[:, oo + 128:oo + 256], ones[:], pt[:, 0:128], True, False, ["ones", ptk], [pok])
                mm(pO[:, oo + 128:oo + 256], ones[:], pt[:, 128:256], False, True, ["ones", ptk], [pok])
                if g == 0:
                    cp("dve", OACC[:, cols], pO[:, oo:oo + 128], [pok], ["OACC"])
                    cp("dve", DEN[:, cols], pO[:, oo + 128:oo + 256], [pok], ["DEN"])
                else:
                    tt("dve", OACC[:, cols], OACC[:, cols], pO[:, oo:oo + 128], ALU.add, [pok, "OACC"], ["OACC"])
                    tt("dve", DEN[:, cols], DEN[:, cols], pO[:, oo + 128:oo + 256], ALU.add, [pok, "DEN"], ["DEN"])

            pend = emit_scores(*blocks[0])
            for bi in range(len(blocks)):
                nxt = emit_scores(*blocks[bi + 1]) if bi + 1 < len(blocks) else None
                emit_pv(blocks[bi][0], blocks[bi][1], *pend)
                pend = nxt
        S.op("dve", lambda e: e.reciprocal(out=DEN[:], in_=DEN[:]), ["DEN"], ["DEN"])
        tt("dve", OT[:, j, :], OACC[:], DEN[:], ALU.mult, ["OACC", "DEN"], [("OT", j)])
    S.barrier()
    A.off = mark2
    CT = A.alloc([128, 8, NT], BF16)
    mark2 = A.off

    checkpoint("attn")
    HT = 1024
    wring = Ring(A, "w", 3, [128, 8 * 128], BF16)
    sg_ring = Ring(A, "sg", 1, [128, BL], F32)
    sqf_ring = Ring(A, "sqf", 1, [128, BL], BF16)
    CONV = A.alloc([128, 8, HT], F32)
    ABF = A.alloc([128, 32 + HT], BF16)
    DG = A.alloc([128, 31, 128], BF16)
    st1 = A.alloc([128, HT], F32)
    st2 = A.alloc([128, HT], F32)
    set_pool([4, 5, 6, 7])
    for hf in range(2):
        t0 = hf * HT
        ps1 = [(PS[0], ("ps", 0)), (PS[1], ("ps", 1))]
        ps2 = [(PS[2], ("ps", 2)), (PS[3], ("ps", 3))]
        for c in range(8):
            wa, wak = load_w_chunk(wring, c * 128)
            wg_, wgk = load_w_chunk(wring, 1024 + c * 128)
            cp("act", ABF[:, 0:32], ahalo[:, c, :], [("ahalo", c)], ["aext_h"])
            for jt in range(31):
                act(DG[:, jt, :], ident[:], AF.Copy, ["ident", "wdw"], [("dg", jt)], scale=wdw[:, c, jt:jt + 1])
            for bb in range(2):
                b = hf * 2 + bb
                tsl = slice(t0 + bb * BL, t0 + (bb + 1) * BL)
                pa, pak = ps_next()
                pg, pgk = ps_next()
                for k in range(8):
                    mm(pa[:], wa[:, k, :], U[:, k, tsl], k == 0, k == 7, [wak, ("U", k, b)], [pak])
                for k in range(8):
                    mm(pg[:], wg_[:, k, :], U[:, k, tsl], k == 0, k == 7, [wgk, ("U", k, b)], [pgk])
                sg, sgk = sg_ring.next()
                act(sg[:], pg[:], AF.Sigmoid, [pgk], [sgk])
                tt("dve", ABF[:, 32 + bb * BL:32 + (bb + 1) * BL], sg[:], pa[:], ALU.mult, [sgk, pak], [("aext", bb)])
            cp("act", ahalo[:, c, :], ABF[:, HT:HT + 32], [("aext", 1)], [("ahalo", c)])
            for bb in range(2):
                pc, pck = ps_next()
                for jt in range(31):
                    o0 = 2 + jt + bb * BL
                    mm(pc[:], DG[:, jt, :], ABF[:, o0:o0 + BL], jt == 0, jt == 30,
                       [("dg", jt), "aext_h", ("aext", 0), ("aext", 1)], [pck])
                ts("dve", CONV[:, c, bb * BL:(bb + 1) * BL], pc[:], gains[:, B_DW, c:c + 1], None, ALU.add, None,
                   [pck, "gains"], [("conv", c, bb)])
            for bb in range(2):
                cb, cbk = sqf_ring.next()
                cp("act", cb[:], CONV[:, c, bb * BL:(bb + 1) * BL], [("conv", c, bb)], [cbk])
                mm(ps1[bb][0][:], ones[:], cb[:], c == 0, c == 7, [cbk, "ones"], [ps1[bb][1]])
                sq, sqk = sqf_ring.next()
                act(sq[:], CONV[:, c, bb * BL:(bb + 1) * BL], AF.Square, [("conv", c, bb)], [sqk])
                mm(ps2[bb][0][:], ones[:], sq[:], c == 0, c == 7, [sqk, "ones"], [ps2[bb][1]])
        for bb in range(2):
            bs = slice(bb * BL, (bb + 1) * BL)
            ts("dve", st1[:, bs], ps1[bb][0][:], 1.0 / D, None, ALU.mult, None, [ps1[bb][1]], [("st1", bb)])
            tt("dve", st2[:, bs], st1[:, bs], st1[:, bs], ALU.mult, [("st1", bb)], [("st2", bb)])
            stt("dve", st2[:, bs], ps2[bb][0][:], 1.0 / D, st2[:, bs], ALU.mult, ALU.subtract, [ps2[bb][1], ("st2", bb)], [("st2", bb)])
            act(st2[:, bs], st2[:, bs], AF.Sqrt, [("st2", bb), "sm"], [("st2", bb)], bias=epsc, scale=1.0)
            S.op("dve", lambda e, bs=bs: e.reciprocal(out=st2[:, bs], in_=st2[:, bs]), [("st2", bb)], [("st2", bb)])
        for c in range(8):
            ck = [("conv", c, 0), ("conv", c, 1)]
            tt("dve", CONV[:, c, :], CONV[:, c, :], st1[:], ALU.subtract, ck + [("st1", 0), ("st1", 1)], ck)
            tt("dve", CONV[:, c, :], CONV[:, c, :], st2[:], ALU.mult, ck + [("st2", 0), ("st2", 1)], ck)
            act(CT[:, c, t0:t0 + HT], CONV[:, c, :], AF.Silu, ck + ["gains"], [("CT", c, hf)],
                bias=gains[:, B_CLN, c:c + 1], scale=gains[:, G_CLN, c:c + 1])
    S.barrier()
    A.off = mark2
    set_pool(range(8))

    checkpoint("conv")
    wring = Ring(A, "w", 2, [128, 8 * 128], BF16)
    wring2 = Ring(A, "w2", 2, [128, 8 * 128], BF16)
    wring3 = Ring(A, "w3", 2, [128, 8 * 128], BF16)
    wring4 = Ring(A, "w4", 2, [128, 4 * 128], BF16)
    sg_ring = Ring(A, "sg", 4, [128, BL], F32)
    M = A.alloc([128, 8, NT], BF16)
    w_co_v = w_co.rearrange("(c p) n -> p c n", p=128)
    w_ao_v = w_ao.rearrange("(c p) n -> p c n", p=128)
    w_mo_v = w_mo.rearrange("(c p) n -> p c n", p=128)
    for c in range(8):
        wga, wgak = load_w_chunk(wring, GOFF + c * 128)
        wgb, wgbk = load_w_chunk(wring2, GOFF + 1024 + c * 128)
        wc, wck = wring3.next()
        wc = wc[:].rearrange("p (c n) -> p c n", c=8)
        pool_load(wc, w_co_v[:, :, c * 128:(c + 1) * 128], wck, "%s%d" % wck)
        wa, wak = wring4.next()
        wa = wa[:].rearrange("p (c n) -> p c n", c=4)
        pool_load(wa, w_ao_v[:, :, c * 128:(c + 1) * 128], wak, "%s%d" % wak)
        for b in range(NB):
            pga, pgak = ps_next()
            pyc, pyck = ps_next()
            pgb, pgbk = ps_next()
            pya, pyak = ps_next()
            for k in range(8):
                mm(pga[:], wga[:, k, :], U[:, k, blk(b)], k == 0, k == 7, [wgak, ("U", k, b)], [pgak])
            for k in range(8):
                mm(pyc[:], wc[:, k, :], CT[:, k, blk(b)], k == 0, k == 7, [wck, ("CT", k, b // 2)], [pyck])
            for k in range(8):
                mm(pgb[:], wgb[:, k, :], U[:, k, blk(b)], k == 0, k == 7, [wgbk, ("U", k, b)], [pgbk])
            for k in range(4):
                mm(pya[:], wa[:, k, :], OT[:, k, blk(b)], k == 0, k == 3, [wak, ("OT", k)], [pyak])
            sa, sak = sg_ring.next()
            sb, sbk = sg_ring.next()
            act(sa[:], pga[:], AF.Sigmoid, [pgak], [sak])
            act(sb[:], pgb[:], AF.Sigmoid, [pgbk], [sbk])
            tt("dve", sa[:], sa[:], pyc[:], ALU.mult, [sak, pyck], [sak])
            tt("dve", sb[:], sb[:], pya[:], ALU.mult, [sbk, pyak], [sbk])
            tt("dve", M[:, c, blk(b)], sa[:], sb[:], ALU.add, [sak, sbk], [("M", c, b)])
    for c in range(8):
        wm, wmk = wring.next()
        wm = wm[:].rearrange("p (c n) -> p c n", c=8)
        pool_load(wm, w_mo_v[:, :, c * 128:(c + 1) * 128], wmk, "%s%d" % wmk)
        for b in range(NB):
            ps, psk = ps_next()
            for k in range(8):
                mm(ps[:], wm[:, k, :], M[:, k, blk(b)], k == 0, k == 7, [wmk, ("M", k, b)], [psk])
            tt("dve", H[:, c, blk(b)], H[:, c, blk(b)], ps[:], ALU.add, [psk, ("H", c, b)], [("H", c, b)])
    S.barrier()
    A.off = mark

    checkpoint("mix")
    ffn(w2g, w2u, w2d, G_FFN2)
    checkpoint("ffn2")

    mark = A.off
    sq_ring = Ring(A, "sq", 2, [128, BL], BF16)
    rt_ring = Ring(A, "rt", 2, [128, BL], F32)
    rmsnorm_to_U(G_PLE, sq_ring, rt_ring)
    PB = A.alloc([128, 2, NT], BF16)
    pool_load(PB[:], pT.rearrange("(c p) t -> p c t", p=128), "PB", "pb")
    wring = Ring(A, "w", 3, [128, 8 * 128], BF16)
    wring4 = Ring(A, "w4", 3, [128, 2 * 128], BF16)
    sg_ring = Ring(A, "sg", 3, [128, BL], F32)
    w_pg_v = w_pg.rearrange("(c p) n -> p c n", p=128)
    w_pp_v = w_pp.rearrange("(c p) n -> p c n", p=128)
    for c in range(8):
        wg_, wgk = wring.next()
        wg_ = wg_[:].rearrange("p (c n) -> p c n", c=8)
        pool_load(wg_, w_pg_v[:, :, c * 128:(c + 1) * 128], wgk, "%s%d" % wgk)
        wp, wpk = wring4.next()
        wp = wp[:].rearrange("p (c n) -> p c n", c=2)
        pool_load(wp, w_pp_v[:, :, c * 128:(c + 1) * 128], wpk, "%s%d" % wpk)
        for b in range(NB):
            pg, pgk = ps_next()
            pp, ppk = ps_next()
            for k in range(8):
                mm(pg[:], wg_[:, k, :], U[:, k, blk(b)], k == 0, k == 7, [wgk, ("U", k, b)], [pgk])
            for k in range(2):
                mm(pp[:], wp[:, k, :], PB[:, k, blk(b)], k == 0, k == 1, [wpk, "PB"], [ppk])
            sg, sgk = sg_ring.next()
            act(sg[:], pg[:], AF.Sigmoid, [pgk], [sgk])
            tt("dve", sg[:], sg[:], pp[:], ALU.mult, [sgk, ppk], [sgk])
            tt("dve", H[:, c, blk(b)], H[:, c, blk(b)], sg[:], ALU.add, [sgk, ("H", c, b)], [("H", c, b)])

    yv = yT.rearrange("(c p) t -> p c t", p=128)
    sqf_ring = Ring(A, "sqf", 2, [128, BL], BF16)
    rtf_ring = Ring(A, "rtf", 2, [128, BL], F32)
    outs = []
    for b in range(NB):
        ps, psk = ps_next()
        for c in range(8):
            sq, sqk = sqf_ring.next()
            act(sq[:], H[:, c, blk(b)], AF.Square, [("H", c, b)], [sqk])
            mm(ps[:], ones[:], sq[:], c == 0, c == 7, [sqk, "ones"], [psk])
        rt, rtk = rtf_ring.next()
        act(rt[:], ps[:], AF.Sqrt, [psk, "sm"], [rtk], bias=epsc, scale=1.0 / D)
        S.op("dve", lambda e, rt=rt: e.reciprocal(out=rt[:], in_=rt[:]), [rtk], [rtk])
        for c in range(8):
            stt("dve", H[:, c, blk(b)], H[:, c, blk(b)], gains[:, G_FIN, c:c + 1], rt[:],
                ALU.mult, ALU.mult, [("H", c, b), rtk, "gains"], [("H", c, b)])
            outs.append(S.op("sp", lambda e, c=c, b=b: e.dma_start(out=yv[:, c, blk(b)], in_=H[:, c, blk(b)]),
                             reads=[("H", c, b)], dma="yout"))
    S.finish()
    S.emit()


_CACHE = {}


def _consts():
    ident = np.eye(128, dtype=np.float32)
    perm = np.zeros((128, 128), np.float32)
    for m in range(128):
        perm[(m + 64) % 128, m] = 1.0
    ones = np.ones((128, 128), np.float32)
    cst = np.stack([ident, perm, ones, ones], axis=1)
    k = np.arange(128)[:, None]
    q = np.arange(128)[None, :]
    mprev = np.where(k >= q, 0.0, NEGM).astype(np.float32)
    mcur = np.where(k <= q, 0.0, NEGM).astype(np.float32)
    mask2 = np.concatenate([mprev, mcur], axis=1)
    invf = (10000.0 ** (-np.arange(0, 128, 2, dtype=np.float32) / 128)).astype(np.float32)
    invf2 = np.concatenate([invf, invf])
    sgn = np.concatenate([-np.ones(64, np.float32), np.ones(64, np.float32)])
    sm = np.stack([invf2, sgn, np.full(128, EPS, np.float32), np.zeros(128, np.float32)], axis=1)
    return cst, mask2, sm.astype(np.float32)


def _pc(v):
    return np.ascontiguousarray(np.asarray(v, np.float32).reshape(8, 128).T)


def kernel(x, p, positions, g_ffn1, w_ffn1_gate, w_ffn1_up, w_ffn1_down, g_mix, w_in,
           w_dw, b_dw, g_conv_ln, b_conv_ln, w_conv_out, w_attn_out, w_mix_out,
           g_ffn2, w_ffn2_gate, w_ffn2_up, w_ffn2_down, g_ple, w_ple_gate, w_ple_proj,
           g_final):
    x = np.asarray(x, np.float32)
    p = np.asarray(p, np.float32)
    positions = np.asarray(positions, np.int32)
    if "nc" not in _CACHE:
        _CACHE["nc"] = build_program()
    nc = _CACHE["nc"]
    cst, mask2, sm = _consts()
    gains = np.stack([_pc(g_ffn1[0]), _pc(g_mix[0]), _pc(g_conv_ln[0]), _pc(b_conv_ln[0]), _pc(b_dw[0]),
                      _pc(g_ffn2[0]), _pc(g_ple[0]), _pc(g_final)], axis=1)
    wdw = np.ascontiguousarray(np.asarray(w_dw[0], np.float32).reshape(31, 8, 128).transpose(2, 1, 0))
    f32 = lambda a: np.ascontiguousarray(np.asarray(a, np.float32))
    shared = {
        "w1g": f32(w_ffn1_gate[0]), "w1u": f32(w_ffn1_up[0]), "w1d": f32(w_ffn1_down[0]),
        "w2g": f32(w_ffn2_gate[0]), "w2u": f32(w_ffn2_up[0]), "w2d": f32(w_ffn2_down[0]),
        "w_in": f32(w_in[0]), "w_co": f32(w_conv_out[0]), "w_ao": f32(w_attn_out[0]), "w_mo": f32(w_mix_out[0]),
        "w_pg": f32(w_ple_gate[0]), "w_pp": f32(w_ple_proj[0]),
        "gains": f32(gains), "wdw": f32(wdw), "cst": f32(cst), "sm": f32(sm),
    }
    in_maps = []
    for core in range(8):
        b, half = core // 2, core % 2
        t0 = half * NT
        xo = np.ascontiguousarray(x[b, t0:t0 + NT, :].T)
        if half == 1:
            xh = np.ascontiguousarray(x[b, 0:NT, :].T)
            ph = positions[b, 0:NT]
        else:
            xh = np.zeros((D, NT), np.float32)
            ph = np.zeros(NT, np.int32)
        pos = np.concatenate([ph, positions[b, t0:t0 + NT]]).astype(np.int32)
        pos = np.ascontiguousarray(np.broadcast_to(pos[None, :], (128, 2 * NT)))
        maskH = mask2.copy()
        if half == 0:
            maskH[:, 0:128] = NEGM
        msk = np.ascontiguousarray(np.stack([mask2, maskH], axis=1))
        m = dict(shared)
        m.update({"xo": xo, "xh": xh, "pT": np.ascontiguousarray(p[0, b, t0:t0 + NT, :].T), "pos": pos, "msk": msk})
        in_maps.append(m)
    import os
    if os.environ.get("KONE"):
        k1 = int(os.environ["KONE"])
        r1 = run_bass_kernel_spmd(nc, [in_maps[k1]], core_ids=[0], trace=bool(os.environ.get("KTRACE")))
        if os.environ.get("KTRACE"):
            print("EXEC_TIME_NS", r1.exec_time_ns)

        class _R:
            results = [r1.results[0] if c == k1 else {"yT": np.zeros((D, NT), np.float32)} for c in range(8)]
        res = _R()
    else:
        res = run_bass_kernel_spmd(nc, in_maps, core_ids=list(range(8)))
    out = np.empty((4, 4096, D), np.float32)
    for core in range(8):
        b, half = core // 2, core % 2
        out[b, half * NT:(half + 1) * NT, :] = res.results[core]["yT"].T
    return out
```

```python
import math
from contextlib import ExitStack

import numpy as np
import concourse.bass as bass
import concourse.mybir as mybir
from concourse.bass_utils import run_bass_kernel_spmd

F32 = mybir.dt.float32
BF16 = mybir.dt.bfloat16
I32 = mybir.dt.int32
ALU = mybir.AluOpType
AF = mybir.ActivationFunctionType

D = 1024
NT = 2048
NB = 4
BL = 512
DFF = 2816
NJ = DFF // 128
INW = 8704
EPS = 1e-6
NEGM = -30000.0
SCALE = 128 ** -0.5
GROUPS = ((1, 16), (4, 4), (16, 1))
TWO_PI = 2.0 * math.pi


class Op:
    __slots__ = ("eng", "fn", "deps", "is_dma", "grp", "val", "need_inc", "count")


class Sched:
    ENGS = ("pe", "act", "dve", "pool", "sp")

    def __init__(self, nc):
        self.nc = nc
        self.ops = {e: [] for e in self.ENGS}
        self.state = {}
        self.groups = {}
        self.pending_dma = []
        self.pending_bar = {e: set() for e in self.ENGS}

    def op(self, eng, fn, reads=(), writes=(), dma=None):
        o = Op()
        o.eng, o.fn, o.deps, o.is_dma, o.grp = eng, fn, set(), dma is not None, dma
        o.need_inc, o.count, o.val = False, 0, 0
        st = self.state
        ps_reads = [k for k in reads if isinstance(k, tuple) and k[0] == "ps"]
        if ps_reads:
            reads = [k for k in reads if k not in ps_reads]
            writes = list(writes) + ps_reads
        for k in reads:
            s = st.get(k)
            if s and s[0] is not None:
                o.deps.add(s[0])
        for k in writes:
            s = st.get(k)
            if s:
                if s[0] is not None:
                    o.deps.add(s[0])
                o.deps.update(s[1])
        for k in reads:
            rl = st.setdefault(k, [None, []])[1]
            if not o.is_dma:
                rl[:] = [r for r in rl if r.is_dma or r.eng != eng]
            rl.append(o)
        for k in writes:
            st[k] = [o, []]
        if self.pending_bar[eng]:
            o.deps.update(self.pending_bar[eng])
            self.pending_bar[eng] = set()
        if dma is not None:
            g = self.groups.setdefault(dma, [None, 0])
            g[1] += 1
            o.val = 16 * g[1]
            self.pending_dma.append(o)
        self.ops[eng].append(o)
        return o

    def barrier(self):
        tails = list(self.pending_dma)
        for e in self.ENGS:
            for o in reversed(self.ops[e]):
                if not o.is_dma:
                    tails.append(o)
                    break
        for e in self.ENGS:
            self.pending_bar[e].update(tails)
        self.pending_dma = []
        self.state = {}

    def finish(self):
        self.barrier()
        self.op("sp", lambda eng: eng.nop())

    def _needs(self, o, d):
        if o.is_dma or d.is_dma or d.eng != o.eng:
            return True
        return o.eng != "pe"

    def emit(self):
        nc = self.nc
        for e in self.ENGS:
            for o in self.ops[e]:
                for d in o.deps:
                    if not d.is_dma and self._needs(o, d):
                        d.need_inc = True
        for e in self.ENGS:
            c = 0
            for o in self.ops[e]:
                if not o.is_dma and o.need_inc:
                    c += 1
                    o.count = c
        with ExitStack() as es:
            sems = {e: es.enter_context(nc.semaphore("s_" + e)) for e in self.ENGS}
            for g in self.groups:
                self.groups[g][0] = es.enter_context(nc.semaphore("d_" + g))
            block = es.enter_context(nc.Block())

            def make(e):
                def body(eng):
                    waited = {}
                    for o in self.ops[e]:
                        ws = {}
                        for d in o.deps:
                            if not self._needs(o, d):
                                continue
                            if d.is_dma:
                                key, sem, val = "d_" + d.grp, self.groups[d.grp][0], d.val
                            else:
                                key, sem, val = "s_" + d.eng, sems[d.eng], d.count
                            if waited.get(key, 0) >= val:
                                continue
                            if key not in ws or ws[key][1] < val:
                                ws[key] = (sem, val)
                        for key, (sem, val) in ws.items():
                            eng.wait_ge(sem, val)
                            waited[key] = val
                        inst = o.fn(eng)
                        if o.is_dma:
                            inst.then_inc(self.groups[o.grp][0], 16)
                        elif o.need_inc:
                            inst.then_inc(sems[e], 1)
                return body

            block.tensor(make("pe"))
            block.scalar(make("act"))
            block.vector(make("dve"))
            block.gpsimd(make("pool"))
            block.sync(make("sp"))


class Arena:
    def __init__(self, nc, limit):
        self.nc, self.off, self.limit, self.n = nc, 16640, limit, 0

    def alloc(self, shape, dt):
        esz = 2 if dt == BF16 else 4
        size = esz
        for s in shape[1:]:
            size *= s
        size = (size + 63) // 64 * 64
        t = self.nc.alloc_sbuf_tensor_at("t%d" % self.n, list(shape), dt, offset=self.off)
        self.n += 1
        self.off += size
        assert self.off <= self.limit, ("SBUF overflow", self.off)
        return t


class Ring:
    def __init__(self, arena, name, n, shape, dt):
        self.t = [arena.alloc(shape, dt) for _ in range(n)]
        self.name, self.i = name, 0

    def next(self):
        i = self.i % len(self.t)
        self.i += 1
        return self.t[i], (self.name, i)


def build_program(dbg=None):
    nc = bass.Bass("TRN2", target_bir_lowering=False)
    try:
        _build_body(nc, dbg)
    except Exception as ex:
        if type(ex).__name__ != "_Stop":
            raise
    return nc


def _build_body(nc, dbg):
    S = None

    def din(name, shape, dt=F32):
        return nc.dram_tensor(name, list(shape), dt, kind="ExternalInput").ap()

    xo = din("xo", [D, NT])
    xh = din("xh", [D, NT])
    pT = din("pT", [256, NT])
    pos = din("pos", [128, 2 * NT], I32)
    w1g, w1u, w1d = din("w1g", [D, DFF]), din("w1u", [D, DFF]), din("w1d", [DFF, D])
    w2g, w2u, w2d = din("w2g", [D, DFF]), din("w2u", [D, DFF]), din("w2d", [DFF, D])
    w_in = din("w_in", [D, INW])
    w_co, w_ao, w_mo = din("w_co", [D, D]), din("w_ao", [512, D]), din("w_mo", [D, D])
    w_pg, w_pp = din("w_pg", [D, D]), din("w_pp", [256, D])
    gains_d = din("gains", [128, 8, 8])
    wdw_d = din("wdw", [128, 8, 31])
    cst_d = din("cst", [128, 4, 128])
    msk_d = din("msk", [128, 2, 256])
    sm_d = din("sm", [128, 4])
    yT = nc.dram_tensor("yT", [D, NT], F32, kind="ExternalOutput").ap()
    khd = nc.dram_tensor("khd", [12, 128, NT], BF16, kind="Internal").ap()
    vhd = nc.dram_tensor("vhd", [12, 128, 16, 128], BF16, kind="Internal").ap()
    tabd = nc.dram_tensor("tabd", [2, 128, NT], F32, kind="Internal").ap()
    dbg_out = None
    if dbg:
        dbg_out = nc.dram_tensor("dbg", [D, NT], F32, kind="ExternalOutput").ap()

    S = Sched(nc)
    A = Arena(nc, 229376)
    PS = [nc.alloc_psum_tensor("ps%d" % i, [128, 512], F32) for i in range(8)]
    ps_pool = {"banks": list(range(8)), "i": 0}

    def set_pool(banks):
        ps_pool["banks"] = list(banks)
        ps_pool["i"] = 0

    def ps_next():
        bk = ps_pool["banks"]
        i = bk[ps_pool["i"] % len(bk)]
        ps_pool["i"] += 1
        return PS[i], ("ps", i)

    gains = A.alloc([128, 8, 8], F32)
    wdw = A.alloc([128, 8, 31], F32)
    sm = A.alloc([128, 4], F32)
    ident = A.alloc([128, 128], BF16)
    perm = A.alloc([128, 128], BF16)
    ones = A.alloc([128, 128], BF16)
    mask2 = A.alloc([128, 256], BF16)
    maskH = A.alloc([128, 256], BF16)
    ahalo = A.alloc([128, 8, 32], F32)
    H = A.alloc([128, 8, NT], F32)
    U = A.alloc([128, 8, NT], BF16)
    base_mark = A.off

    G_FFN1, G_MIX, G_CLN, B_CLN, B_DW, G_FFN2, G_PLE, G_FIN = range(8)
    invf, sgn, epsc = sm[:, 0:1], sm[:, 1:2], sm[:, 2:3]

    def sp_load(dst, src, key, grp):
        S.op("sp", lambda e: e.dma_start(out=dst, in_=src), writes=[key], dma=grp)

    def pool_load(dst, src, key, grp):
        S.op("pool", lambda e: e.dma_start(out=dst, in_=src), writes=[key], dma=grp)

    sp_load(gains[:], gains_d, "gains", "c0")
    sp_load(wdw[:], wdw_d, "wdw", "c0")
    sp_load(sm[:], sm_d, "sm", "c0")
    pool_load(ident[:], cst_d[:, 0, :], "ident", "c1")
    pool_load(perm[:], cst_d[:, 1, :], "perm", "c1")
    pool_load(ones[:], cst_d[:, 2, :], "ones", "c1")
    pool_load(mask2[:], msk_d[:, 0, :], "mask2", "c1")
    pool_load(maskH[:], msk_d[:, 1, :], "maskH", "c1")
    S.barrier()

    def blk(b):
        return slice(b * BL, (b + 1) * BL)

    import os
    STOP = os.environ.get("KSTOP", "")

    class _Stop(Exception):
        pass

    def checkpoint(name):
        if STOP == name:
            yv_ = yT.rearrange("(c p) t -> p c t", p=128)
            for c in range(8):
                S.op("sp", lambda e, c=c: e.dma_start(out=yv_[:, c, :], in_=H[:, c, :]),
                     reads=[("H", c, b) for b in range(NB)], dma="yout")
            S.finish()
            S.emit()
            raise _Stop()

    def mm(out, lhsT, rhs, start, stop, reads, writes):
        S.op("pe", lambda e: e.matmul(out, lhsT=lhsT, rhs=rhs, start=start, stop=stop), reads, writes)

    def act(out, in_, func, reads, writes, bias=None, scale=None):
        kw = {}
        if bias is not None:
            kw["bias"] = bias
        if scale is not None:
            kw["scale"] = scale
        S.op("act", lambda e: e.activation(out=out, in_=in_, func=func, **kw), reads, writes)

    def tt(eng, out, in0, in1, op, reads, writes):
        S.op(eng, lambda e: e.tensor_tensor(out=out, in0=in0, in1=in1, op=op), reads, writes)

    def stt(eng, out, in0, scalar, in1, op0, op1, reads, writes):
        S.op(eng, lambda e: e.scalar_tensor_tensor(out=out, in0=in0, scalar=scalar, in1=in1, op0=op0, op1=op1), reads, writes)

    def ts(eng, out, in0, s1, s2, op0, op1, reads, writes):
        if s2 is None:
            S.op(eng, lambda e: e.tensor_scalar(out=out, in0=in0, scalar1=s1, scalar2=None, op0=op0), reads, writes)
        else:
            S.op(eng, lambda e: e.tensor_scalar(out=out, in0=in0, scalar1=s1, scalar2=s2, op0=op0, op1=op1), reads, writes)

    def cp(eng, out, in_, reads, writes):
        if eng == "act":
            S.op(eng, lambda e: e.copy(out=out, in_=in_), reads, writes)
        else:
            S.op(eng, lambda e: e.tensor_copy(out=out, in_=in_), reads, writes)

    def load_xT(src):
        v = src.rearrange("(c p) t -> p c t", p=128)
        for c in range(8):
            S.op("sp", lambda e, c=c: e.dma_start(out=H[:, c, :], in_=v[:, c, :]),
                 writes=[("H", c, b) for b in range(NB)], dma="xin%d" % c)

    def rmsnorm_to_U(which, sq_ring, rt_ring):
        for b in range(NB):
            ps, psk = ps_next()
            for c in range(8):
                sq, sqk = sq_ring.next()
                act(sq[:], H[:, c, blk(b)], AF.Square, [("H", c, b)], [sqk])
                mm(ps[:], ones[:], sq[:], c == 0, c == 7, [sqk, "ones"], [psk])
            rt, rtk = rt_ring.next()
            act(rt[:], ps[:], AF.Sqrt, [psk, "sm"], [rtk], bias=epsc, scale=1.0 / D)
            S.op("dve", lambda e, rt=rt: e.reciprocal(out=rt[:], in_=rt[:]), [rtk], [rtk])
            for c in range(8):
                stt("dve", U[:, c, blk(b)], H[:, c, blk(b)], gains[:, which, c:c + 1], rt[:],
                    ALU.mult, ALU.mult, [("H", c, b), rtk, "gains"], [("U", c, b)])

    def ffn(wg_d, wu_d, wd_d, gain_idx):
        mark = A.off
        sq_ring = Ring(A, "sq", 2, [128, BL], BF16)
        rt_ring = Ring(A, "rt", 2, [128, BL], F32)
        st_ring = Ring(A, "st", 2, [128, BL], F32)
        ACTB = A.alloc([128, 8, NT], BF16)
        wring = Ring(A, "w", 6, [128, 4096], BF16)
        rmsnorm_to_U(gain_idx, sq_ring, rt_ring)
        wgv = wg_d.rearrange("(c p) n -> p c n", p=128)
        wuv = wu_d.rearrange("(c p) n -> p c n", p=128)
        wdv = wd_d.rearrange("(j p) n -> p j n", p=128)
        for (j0, j1) in ((0, 8), (8, 16), (16, 22)):
            nj = j1 - j0
            for gj in range(j0, j1, 4):
                n4 = min(4, j1 - gj)
                wgt, wgk = wring.next()
                wgt = wgt[:, 0:8 * n4 * 128].rearrange("p (c n) -> p c n", c=8)
                pool_load(wgt, wgv[:, :, gj * 128:(gj + n4) * 128], wgk, "%s%d" % wgk)
                wut, wuk = wring.next()
                wut = wut[:, 0:8 * n4 * 128].rearrange("p (c n) -> p c n", c=8)
                pool_load(wut, wuv[:, :, gj * 128:(gj + n4) * 128], wuk, "%s%d" % wuk)
                for jl in range(n4):
                    jj = gj + jl - j0
                    cs = slice(jl * 128, (jl + 1) * 128)
                    for b in range(NB):
                        pg, pgk = ps_next()
                        pu, puk = ps_next()
                        for c in range(8):
                            mm(pg[:], wgt[:, c, cs], U[:, c, blk(b)], c == 0, c == 7, [wgk, ("U", c, b)], [pgk])
                        for c in range(8):
                            mm(pu[:], wut[:, c, cs], U[:, c, blk(b)], c == 0, c == 7, [wuk, ("U", c, b)], [puk])
                        st, stk = st_ring.next()
                        act(st[:], pg[:], AF.Silu, [pgk], [stk])
                        tt("dve", ACTB[:, jj, blk(b)], st[:], pu[:], ALU.mult, [stk, puk], [("A", jj, b)])
            wds = []
            for gj in range(j0, j1, 4):
                n4 = min(4, j1 - gj)
                wdt, wdk = wring.next()
                wdt = wdt[:, 0:n4 * 1024].rearrange("p (j n) -> p j n", j=n4)
                pool_load(wdt, wdv[:, gj:gj + n4, :], wdk, "%s%d" % wdk)
                wds.append((wdt, wdk))
            for c in range(8):
                for b in range(NB):
                    ps, psk = ps_next()
                    for jj in range(nj):
                        wdt, wdk = wds[jj // 4]
                        mm(ps[:], wdt[:, jj % 4, c * 128:(c + 1) * 128], ACTB[:, jj, blk(b)],
                           jj == 0, jj == nj - 1, [wdk, ("A", jj, b)], [psk])
                    stt("dve", H[:, c, blk(b)], ps[:], 0.5, H[:, c, blk(b)], ALU.mult, ALU.add,
                        [psk, ("H", c, b)], [("H", c, b)])
        S.barrier()
        A.off = mark

    def build_tables(cos2, sinS, pcol0, tmp_f, tmp_i, tmp_k):
        sp_load(tmp_i[:], pos[:, pcol0:pcol0 + NT], "tmp_i", "tab")
        for tab, key, shift in ((sinS, "sinS", 0.0), (cos2, "cos2", math.pi / 2)):
            cp("dve", tab[:], tmp_i[:], ["tmp_i"], [key])
            ts("dve", tab[:], tab[:], invf, shift, ALU.mult, ALU.add, [key, "sm"], [key])
            ts("dve", tmp_f[:], tab[:], 1.0 / TWO_PI, None, ALU.mult, None, [key], ["tmp_f"])
            cp("dve", tmp_k[:], tmp_f[:], ["tmp_f"], ["tmp_k"])
            cp("dve", tmp_f[:], tmp_k[:], ["tmp_k"], ["tmp_f"])
            stt("dve", tab[:], tmp_f[:], -TWO_PI, tab[:], ALU.mult, ALU.add, ["tmp_f", key], [key])
            ts("dve", tmp_f[:], tab[:], math.pi, -TWO_PI, ALU.is_gt, ALU.mult, [key], ["tmp_f"])
            tt("dve", tab[:], tab[:], tmp_f[:], ALU.add, [key, "tmp_f"], [key])
            ts("dve", tmp_f[:], tab[:], -math.pi, TWO_PI, ALU.is_lt, ALU.mult, [key], ["tmp_f"])
            tt("dve", tab[:], tab[:], tmp_f[:], ALU.add, [key, "tmp_f"], [key])
            act(tab[:], tab[:], AF.Sin, [key], [key])
        ts("dve", sinS[:], sinS[:], sgn, None, ALU.mult, None, ["sinS", "sm"], ["sinS"])

    def proj_rot(wt, wk, dst, dst_key, t0, ntok, cos2, sinS, qb_ring, t_ring):
        for o0 in range(0, ntok, BL):
            n = min(BL, ntok - o0)
            b = (t0 + o0) // BL
            tsl = slice(t0 + o0, t0 + o0 + n)
            pz, pzk = ps_next()
            for c in range(8):
                mm(pz[:, 0:n], wt[:, c, :], U[:, c, tsl], c == 0, c == 7, [wk, ("U", c, b)], [pzk])
            qb, qbk = qb_ring.next()
            act(qb[:, 0:n], pz[:, 0:n], AF.Copy, [pzk], [qbk])
            pw, pwk = ps_next()
            mm(pw[:, 0:n], perm[:], qb[:, 0:n], True, True, [qbk, "perm"], [pwk])
            t1, t1k = t_ring.next()
            tt("dve", t1[:, 0:n], pz[:, 0:n], cos2[:, tsl], ALU.mult, [pzk, "cos2"], [t1k])
            t2, t2k = t_ring.next()
            tt("dve", t2[:, 0:n], pw[:, 0:n], sinS[:, tsl], ALU.mult, [pwk, "sinS"], [t2k])
            tt("dve", dst[:, o0:o0 + n], t1[:, 0:n], t2[:, 0:n], ALU.add, [t1k, t2k], [(dst_key, o0 // BL)])

    def proj_v(wt, wk, dst, dst_key, tile_starts, d, vring_keys=None):
        for i, st0 in enumerate(tile_starts):
            pv, pvk = ps_next()
            bs = sorted({(st0 + d * q) // BL for q in (0, 127)})
            bs = list(range(bs[0], bs[-1] + 1))
            for c in range(8):
                mm(pv[:, 0:128], U[:, c, st0:st0 + 127 * d + 1:d], wt[:, c, :],
                   c == 0, c == 7, [wk] + [("U", c, b) for b in bs], [pvk])
            cp("act" if i % 2 else "dve", dst[:, i, :], pv[:, 0:128], [pvk], [(dst_key, i)])

    def proj_rot_g(wt, wk, dst, dst_key, t0, ntok, cos2, sinS, qb_ring, t_ring):
        for o0 in range(0, ntok, BL):
            n = min(BL, ntok - o0)
            b = (t0 + o0) // BL
            tsl = slice(t0 + o0, t0 + o0 + n)
            pz, pzk = ps_next()
            for c in range(8):
                mm(pz[:, 0:n], wt[:, c, :], U[:, c, tsl], c == 0, c == 7, [wk, ("U", c, b)], [pzk])
            qb, qbk = qb_ring.next()
            act(qb[:, 0:n], pz[:, 0:n], AF.Copy, [pzk], [qbk])
            pw, pwk = ps_next()
            mm(pw[:, 0:n], perm[:], qb[:, 0:n], True, True, [qbk, "perm"], [pwk])
            t1, t1k = t_ring.next()
            tt("dve", t1[:, 0:n], pz[:, 0:n], cos2[:, tsl], ALU.mult, [pzk, "cos2"], [t1k])
            t2, t2k = t_ring.next()
            tt("dve", t2[:, 0:n], pw[:, 0:n], sinS[:, tsl], ALU.mult, [pwk, "sinS"], [t2k])
            tt("dve", dst[:, o0:o0 + n], t1[:, 0:n], t2[:, 0:n], ALU.add, [t1k, t2k], [(dst_key, o0 // BL)])
            yield

    def proj_v_g(wt, wk, dst, dst_key, tile_starts, d, vring_keys=None):
        for i, st0 in enumerate(tile_starts):
            pv, pvk = ps_next()
            bs = sorted({(st0 + d * q) // BL for q in (0, 127)})
            bs = list(range(bs[0], bs[-1] + 1))
            for c in range(8):
                mm(pv[:, 0:128], U[:, c, st0:st0 + 127 * d + 1:d], wt[:, c, :],
                   c == 0, c == 7, [wk] + [("U", c, b) for b in bs], [pvk])
            cp("act" if i % 2 else "dve", dst[:, i, :], pv[:, 0:128], [pvk], [(dst_key, i)])
            if i % 2:
                yield

    w_in_v = w_in.rearrange("(c p) n -> p c n", p=128)

    def load_w_chunk(ring, col0, ncol=128):
        wt, wk = ring.next()
        wt = wt[:, 0:8 * ncol].rearrange("p (c n) -> p c n", c=8)
        pool_load(wt, w_in_v[:, :, col0:col0 + ncol], wk, "%s%d" % wk)
        return wt, wk

    QOFF, KOFF, VOFF, GOFF = 2048, 2048 + 1536, 2048 + 3072, 2048 + 4608

    checkpoint("init")
    load_xT(xh)
    checkpoint("load")
    ffn(w1g, w1u, w1d, G_FFN1)
    checkpoint("ffn0")
    mark = A.off
    sq_ring = Ring(A, "sq", 2, [128, BL], BF16)
    rt_ring = Ring(A, "rt", 2, [128, BL], F32)
    rmsnorm_to_U(G_MIX, sq_ring, rt_ring)
    checkpoint("p0a")
    cos2 = A.alloc([128, NT], F32)
    sinS = A.alloc([128, NT], F32)
    tmp_f = A.alloc([128, NT], F32)
    tmp_i = A.alloc([128, NT], I32)
    tmp_k = A.alloc([128, NT], I32)
    build_tables(cos2, sinS, NT, tmp_f, tmp_i, tmp_k)
    S.op("sp", lambda e, t_=cos2: e.dma_start(out=tabd[0], in_=t_[:]), reads=["cos2"], writes=["tabd0"], dma="tst0")
    S.op("sp", lambda e, t_=sinS: e.dma_start(out=tabd[1], in_=t_[:]), reads=["sinS"], writes=["tabd1"], dma="tst1")
    build_tables(cos2, sinS, 0, tmp_f, tmp_i, tmp_k)
    checkpoint("p0b")
    wring = Ring(A, "w", int(os.environ.get("KW", "4")), [128, 8 * 128], BF16)
    qb_ring = Ring(A, "qb", 2, [128, BL], BF16)
    t_ring = Ring(A, "t", 4, [128, BL], F32)
    kst_ring = Ring(A, "kst", 2, [128, NT], BF16)
    vst_ring = Ring(A, "vst", 2, [128, 16, 128], BF16)
    for g, (d, nbl) in enumerate(GROUPS):
        nh = 128 * d
        for j in range(4):
            head = 4 * g + j
            if head < int(os.environ.get("KSKIP", "0")):
                continue
            wt, wk = load_w_chunk(wring, KOFF + head * 128)
            kst, kstk = kst_ring.next()
            proj_rot(wt, wk, kst, kstk, NT - nh, nh, cos2, sinS, qb_ring, t_ring)
            if not os.environ.get("KNOSTORE"):
              S.op("sp", lambda e, kst=kst, head=head, nh=nh: e.dma_start(out=khd[head, :, 0:nh], in_=kst[:, 0:nh]),
                 reads=[(kstk, i) for i in range((nh + BL - 1) // BL)], writes=[("khd", head)], dma="hstk%d" % kstk[1])
            if head == 0:
                checkpoint("p0c")
            wt, wk = load_w_chunk(wring, VOFF + head * 128)
            vst, vstk = vst_ring.next()
            starts = [NT - nh + r for r in range(d)]
            proj_v(wt, wk, vst, vstk, starts, d)
            if not os.environ.get("KNOSTORE"):
              S.op("sp", lambda e, vst=vst, head=head, d=d: e.dma_start(out=vhd[head, :, 0:d, :], in_=vst[:, 0:d, :]),
                 reads=[(vstk, i) for i in range(d)], writes=[("vhd", head)], dma="hstv%d" % vstk[1])
            if head == 0:
                checkpoint("p0d")
            if head == 11:
                checkpoint("p0e")
            checkpoint("p0h%d" % head)
    sg_ring = Ring(A, "sg", 2, [128, BL], F32)
    for c in range(8):
        wa, wak = load_w_chunk(wring, c * 128)
        wg_, wgk = load_w_chunk(wring, 1024 + c * 128)
        pa, pak = ps_next()
        pg, pgk = ps_next()
        tsl = slice(NT - 128, NT)
        for k in range(8):
            mm(pa[:, 0:128], wa[:, k, :], U[:, k, tsl], k == 0, k == 7, [wak, ("U", k, 3)], [pak])
        for k in range(8):
            mm(pg[:, 0:128], wg_[:, k, :], U[:, k, tsl], k == 0, k == 7, [wgk, ("U", k, 3)], [pgk])
        sg, sgk = sg_ring.next()
        act(sg[:, 0:128], pg[:, 0:128], AF.Sigmoid, [pgk], [sgk])
        tt("dve", ahalo[:, c, :], sg[:, 96:128], pa[:, 96:128], ALU.mult, [sgk, pak], [("ahalo", c)])
    S.barrier()
    A.off = mark

    checkpoint("p0")
    load_xT(xo)
    ffn(w1g, w1u, w1d, G_FFN1)
    checkpoint("p1")

    mark = A.off
    sq_ring = Ring(A, "sq", 2, [128, BL], BF16)
    rt_ring = Ring(A, "rt", 2, [128, BL], F32)
    rmsnorm_to_U(G_MIX, sq_ring, rt_ring)
    S.barrier()
    A.off = mark
    OT = A.alloc([128, 4, NT], BF16)
    mark2 = A.off

    cos2 = A.alloc([128, NT], F32)
    sinS = A.alloc([128, NT], F32)
    sp_load(cos2[:], tabd[0], "cos2", "tld0")
    sp_load(sinS[:], tabd[1], "sinS", "tld1")
    set_pool([0, 1, 2, 3])
    wring = Ring(A, "w", 3, [128, 8 * 128], BF16)
    qb_ring = Ring(A, "qb", 2, [128, BL], BF16)
    t_ring = Ring(A, "t", 4, [128, BL], F32)
    pt_ring = Ring(A, "pt", 3, [128, 256], BF16)
    q_ring = Ring(A, "QT", 2, [128, NT], BF16)
    k_ring = Ring(A, "KT", 2, [128, NT], BF16)
    v_ring = Ring(A, "VT", 2, [128, 16, 128], BF16)
    KHb = A.alloc([128, NT], BF16)
    VHb = A.alloc([128, 16, 128], BF16)
    OACC = A.alloc([128, NT], F32)
    DEN = A.alloc([128, NT], F32)
    ps_half = {"s": 0, "o": 0}

    def half_next(kind):
        i = ps_half[kind] % 2
        ps_half[kind] += 1
        bank = (4 if kind == "s" else 6) + i
        return PS[bank], 0, ("ps", bank)

    heads = [(j, g) for j in range(4) for g in range(3)]
    hbuf = {}

    def proj_gen(hi):
        j, g = heads[hi]
        d, nbl = GROUPS[g]
        head = 4 * g + j
        QTt, qkey = q_ring.next()
        KTt, kkey = k_ring.next()
        VTt, vkey = v_ring.next()
        hbuf[hi] = (QTt, qkey, KTt, kkey, VTt, vkey)
        wt, wk = load_w_chunk(wring, QOFF + head * 128)
        yield from proj_rot_g(wt, wk, QTt, qkey, 0, NT, cos2, sinS, qb_ring, t_ring)
        wt, wk = load_w_chunk(wring, KOFF + head * 128)
        yield from proj_rot_g(wt, wk, KTt, kkey, 0, NT, cos2, sinS, qb_ring, t_ring)
        wt, wk = load_w_chunk(wring, VOFF + head * 128)
        starts = [128 * n * d + r for r in range(d) for n in range(nbl)]
        yield from proj_v_g(wt, wk, VTt, vkey, starts, d)

    def attn_gen(hi):
        j, g = heads[hi]
        d, nbl = GROUPS[g]
        head = 4 * g + j
        QTt, qkey, KTt, kkey, VTt, vkey = hbuf[hi]
        S.op("sp", lambda e, head=head, d=d: e.dma_start(out=KHb[:, 0:128 * d], in_=khd[head, :, 0:128 * d]),
             reads=[("khd", head)], writes=["KH"], dma="hldk")
        S.op("sp", lambda e, head=head, d=d: e.dma_start(out=VHb[:, 0:d, :], in_=vhd[head, :, 0:d, :]),
             reads=[("vhd", head)], writes=["VH"], dma="hldv")
        blocks = [(r, n) for r in range(d) for n in range(nbl)]

        def emit_scores(r, n):
            s0 = 128 * n * d + r
            cols = slice(s0, s0 + 127 * d + 1, d)
            bs = list(range(s0 // BL, (s0 + 127 * d) // BL + 1))
            qk = [(qkey, b) for b in bs]
            kk = [(kkey, b) for b in bs]
            if n == 0:
                pcols = slice(r, r + 127 * d + 1, d)
                kprev, kpk = KHb[:, pcols], ["KH"]
                msk, mkk = maskH, "maskH"
            else:
                p0 = 128 * (n - 1) * d + r
                pcols = slice(p0, p0 + 127 * d + 1, d)
                pbs = list(range(p0 // BL, (p0 + 127 * d) // BL + 1))
                kprev, kpk = KTt[:, pcols], [(kkey, b) for b in pbs]
                msk, mkk = mask2, "mask2"
            pS, so, psk = half_next("s")
            mm(pS[:, so:so + 256], ident[:], msk[:], True, False, ["ident", mkk], [psk])
            mm(pS[:, so:so + 128], kprev, QTt[:, cols], False, False, kpk + qk, [psk])
            mm(pS[:, so + 128:so + 256], KTt[:, cols], QTt[:, cols], False, True, kk + qk, [psk])
            pt, ptk = pt_ring.next()
            act(pt[:], pS[:, so:so + 256], AF.Exp, [psk], [ptk], scale=SCALE)
            return pt, ptk, cols

        def emit_pv(r, n, pt, ptk, cols):
            if n == 0:
                vprev, vpk = VHb[:, r, :], ["VH"]
            else:
                vprev, vpk = VTt[:, r * nbl + n - 1, :], [(vkey, r * nbl + n - 1)]
            vcur, vck = VTt[:, r * nbl + n, :], [(vkey, r * nbl + n)]
            pO, oo, pok = half_next("o")
            mm(pO[:, oo:oo + 128], vprev, pt[:, 0:128], True, False, vpk + [ptk], [pok])
            mm(pO[:, oo:oo + 128], vcur, pt[:, 128:256], False, True, vck + [ptk], [pok])
            mm(pO[:, oo + 128:oo + 256], ones[:], pt[:, 0:128], True, False, ["ones", ptk], [pok])
            mm(pO[:, oo + 128:oo + 256], ones[:], pt[:, 128:256], False, True, ["ones", ptk], [pok])
            if g == 0:
                cp("dve", OACC[:, cols], pO[:, oo:oo + 128], [pok], ["OACC"])
                cp("dve", DEN[:, cols], pO[:, oo + 128:oo + 256], [pok], ["DEN"])
            else:
                tt("dve", OACC[:, cols], OACC[:, cols], pO[:, oo:oo + 128], ALU.add, [pok, "OACC"], ["OACC"])
                tt("dve", DEN[:, cols], DEN[:, cols], pO[:, oo + 128:oo + 256], ALU.add, [pok, "DEN"], ["DEN"])


        pend = emit_scores(*blocks[0])
        for bi in range(len(blocks)):
            nxt = emit_scores(*blocks[bi + 1]) if bi + 1 < len(blocks) else None
            emit_pv(blocks[bi][0], blocks[bi][1], *pend)
            pend = nxt
            yield
        if g == 2:
            S.op("dve", lambda e: e.reciprocal(out=DEN[:], in_=DEN[:]), ["DEN"], ["DEN"])
            tt("dve", OT[:, j, :], OACC[:], DEN[:], ALU.mult, ["OACC", "DEN"], [("OT", j)])

    for _ in proj_gen(0):
        pass
    for hi in range(len(heads)):
        ga = attn_gen(hi)
        gp = proj_gen(hi + 1) if hi + 1 < len(heads) else iter(())
        a_done = p_done = False
        while not (a_done and p_done):
            if not a_done:
                try:
                    next(ga)
                except StopIteration:
                    a_done = True
            if not p_done:
                try:
                    next(gp)
                    next(gp)
                except StopIteration:
                    p_done = True
    S.barrier()
    A.off = mark2
    CT = A.alloc([128, 8, NT], BF16)
    mark2 = A.off

    checkpoint("attn")
    HT = 1024
    wring = Ring(A, "w", 3, [128, 8 * 128], BF16)
    sg_ring = Ring(A, "sg", 1, [128, BL], F32)
    sqf_ring = Ring(A, "sqf", 1, [128, BL], BF16)
    CONV = A.alloc([128, 8, HT], F32)
    ABF = A.alloc([128, 32 + HT], BF16)
    DG = A.alloc([128, 31, 128], BF16)
    st1 = A.alloc([128, HT], F32)
    st2 = A.alloc([128, HT], F32)
    set_pool([4, 5, 6, 7])
    for hf in range(2):
        t0 = hf * HT
        ps1 = [(PS[0], ("ps", 0)), (PS[1], ("ps", 1))]
        ps2 = [(PS[2], ("ps", 2)), (PS[3], ("ps", 3))]
        for c in range(8):
            wa, wak = load_w_chunk(wring, c * 128)
            wg_, wgk = load_w_chunk(wring, 1024 + c * 128)
            cp("act", ABF[:, 0:32], ahalo[:, c, :], [("ahalo", c)], ["aext_h"])
            for jt in range(31):
                act(DG[:, jt, :], ident[:], AF.Copy, ["ident", "wdw"], [("dg", jt)], scale=wdw[:, c, jt:jt + 1])
            for bb in range(2):
                b = hf * 2 + bb
                tsl = slice(t0 + bb * BL, t0 + (bb + 1) * BL)
                pa, pak = ps_next()
                pg, pgk = ps_next()
                for k in range(8):
                    mm(pa[:], wa[:, k, :], U[:, k, tsl], k == 0, k == 7, [wak, ("U", k, b)], [pak])
                for k in range(8):
                    mm(pg[:], wg_[:, k, :], U[:, k, tsl], k == 0, k == 7, [wgk, ("U", k, b)], [pgk])
                sg, sgk = sg_ring.next()
                act(sg[:], pg[:], AF.Sigmoid, [pgk], [sgk])
                tt("dve", ABF[:, 32 + bb * BL:32 + (bb + 1) * BL], sg[:], pa[:], ALU.mult, [sgk, pak], [("aext", bb)])
            cp("act", ahalo[:, c, :], ABF[:, HT:HT + 32], [("aext", 1)], [("ahalo", c)])
            for bb in range(2):
                pc, pck = ps_next()
                for jt in range(31):
                    o0 = 2 + jt + bb * BL
                    mm(pc[:], DG[:, jt, :], ABF[:, o0:o0 + BL], jt == 0, jt == 30,
                       [("dg", jt), "aext_h", ("aext", 0), ("aext", 1)], [pck])
                ts("dve", CONV[:, c, bb * BL:(bb + 1) * BL], pc[:], gains[:, B_DW, c:c + 1], None, ALU.add, None,
                   [pck, "gains"], [("conv", c, bb)])
            for bb in range(2):
                cb, cbk = sqf_ring.next()
                cp("act", cb[:], CONV[:, c, bb * BL:(bb + 1) * BL], [("conv", c, bb)], [cbk])
                mm(ps1[bb][0][:], ones[:], cb[:], c == 0, c == 7, [cbk, "ones"], [ps1[bb][1]])
                sq, sqk = sqf_ring.next()
                act(sq[:], CONV[:, c, bb * BL:(bb + 1) * BL], AF.Square, [("conv", c, bb)], [sqk])
                mm(ps2[bb][0][:], ones[:], sq[:], c == 0, c == 7, [sqk, "ones"], [ps2[bb][1]])
        for bb in range(2):
            bs = slice(bb * BL, (bb + 1) * BL)
            ts("dve", st1[:, bs], ps1[bb][0][:], 1.0 / D, None, ALU.mult, None, [ps1[bb][1]], [("st1", bb)])
            tt("dve", st2[:, bs], st1[:, bs], st1[:, bs], ALU.mult, [("st1", bb)], [("st2", bb)])
            stt("dve", st2[:, bs], ps2[bb][0][:], 1.0 / D, st2[:, bs], ALU.mult, ALU.subtract, [ps2[bb][1], ("st2", bb)], [("st2", bb)])
            act(st2[:, bs], st2[:, bs], AF.Sqrt, [("st2", bb), "sm"], [("st2", bb)], bias=epsc, scale=1.0)
            S.op("dve", lambda e, bs=bs: e.reciprocal(out=st2[:, bs], in_=st2[:, bs]), [("st2", bb)], [("st2", bb)])
        for c in range(8):
            ck = [("conv", c, 0), ("conv", c, 1)]
            tt("dve", CONV[:, c, :], CONV[:, c, :], st1[:], ALU.subtract, ck + [("st1", 0), ("st1", 1)], ck)
            tt("dve", CONV[:, c, :], CONV[:, c, :], st2[:], ALU.mult, ck + [("st2", 0), ("st2", 1)], ck)
            act(CT[:, c, t0:t0 + HT], CONV[:, c, :], AF.Silu, ck + ["gains"], [("CT", c, hf)],
                bias=gains[:, B_CLN, c:c + 1], scale=gains[:, G_CLN, c:c + 1])
    S.barrier()
    A.off = mark2
    set_pool(range(8))

    checkpoint("conv")
    wring = Ring(A, "w", 2, [128, 8 * 128], BF16)
    wring2 = Ring(A, "w2", 2, [128, 8 * 128], BF16)
    wring3 = Ring(A, "w3", 2, [128, 8 * 128], BF16)
    wring4 = Ring(A, "w4", 2, [128, 4 * 128], BF16)
    sg_ring = Ring(A, "sg", 4, [128, BL], F32)
    M = A.alloc([128, 8, NT], BF16)
    w_co_v = w_co.rearrange("(c p) n -> p c n", p=128)
    w_ao_v = w_ao.rearrange("(c p) n -> p c n", p=128)
    w_mo_v = w_mo.rearrange("(c p) n -> p c n", p=128)
    for c in range(8):
        wga, wgak = load_w_chunk(wring, GOFF + c * 128)
        wgb, wgbk = load_w_chunk(wring2, GOFF + 1024 + c * 128)
        wc, wck = wring3.next()
        wc = wc[:].rearrange("p (c n) -> p c n", c=8)
        pool_load(wc, w_co_v[:, :, c * 128:(c + 1) * 128], wck, "%s%d" % wck)
        wa, wak = wring4.next()
        wa = wa[:].rearrange("p (c n) -> p c n", c=4)
        pool_load(wa, w_ao_v[:, :, c * 128:(c + 1) * 128], wak, "%s%d" % wak)
        for b in range(NB):
            pga, pgak = ps_next()
            pyc, pyck = ps_next()
            pgb, pgbk = ps_next()
            pya, pyak = ps_next()
            for k in range(8):
                mm(pga[:], wga[:, k, :], U[:, k, blk(b)], k == 0, k == 7, [wgak, ("U", k, b)], [pgak])
            for k in range(8):
                mm(pyc[:], wc[:, k, :], CT[:, k, blk(b)], k == 0, k == 7, [wck, ("CT", k, b // 2)], [pyck])
            for k in range(8):
                mm(pgb[:], wgb[:, k, :], U[:, k, blk(b)], k == 0, k == 7, [wgbk, ("U", k, b)], [pgbk])
            for k in range(4):
                mm(pya[:], wa[:, k, :], OT[:, k, blk(b)], k == 0, k == 3, [wak, ("OT", k)], [pyak])
            sa, sak = sg_ring.next()
            sb, sbk = sg_ring.next()
            act(sa[:], pga[:], AF.Sigmoid, [pgak], [sak])
            act(sb[:], pgb[:], AF.Sigmoid, [pgbk], [sbk])
            tt("dve", sa[:], sa[:], pyc[:], ALU.mult, [sak, pyck], [sak])
            tt("dve", sb[:], sb[:], pya[:], ALU.mult, [sbk, pyak], [sbk])
            tt("dve", M[:, c, blk(b)], sa[:], sb[:], ALU.add, [sak, sbk], [("M", c, b)])
    for c in range(8):
        wm, wmk = wring.next()
        wm = wm[:].rearrange("p (c n) -> p c n", c=8)
        pool_load(wm, w_mo_v[:, :, c * 128:(c + 1) * 128], wmk, "%s%d" % wmk)
        for b in range(NB):
            ps, psk = ps_next()
            for k in range(8):
                mm(ps[:], wm[:, k, :], M[:, k, blk(b)], k == 0, k == 7, [wmk, ("M", k, b)], [psk])
            tt("dve", H[:, c, blk(b)], H[:, c, blk(b)], ps[:], ALU.add, [psk, ("H", c, b)], [("H", c, b)])
    S.barrier()
    A.off = mark

    checkpoint("mix")
    ffn(w2g, w2u, w2d, G_FFN2)
    checkpoint("ffn2")

    mark = A.off
    sq_ring = Ring(A, "sq", 2, [128, BL], BF16)
    rt_ring = Ring(A, "rt", 2, [128, BL], F32)
    rmsnorm_to_U(G_PLE, sq_ring, rt_ring)
    PB = A.alloc([128, 2, NT], BF16)
    pool_load(PB[:], pT.rearrange("(c p) t -> p c t", p=128), "PB", "pb")
    wring = Ring(A, "w", 3, [128, 8 * 128], BF16)
    wring4 = Ring(A, "w4", 3, [128, 2 * 128], BF16)
    sg_ring = Ring(A, "sg", 3, [128, BL], F32)
    w_pg_v = w_pg.rearrange("(c p) n -> p c n", p=128)
    w_pp_v = w_pp.rearrange("(c p) n -> p c n", p=128)
    for c in range(8):
        wg_, wgk = wring.next()
        wg_ = wg_[:].rearrange("p (c n) -> p c n", c=8)
        pool_load(wg_, w_pg_v[:, :, c * 128:(c + 1) * 128], wgk, "%s%d" % wgk)
        wp, wpk = wring4.next()
        wp = wp[:].rearrange("p (c n) -> p c n", c=2)
        pool_load(wp, w_pp_v[:, :, c * 128:(c + 1) * 128], wpk, "%s%d" % wpk)
        for b in range(NB):
            pg, pgk = ps_next()
            pp, ppk = ps_next()
            for k in range(8):
                mm(pg[:], wg_[:, k, :], U[:, k, blk(b)], k == 0, k == 7, [wgk, ("U", k, b)], [pgk])
            for k in range(2):
                mm(pp[:], wp[:, k, :], PB[:, k, blk(b)], k == 0, k == 1, [wpk, "PB"], [ppk])
            sg, sgk = sg_ring.next()
            act(sg[:], pg[:], AF.Sigmoid, [pgk], [sgk])
            tt("dve", sg[:], sg[:], pp[:], ALU.mult, [sgk, ppk], [sgk])
            tt("dve", H[:, c, blk(b)], H[:, c, blk(b)], sg[:], ALU.add, [sgk, ("H", c, b)], [("H", c, b)])

    yv = yT.rearrange("(c p) t -> p c t", p=128)
    sqf_ring = Ring(A, "sqf", 2, [128, BL], BF16)
    rtf_ring = Ring(A, "rtf", 2, [128, BL], F32)
    outs = []
    for b in range(NB):
        ps, psk = ps_next()
        for c in range(8):
            sq, sqk = sqf_ring.next()
            act(sq[:], H[:, c, blk(b)], AF.Square, [("H", c, b)], [sqk])
            mm(ps[:], ones[:], sq[:], c == 0, c == 7, [sqk, "ones"], [psk])
        rt, rtk = rtf_ring.next()
        act(rt[:], ps[:], AF.Sqrt, [psk, "sm"], [rtk], bias=epsc, scale=1.0 / D)
        S.op("dve", lambda e, rt=rt: e.reciprocal(out=rt[:], in_=rt[:]), [rtk], [rtk])
        for c in range(8):
            stt("dve", H[:, c, blk(b)], H[:, c, blk(b)], gains[:, G_FIN, c:c + 1], rt[:],
                ALU.mult, ALU.mult, [("H", c, b), rtk, "gains"], [("H", c, b)])
            outs.append(S.op("sp", lambda e, c=c, b=b: e.dma_start(out=yv[:, c, blk(b)], in_=H[:, c, blk(b)]),
                             reads=[("H", c, b)], dma="yout"))
    S.finish()
    S.emit()


_CACHE = {}


def _consts():
    ident = np.eye(128, dtype=np.float32)
    perm = np.zeros((128, 128), np.float32)
    for m in range(128):
        perm[(m + 64) % 128, m] = 1.0
    ones = np.ones((128, 128), np.float32)
    cst = np.stack([ident, perm, ones, ones], axis=1)
    k = np.arange(128)[:, None]
    q = np.arange(128)[None, :]
    mprev = np.where(k >= q, 0.0, NEGM).astype(np.float32)
    mcur = np.where(k <= q, 0.0, NEGM).astype(np.float32)
    mask2 = np.concatenate([mprev, mcur], axis=1)
    invf = (10000.0 ** (-np.arange(0, 128, 2, dtype=np.float32) / 128)).astype(np.float32)
    invf2 = np.concatenate([invf, invf])
    sgn = np.concatenate([-np.ones(64, np.float32), np.ones(64, np.float32)])
    sm = np.stack([invf2, sgn, np.full(128, EPS, np.float32), np.zeros(128, np.float32)], axis=1)
    return cst, mask2, sm.astype(np.float32)


def _pc(v):
    return np.ascontiguousarray(np.asarray(v, np.float32).reshape(8, 128).T)


def kernel(x, p, positions, g_ffn1, w_ffn1_gate, w_ffn1_up, w_ffn1_down, g_mix, w_in,
           w_dw, b_dw, g_conv_ln, b_conv_ln, w_conv_out, w_attn_out, w_mix_out,
           g_ffn2, w_ffn2_gate, w_ffn2_up, w_ffn2_down, g_ple, w_ple_gate, w_ple_proj,
           g_final):
    x = np.asarray(x, np.float32)
    p = np.asarray(p, np.float32)
    positions = np.asarray(positions, np.int32)
    if "nc" not in _CACHE:
        _CACHE["nc"] = build_program()
    nc = _CACHE["nc"]
    cst, mask2, sm = _consts()
    gains = np.stack([_pc(g_ffn1[0]), _pc(g_mix[0]), _pc(g_conv_ln[0]), _pc(b_conv_ln[0]), _pc(b_dw[0]),
                      _pc(g_ffn2[0]), _pc(g_ple[0]), _pc(g_final)], axis=1)
    wdw = np.ascontiguousarray(np.asarray(w_dw[0], np.float32).reshape(31, 8, 128).transpose(2, 1, 0))
    f32 = lambda a: np.ascontiguousarray(np.asarray(a, np.float32))
    shared = {
        "w1g": f32(w_ffn1_gate[0]), "w1u": f32(w_ffn1_up[0]), "w1d": f32(w_ffn1_down[0]),
        "w2g": f32(w_ffn2_gate[0]), "w2u": f32(w_ffn2_up[0]), "w2d": f32(w_ffn2_down[0]),
        "w_in": f32(w_in[0]), "w_co": f32(w_conv_out[0]), "w_ao": f32(w_attn_out[0]), "w_mo": f32(w_mix_out[0]),
        "w_pg": f32(w_ple_gate[0]), "w_pp": f32(w_ple_proj[0]),
        "gains": f32(gains), "wdw": f32(wdw), "cst": f32(cst), "sm": f32(sm),
    }
    in_maps = []
    for core in range(8):
        b, half = core // 2, core % 2
        t0 = half * NT
        xo = np.ascontiguousarray(x[b, t0:t0 + NT, :].T)
        if half == 1:
            xh = np.ascontiguousarray(x[b, 0:NT, :].T)
            ph = positions[b, 0:NT]
        else:
            xh = np.zeros((D, NT), np.float32)
            ph = np.zeros(NT, np.int32)
        pos = np.concatenate([ph, positions[b, t0:t0 + NT]]).astype(np.int32)
        pos = np.ascontiguousarray(np.broadcast_to(pos[None, :], (128, 2 * NT)))
        maskH = mask2.copy()
        if half == 0:
            maskH[:, 0:128] = NEGM
        msk = np.ascontiguousarray(np.stack([mask2, maskH], axis=1))
        m = dict(shared)
        m.update({"xo": xo, "xh": xh, "pT": np.ascontiguousarray(p[0, b, t0:t0 + NT, :].T), "pos": pos, "msk": msk})
        in_maps.append(m)
    import os
    if os.environ.get("KONE"):
        k1 = int(os.environ["KONE"])
        r1 = run_bass_kernel_spmd(nc, [in_maps[k1]], core_ids=[0], trace=bool(os.environ.get("KTRACE")))
        if os.environ.get("KTRACE"):
            print("EXEC_TIME_NS", r1.exec_time_ns)

        class _R:
            results = [r1.results[0] if c == k1 else {"yT": np.zeros((D, NT), np.float32)} for c in range(8)]
        res = _R()
    else:
        res = run_bass_kernel_spmd(nc, in_maps, core_ids=list(range(8)))
    out = np.empty((4, 4096, D), np.float32)
    for core in range(8):
        b, half = core // 2, core % 2
        out[b, half * NT:(half + 1) * NT, :] = res.results[core]["yT"].T
    return out
```

```python
import math
from contextlib import ExitStack

import numpy as np
import concourse.bass as bass
import concourse.mybir as mybir
from concourse.bass_utils import run_bass_kernel_spmd

F32 = mybir.dt.float32
BF16 = mybir.dt.bfloat16
I32 = mybir.dt.int32
ALU = mybir.AluOpType
AF = mybir.ActivationFunctionType

D = 1024
NT = 2048
NB = 4
BL = 512
DFF = 2816
NJ = DFF // 128
INW = 8704
EPS = 1e-6
NEGM = -30000.0
SCALE = 128 ** -0.5
GROUPS = ((1, 16), (4, 4), (16, 1))
TWO_PI = 2.0 * math.pi


class Op:
    __slots__ = ("eng", "fn", "deps", "is_dma", "grp", "val", "need_inc", "count")


class Sched:
    ENGS = ("pe", "act", "dve", "pool", "sp")

    def __init__(self, nc):
        self.nc = nc
        self.ops = {e: [] for e in self.ENGS}
        self.state = {}
        self.groups = {}
        self.pending_dma = []
        self.pending_bar = {e: set() for e in self.ENGS}

    def op(self, eng, fn, reads=(), writes=(), dma=None):
        o = Op()
        o.eng, o.fn, o.deps, o.is_dma, o.grp = eng, fn, set(), dma is not None, dma
        o.need_inc, o.count, o.val = False, 0, 0
        st = self.state
        ps_reads = [k for k in reads if isinstance(k, tuple) and k[0] == "ps"]
        if ps_reads:
            reads = [k for k in reads if k not in ps_reads]
            writes = list(writes) + ps_reads
        for k in reads:
            s = st.get(k)
            if s and s[0] is not None:
                o.deps.add(s[0])
        for k in writes:
            s = st.get(k)
            if s:
                if s[0] is not None:
                    o.deps.add(s[0])
                o.deps.update(s[1])
        for k in reads:
            rl = st.setdefault(k, [None, []])[1]
            if not o.is_dma:
                rl[:] = [r for r in rl if r.is_dma or r.eng != eng]
            rl.append(o)
        for k in writes:
            st[k] = [o, []]
        if self.pending_bar[eng]:
            o.deps.update(self.pending_bar[eng])
            self.pending_bar[eng] = set()
        if dma is not None:
            g = self.groups.setdefault(dma, [None, 0])
            g[1] += 1
            o.val = 16 * g[1]
            self.pending_dma.append(o)
        self.ops[eng].append(o)
        return o

    def barrier(self):
        tails = list(self.pending_dma)
        for e in self.ENGS:
            for o in reversed(self.ops[e]):
                if not o.is_dma:
                    tails.append(o)
                    break
        for e in self.ENGS:
            self.pending_bar[e].update(tails)
        self.pending_dma = []
        self.state = {}

    def finish(self):
        self.barrier()
        self.op("sp", lambda eng: eng.nop())

    def _needs(self, o, d):
        if o.is_dma or d.is_dma or d.eng != o.eng:
            return True
        return o.eng != "pe"

    def emit(self):
        nc = self.nc
        for e in self.ENGS:
            for o in self.ops[e]:
                for d in o.deps:
                    if not d.is_dma and self._needs(o, d):
                        d.need_inc = True
        for e in self.ENGS:
            c = 0
            for o in self.ops[e]:
                if not o.is_dma and o.need_inc:
                    c += 1
                    o.count = c
        with ExitStack() as es:
            sems = {e: es.enter_context(nc.semaphore("s_" + e)) for e in self.ENGS}
            for g in self.groups:
                self.groups[g][0] = es.enter_context(nc.semaphore("d_" + g))
            block = es.enter_context(nc.Block())

            def make(e):
                def body(eng):
                    waited = {}
                    for o in self.ops[e]:
                        ws = {}
                        for d in o.deps:
                            if not self._needs(o, d):
                                continue
                            if d.is_dma:
                                key, sem, val = "d_" + d.grp, self.groups[d.grp][0], d.val
                            else:
                                key, sem, val = "s_" + d.eng, sems[d.eng], d.count
                            if waited.get(key, 0) >= val:
                                continue
                            if key not in ws or ws[key][1] < val:
                                ws[key] = (sem, val)
                        for key, (sem, val) in ws.items():
                            eng.wait_ge(sem, val)
                            waited[key] = val
                        inst = o.fn(eng)
                        if o.is_dma:
                            inst.then_inc(self.groups[o.grp][0], 16)
                        elif o.need_inc:
                            inst.then_inc(sems[e], 1)
                return body

            block.tensor(make("pe"))
            block.scalar(make("act"))
            block.vector(make("dve"))
            block.gpsimd(make("pool"))
            block.sync(make("sp"))


class Arena:
    def __init__(self, nc, limit):
        self.nc, self.off, self.limit, self.n = nc, 16640, limit, 0

    def alloc(self, shape, dt):
        esz = 2 if dt == BF16 else 4
        size = esz
        for s in shape[1:]:
            size *= s
        size = (size + 63) // 64 * 64
        t = self.nc.alloc_sbuf_tensor_at("t%d" % self.n, list(shape), dt, offset=self.off)
        self.n += 1
        self.off += size
        assert self.off <= self.limit, ("SBUF overflow", self.off)
        return t


class Ring:
    def __init__(self, arena, name, n, shape, dt):
        self.t = [arena.alloc(shape, dt) for _ in range(n)]
        self.name, self.i = name, 0

    def next(self):
        i = self.i % len(self.t)
        self.i += 1
        return self.t[i], (self.name, i)


def build_program(dbg=None):
    nc = bass.Bass("TRN2", target_bir_lowering=False)
    try:
        _build_body(nc, dbg)
    except Exception as ex:
        if type(ex).__name__ != "_Stop":
            raise
    return nc


def _build_body(nc, dbg):
    S = None

    def din(name, shape, dt=F32):
        return nc.dram_tensor(name, list(shape), dt, kind="ExternalInput").ap()

    xo = din("xo", [D, NT])
    xh = din("xh", [D, NT])
    pT = din("pT", [256, NT])
    pos = din("pos", [128, 2 * NT], I32)
    w1g, w1u, w1d = din("w1g", [D, DFF]), din("w1u", [D, DFF]), din("w1d", [DFF, D])
    w2g, w2u, w2d = din("w2g", [D, DFF]), din("w2u", [D, DFF]), din("w2d", [DFF, D])
    w_in = din("w_in", [D, INW])
    w_co, w_ao, w_mo = din("w_co", [D, D]), din("w_ao", [512, D]), din("w_mo", [D, D])
    w_pg, w_pp = din("w_pg", [D, D]), din("w_pp", [256, D])
    gains_d = din("gains", [128, 8, 8])
    wdw_d = din("wdw", [128, 8, 31])
    cst_d = din("cst", [128, 4, 128])
    msk_d = din("msk", [128, 2, 256])
    sm_d = din("sm", [128, 4])
    yT = nc.dram_tensor("yT", [D, NT], F32, kind="ExternalOutput").ap()
    khd = nc.dram_tensor("khd", [12, 128, NT], BF16, kind="Internal").ap()
    vhd = nc.dram_tensor("vhd", [12, 128, 16, 128], BF16, kind="Internal").ap()
    dbg_out = None
    if dbg:
        dbg_out = nc.dram_tensor("dbg", [D, NT], F32, kind="ExternalOutput").ap()

    S = Sched(nc)
    A = Arena(nc, 229376)
    PS = [nc.alloc_psum_tensor("ps%d" % i, [128, 512], F32) for i in range(8)]
    ps_pool = {"banks": list(range(8)), "i": 0}

    def set_pool(banks):
        ps_pool["banks"] = list(banks)
        ps_pool["i"] = 0

    def ps_next():
        bk = ps_pool["banks"]
        i = bk[ps_pool["i"] % len(bk)]
        ps_pool["i"] += 1
        return PS[i], ("ps", i)

    gains = A.alloc([128, 8, 8], F32)
    wdw = A.alloc([128, 8, 31], F32)
    sm = A.alloc([128, 4], F32)
    ident = A.alloc([128, 128], BF16)
    perm = A.alloc([128, 128], BF16)
    ones = A.alloc([128, 128], BF16)
    mask2 = A.alloc([128, 256], BF16)
    maskH = A.alloc([128, 256], BF16)
    ahalo = A.alloc([128, 8, 32], F32)
    H = A.alloc([128, 8, NT], F32)
    U = A.alloc([128, 8, NT], BF16)
    base_mark = A.off

    G_FFN1, G_MIX, G_CLN, B_CLN, B_DW, G_FFN2, G_PLE, G_FIN = range(8)
    invf, sgn, epsc = sm[:, 0:1], sm[:, 1:2], sm[:, 2:3]

    def sp_load(dst, src, key, grp):
        S.op("sp", lambda e: e.dma_start(out=dst, in_=src), writes=[key], dma=grp)

    def pool_load(dst, src, key, grp):
        S.op("pool", lambda e: e.dma_start(out=dst, in_=src), writes=[key], dma=grp)

    sp_load(gains[:], gains_d, "gains", "c0")
    sp_load(wdw[:], wdw_d, "wdw", "c0")
    sp_load(sm[:], sm_d, "sm", "c0")
    pool_load(ident[:], cst_d[:, 0, :], "ident", "c1")
    pool_load(perm[:], cst_d[:, 1, :], "perm", "c1")
    pool_load(ones[:], cst_d[:, 2, :], "ones", "c1")
    pool_load(mask2[:], msk_d[:, 0, :], "mask2", "c1")
    pool_load(maskH[:], msk_d[:, 1, :], "maskH", "c1")
    S.barrier()

    def blk(b):
        return slice(b * BL, (b + 1) * BL)

    import os
    STOP = os.environ.get("KSTOP", "")

    class _Stop(Exception):
        pass

    def checkpoint(name):
        if STOP == name:
            yv_ = yT.rearrange("(c p) t -> p c t", p=128)
            for c in range(8):
                S.op("sp", lambda e, c=c: e.dma_start(out=yv_[:, c, :], in_=H[:, c, :]),
                     reads=[("H", c, b) for b in range(NB)], dma="yout")
            S.finish()
            S.emit()
            raise _Stop()

    def mm(out, lhsT, rhs, start, stop, reads, writes):
        S.op("pe", lambda e: e.matmul(out, lhsT=lhsT, rhs=rhs, start=start, stop=stop), reads, writes)

    def act(out, in_, func, reads, writes, bias=None, scale=None):
        kw = {}
        if bias is not None:
            kw["bias"] = bias
        if scale is not None:
            kw["scale"] = scale
        S.op("act", lambda e: e.activation(out=out, in_=in_, func=func, **kw), reads, writes)

    def tt(eng, out, in0, in1, op, reads, writes):
        S.op(eng, lambda e: e.tensor_tensor(out=out, in0=in0, in1=in1, op=op), reads, writes)

    def stt(eng, out, in0, scalar, in1, op0, op1, reads, writes):
        S.op(eng, lambda e: e.scalar_tensor_tensor(out=out, in0=in0, scalar=scalar, in1=in1, op0=op0, op1=op1), reads, writes)

    def ts(eng, out, in0, s1, s2, op0, op1, reads, writes):
        if s2 is None:
            S.op(eng, lambda e: e.tensor_scalar(out=out, in0=in0, scalar1=s1, scalar2=None, op0=op0), reads, writes)
        else:
            S.op(eng, lambda e: e.tensor_scalar(out=out, in0=in0, scalar1=s1, scalar2=s2, op0=op0, op1=op1), reads, writes)

    def cp(eng, out, in_, reads, writes):
        if eng == "act":
            S.op(eng, lambda e: e.copy(out=out, in_=in_), reads, writes)
        else:
            S.op(eng, lambda e: e.tensor_copy(out=out, in_=in_), reads, writes)

    def load_xT(src):
        v = src.rearrange("(c p) t -> p c t", p=128)
        for c in range(8):
            S.op("sp", lambda e, c=c: e.dma_start(out=H[:, c, :], in_=v[:, c, :]),
                 writes=[("H", c, b) for b in range(NB)], dma="xin%d" % c)

    def rmsnorm_to_U(which, sq_ring, rt_ring):
        for b in range(NB):
            ps, psk = ps_next()
            for c in range(8):
                sq, sqk = sq_ring.next()
                act(sq[:], H[:, c, blk(b)], AF.Square, [("H", c, b)], [sqk])
                mm(ps[:], ones[:], sq[:], c == 0, c == 7, [sqk, "ones"], [psk])
            rt, rtk = rt_ring.next()
            act(rt[:], ps[:], AF.Sqrt, [psk, "sm"], [rtk], bias=epsc, scale=1.0 / D)
            S.op("dve", lambda e, rt=rt: e.reciprocal(out=rt[:], in_=rt[:]), [rtk], [rtk])
            for c in range(8):
                stt("dve", U[:, c, blk(b)], H[:, c, blk(b)], gains[:, which, c:c + 1], rt[:],
                    ALU.mult, ALU.mult, [("H", c, b), rtk, "gains"], [("U", c, b)])

    def ffn(wg_d, wu_d, wd_d, gain_idx):
        mark = A.off
        sq_ring = Ring(A, "sq", 2, [128, BL], BF16)
        rt_ring = Ring(A, "rt", 2, [128, BL], F32)
        st_ring = Ring(A, "st", 2, [128, BL], F32)
        ACTB = A.alloc([128, 8, NT], BF16)
        wring = Ring(A, "w", 6, [128, 4096], BF16)
        rmsnorm_to_U(gain_idx, sq_ring, rt_ring)
        wgv = wg_d.rearrange("(c p) n -> p c n", p=128)
        wuv = wu_d.rearrange("(c p) n -> p c n", p=128)
        wdv = wd_d.rearrange("(j p) n -> p j n", p=128)
        for (j0, j1) in ((0, 8), (8, 16), (16, 22)):
            nj = j1 - j0
            for gj in range(j0, j1, 4):
                n4 = min(4, j1 - gj)
                wgt, wgk = wring.next()
                wgt = wgt[:, 0:8 * n4 * 128].rearrange("p (c n) -> p c n", c=8)
                pool_load(wgt, wgv[:, :, gj * 128:(gj + n4) * 128], wgk, "%s%d" % wgk)
                wut, wuk = wring.next()
                wut = wut[:, 0:8 * n4 * 128].rearrange("p (c n) -> p c n", c=8)
                pool_load(wut, wuv[:, :, gj * 128:(gj + n4) * 128], wuk, "%s%d" % wuk)
                for jl in range(n4):
                    jj = gj + jl - j0
                    cs = slice(jl * 128, (jl + 1) * 128)
                    for b in range(NB):
                        pg, pgk = ps_next()
                        pu, puk = ps_next()
                        for c in range(8):
                            mm(pg[:], wgt[:, c, cs], U[:, c, blk(b)], c == 0, c == 7, [wgk, ("U", c, b)], [pgk])
                        for c in range(8):
                            mm(pu[:], wut[:, c, cs], U[:, c, blk(b)], c == 0, c == 7, [wuk, ("U", c, b)], [puk])
                        st, stk = st_ring.next()
                        act(st[:], pg[:], AF.Silu, [pgk], [stk])
                        tt("dve", ACTB[:, jj, blk(b)], st[:], pu[:], ALU.mult, [stk, puk], [("A", jj, b)])
            wds = []
            for gj in range(j0, j1, 4):
                n4 = min(4, j1 - gj)
                wdt, wdk = wring.next()
                wdt = wdt[:, 0:n4 * 1024].rearrange("p (j n) -> p j n", j=n4)
                pool_load(wdt, wdv[:, gj:gj + n4, :], wdk, "%s%d" % wdk)
                wds.append((wdt, wdk))
            for c in range(8):
                for b in range(NB):
                    ps, psk = ps_next()
                    for jj in range(nj):
                        wdt, wdk = wds[jj // 4]
                        mm(ps[:], wdt[:, jj % 4, c * 128:(c + 1) * 128], ACTB[:, jj, blk(b)],
                           jj == 0, jj == nj - 1, [wdk, ("A", jj, b)], [psk])
                    stt("dve", H[:, c, blk(b)], ps[:], 0.5, H[:, c, blk(b)], ALU.mult, ALU.add,
                        [psk, ("H", c, b)], [("H", c, b)])
        S.barrier()
        A.off = mark

    def build_tables(cos2, sinS, pcol0, tmp_f, tmp_i, tmp_k):
        sp_load(tmp_i[:], pos[:, pcol0:pcol0 + NT], "tmp_i", "tab")
        for tab, key, shift in ((sinS, "sinS", 0.0), (cos2, "cos2", math.pi / 2)):
            cp("dve", tab[:], tmp_i[:], ["tmp_i"], [key])
            ts("dve", tab[:], tab[:], invf, shift, ALU.mult, ALU.add, [key, "sm"], [key])
            ts("dve", tmp_f[:], tab[:], 1.0 / TWO_PI, None, ALU.mult, None, [key], ["tmp_f"])
            cp("dve", tmp_k[:], tmp_f[:], ["tmp_f"], ["tmp_k"])
            cp("dve", tmp_f[:], tmp_k[:], ["tmp_k"], ["tmp_f"])
            stt("dve", tab[:], tmp_f[:], -TWO_PI, tab[:], ALU.mult, ALU.add, ["tmp_f", key], [key])
            ts("dve", tmp_f[:], tab[:], math.pi, -TWO_PI, ALU.is_gt, ALU.mult, [key], ["tmp_f"])
            tt("dve", tab[:], tab[:], tmp_f[:], ALU.add, [key, "tmp_f"], [key])
            ts("dve", tmp_f[:], tab[:], -math.pi, TWO_PI, ALU.is_lt, ALU.mult, [key], ["tmp_f"])
            tt("dve", tab[:], tab[:], tmp_f[:], ALU.add, [key, "tmp_f"], [key])
            act(tab[:], tab[:], AF.Sin, [key], [key])
        ts("dve", sinS[:], sinS[:], sgn, None, ALU.mult, None, ["sinS", "sm"], ["sinS"])

    def proj_rot(wt, wk, dst, dst_key, t0, ntok, cos2, sinS, qb_ring, t_ring):
        for o0 in range(0, ntok, BL):
            n = min(BL, ntok - o0)
            b = (t0 + o0) // BL
            tsl = slice(t0 + o0, t0 + o0 + n)
            pz, pzk = ps_next()
            for c in range(8):
                mm(pz[:, 0:n], wt[:, c, :], U[:, c, tsl], c == 0, c == 7, [wk, ("U", c, b)], [pzk])
            qb, qbk = qb_ring.next()
            act(qb[:, 0:n], pz[:, 0:n], AF.Copy, [pzk], [qbk])
            pw, pwk = ps_next()
            mm(pw[:, 0:n], perm[:], qb[:, 0:n], True, True, [qbk, "perm"], [pwk])
            t1, t1k = t_ring.next()
            tt("dve", t1[:, 0:n], pz[:, 0:n], cos2[:, tsl], ALU.mult, [pzk, "cos2"], [t1k])
            t2, t2k = t_ring.next()
            tt("dve", t2[:, 0:n], pw[:, 0:n], sinS[:, tsl], ALU.mult, [pwk, "sinS"], [t2k])
            tt("dve", dst[:, o0:o0 + n], t1[:, 0:n], t2[:, 0:n], ALU.add, [t1k, t2k], [(dst_key, o0 // BL)])

    def proj_v(wt, wk, dst, dst_key, tile_starts, d, vring_keys=None):
        for i, st0 in enumerate(tile_starts):
            pv, pvk = ps_next()
            bs = sorted({(st0 + d * q) // BL for q in (0, 127)})
            bs = list(range(bs[0], bs[-1] + 1))
            for c in range(8):
                mm(pv[:, 0:128], U[:, c, st0:st0 + 127 * d + 1:d], wt[:, c, :],
                   c == 0, c == 7, [wk] + [("U", c, b) for b in bs], [pvk])
            cp("act" if i % 2 else "dve", dst[:, i, :], pv[:, 0:128], [pvk], [(dst_key, i)])

    def proj_rot_g(wt, wk, dst, dst_key, t0, ntok, cos2, sinS, qb_ring, t_ring):
        for o0 in range(0, ntok, BL):
            n = min(BL, ntok - o0)
            b = (t0 + o0) // BL
            tsl = slice(t0 + o0, t0 + o0 + n)
            pz, pzk = ps_next()
            for c in range(8):
                mm(pz[:, 0:n], wt[:, c, :], U[:, c, tsl], c == 0, c == 7, [wk, ("U", c, b)], [pzk])
            qb, qbk = qb_ring.next()
            act(qb[:, 0:n], pz[:, 0:n], AF.Copy, [pzk], [qbk])
            pw, pwk = ps_next()
            mm(pw[:, 0:n], perm[:], qb[:, 0:n], True, True, [qbk, "perm"], [pwk])
            t1, t1k = t_ring.next()
            tt("dve", t1[:, 0:n], pz[:, 0:n], cos2[:, tsl], ALU.mult, [pzk, "cos2"], [t1k])
            t2, t2k = t_ring.next()
            tt("dve", t2[:, 0:n], pw[:, 0:n], sinS[:, tsl], ALU.mult, [pwk, "sinS"], [t2k])
            tt("dve", dst[:, o0:o0 + n], t1[:, 0:n], t2[:, 0:n], ALU.add, [t1k, t2k], [(dst_key, o0 // BL)])
            yield

    def proj_v_g(wt, wk, dst, dst_key, tile_starts, d, vring_keys=None):
        for i, st0 in enumerate(tile_starts):
            pv, pvk = ps_next()
            bs = sorted({(st0 + d * q) // BL for q in (0, 127)})
            bs = list(range(bs[0], bs[-1] + 1))
            for c in range(8):
                mm(pv[:, 0:128], U[:, c, st0:st0 + 127 * d + 1:d], wt[:, c, :],
                   c == 0, c == 7, [wk] + [("U", c, b) for b in bs], [pvk])
            cp("act" if i % 2 else "dve", dst[:, i, :], pv[:, 0:128], [pvk], [(dst_key, i)])
            if i % 2:
                yield

    w_in_v = w_in.rearrange("(c p) n -> p c n", p=128)

    def load_w_chunk(ring, col0, ncol=128):
        wt, wk = ring.next()
        wt = wt[:, 0:8 * ncol].rearrange("p (c n) -> p c n", c=8)
        pool_load(wt, w_in_v[:, :, col0:col0 + ncol], wk, "%s%d" % wk)
        return wt, wk

    QOFF, KOFF, VOFF, GOFF = 2048, 2048 + 1536, 2048 + 3072, 2048 + 4608

    checkpoint("init")
    load_xT(xh)
    checkpoint("load")
    ffn(w1g, w1u, w1d, G_FFN1)
    checkpoint("ffn0")
    mark = A.off
    sq_ring = Ring(A, "sq", 2, [128, BL], BF16)
    rt_ring = Ring(A, "rt", 2, [128, BL], F32)
    rmsnorm_to_U(G_MIX, sq_ring, rt_ring)
    checkpoint("p0a")
    cos2 = A.alloc([128, NT], F32)
    sinS = A.alloc([128, NT], F32)
    tmp_f = A.alloc([128, NT], F32)
    tmp_i = A.alloc([128, NT], I32)
    tmp_k = A.alloc([128, NT], I32)
    build_tables(cos2, sinS, 0, tmp_f, tmp_i, tmp_k)
    checkpoint("p0b")
    wring = Ring(A, "w", int(os.environ.get("KW", "4")), [128, 8 * 128], BF16)
    qb_ring = Ring(A, "qb", 2, [128, BL], BF16)
    t_ring = Ring(A, "t", 4, [128, BL], F32)
    kst_ring = Ring(A, "kst", 2, [128, NT], BF16)
    vst_ring = Ring(A, "vst", 2, [128, 16, 128], BF16)
    for g, (d, nbl) in enumerate(GROUPS):
        nh = 128 * d
        for j in range(4):
            head = 4 * g + j
            if head < int(os.environ.get("KSKIP", "0")):
                continue
            wt, wk = load_w_chunk(wring, KOFF + head * 128)
            kst, kstk = kst_ring.next()
            proj_rot(wt, wk, kst, kstk, NT - nh, nh, cos2, sinS, qb_ring, t_ring)
            if not os.environ.get("KNOSTORE"):
              S.op("sp", lambda e, kst=kst, head=head, nh=nh: e.dma_start(out=khd[head, :, 0:nh], in_=kst[:, 0:nh]),
                 reads=[(kstk, i) for i in range((nh + BL - 1) // BL)], writes=[("khd", head)], dma="hstk%d" % kstk[1])
            if head == 0:
                checkpoint("p0c")
            wt, wk = load_w_chunk(wring, VOFF + head * 128)
            vst, vstk = vst_ring.next()
            starts = [NT - nh + r for r in range(d)]
            proj_v(wt, wk, vst, vstk, starts, d)
            if not os.environ.get("KNOSTORE"):
              S.op("sp", lambda e, vst=vst, head=head, d=d: e.dma_start(out=vhd[head, :, 0:d, :], in_=vst[:, 0:d, :]),
                 reads=[(vstk, i) for i in range(d)], writes=[("vhd", head)], dma="hstv%d" % vstk[1])
            if head == 0:
                checkpoint("p0d")
            if head == 11:
                checkpoint("p0e")
            checkpoint("p0h%d" % head)
    sg_ring = Ring(A, "sg", 2, [128, BL], F32)
    for c in range(8):
        wa, wak = load_w_chunk(wring, c * 128)
        wg_, wgk = load_w_chunk(wring, 1024 + c * 128)
        pa, pak = ps_next()
        pg, pgk = ps_next()
        tsl = slice(NT - 128, NT)
        for k in range(8):
            mm(pa[:, 0:128], wa[:, k, :], U[:, k, tsl], k == 0, k == 7, [wak, ("U", k, 3)], [pak])
        for k in range(8):
            mm(pg[:, 0:128], wg_[:, k, :], U[:, k, tsl], k == 0, k == 7, [wgk, ("U", k, 3)], [pgk])
        sg, sgk = sg_ring.next()
        act(sg[:, 0:128], pg[:, 0:128], AF.Sigmoid, [pgk], [sgk])
        tt("dve", ahalo[:, c, :], sg[:, 96:128], pa[:, 96:128], ALU.mult, [sgk, pak], [("ahalo", c)])
    S.barrier()
    A.off = mark

    checkpoint("p0")
    load_xT(xo)
    ffn(w1g, w1u, w1d, G_FFN1)
    checkpoint("p1")

    mark = A.off
    sq_ring = Ring(A, "sq", 2, [128, BL], BF16)
    rt_ring = Ring(A, "rt", 2, [128, BL], F32)
    rmsnorm_to_U(G_MIX, sq_ring, rt_ring)
    S.barrier()
    A.off = mark
    OT = A.alloc([128, 4, NT], BF16)
    mark2 = A.off

    cos2 = A.alloc([128, NT], F32)
    sinS = A.alloc([128, NT], F32)
    mark3 = A.off
    tmp_f = A.alloc([128, NT], F32)
    tmp_i = A.alloc([128, NT], I32)
    tmp_k = A.alloc([128, NT], I32)
    build_tables(cos2, sinS, NT, tmp_f, tmp_i, tmp_k)
    S.barrier()
    A.off = mark3
    set_pool([0, 1, 2, 3])
    wring = Ring(A, "w", 3, [128, 8 * 128], BF16)
    qb_ring = Ring(A, "qb", 2, [128, BL], BF16)
    t_ring = Ring(A, "t", 4, [128, BL], F32)
    pt_ring = Ring(A, "pt", 3, [128, 256], BF16)
    q_ring = Ring(A, "QT", 2, [128, NT], BF16)
    k_ring = Ring(A, "KT", 2, [128, NT], BF16)
    v_ring = Ring(A, "VT", 2, [128, 16, 128], BF16)
    KHb = A.alloc([128, NT], BF16)
    VHb = A.alloc([128, 16, 128], BF16)
    OACC = A.alloc([128, NT], F32)
    DEN = A.alloc([128, NT], F32)
    ps_half = {"s": 0, "o": 0}

    def half_next(kind):
        i = ps_half[kind] % 2
        ps_half[kind] += 1
        bank = (4 if kind == "s" else 6) + i
        return PS[bank], 0, ("ps", bank)

    heads = [(j, g) for j in range(4) for g in range(3)]
    hbuf = {}

    def proj_gen(hi):
        j, g = heads[hi]
        d, nbl = GROUPS[g]
        head = 4 * g + j
        QTt, qkey = q_ring.next()
        KTt, kkey = k_ring.next()
        VTt, vkey = v_ring.next()
        hbuf[hi] = (QTt, qkey, KTt, kkey, VTt, vkey)
        wt, wk = load_w_chunk(wring, QOFF + head * 128)
        yield from proj_rot_g(wt, wk, QTt, qkey, 0, NT, cos2, sinS, qb_ring, t_ring)
        wt, wk = load_w_chunk(wring, KOFF + head * 128)
        yield from proj_rot_g(wt, wk, KTt, kkey, 0, NT, cos2, sinS, qb_ring, t_ring)
        wt, wk = load_w_chunk(wring, VOFF + head * 128)
        starts = [128 * n * d + r for r in range(d) for n in range(nbl)]
        yield from proj_v_g(wt, wk, VTt, vkey, starts, d)

    def attn_gen(hi):
        j, g = heads[hi]
        d, nbl = GROUPS[g]
        head = 4 * g + j
        QTt, qkey, KTt, kkey, VTt, vkey = hbuf[hi]
        S.op("sp", lambda e, head=head, d=d: e.dma_start(out=KHb[:, 0:128 * d], in_=khd[head, :, 0:128 * d]),
             reads=[("khd", head)], writes=["KH"], dma="hldk")
        S.op("sp", lambda e, head=head, d=d: e.dma_start(out=VHb[:, 0:d, :], in_=vhd[head, :, 0:d, :]),
             reads=[("vhd", head)], writes=["VH"], dma="hldv")
        blocks = [(r, n) for r in range(d) for n in range(nbl)]

        def emit_scores(r, n):
            s0 = 128 * n * d + r
            cols = slice(s0, s0 + 127 * d + 1, d)
            bs = list(range(s0 // BL, (s0 + 127 * d) // BL + 1))
            qk = [(qkey, b) for b in bs]
            kk = [(kkey, b) for b in bs]
            if n == 0:
                pcols = slice(r, r + 127 * d + 1, d)
                kprev, kpk = KHb[:, pcols], ["KH"]
                msk, mkk = maskH, "maskH"
            else:
                p0 = 128 * (n - 1) * d + r
                pcols = slice(p0, p0 + 127 * d + 1, d)
                pbs = list(range(p0 // BL, (p0 + 127 * d) // BL + 1))
                kprev, kpk = KTt[:, pcols], [(kkey, b) for b in pbs]
                msk, mkk = mask2, "mask2"
            pS, so, psk = half_next("s")
            mm(pS[:, so:so + 256], ident[:], msk[:], True, False, ["ident", mkk], [psk])
            mm(pS[:, so:so + 128], kprev, QTt[:, cols], False, False, kpk + qk, [psk])
            mm(pS[:, so + 128:so + 256], KTt[:, cols], QTt[:, cols], False, True, kk + qk, [psk])
            pt, ptk = pt_ring.next()
            act(pt[:], pS[:, so:so + 256], AF.Exp, [psk], [ptk], scale=SCALE)
            return pt, ptk, cols

        def emit_pv(r, n, pt, ptk, cols):
            if n == 0:
                vprev, vpk = VHb[:, r, :], ["VH"]
            else:
                vprev, vpk = VTt[:, r * nbl + n - 1, :], [(vkey, r * nbl + n - 1)]
            vcur, vck = VTt[:, r * nbl + n, :], [(vkey, r * nbl + n)]
            pO, oo, pok = half_next("o")
            mm(pO[:, oo:oo + 128], vprev, pt[:, 0:128], True, False, vpk + [ptk], [pok])
            mm(pO[:, oo:oo + 128], vcur, pt[:, 128:256], False, True, vck + [ptk], [pok])
            mm(pO[:, oo + 128:oo + 256], ones[:], pt[:, 0:128], True, False, ["ones", ptk], [pok])
            mm(pO[:, oo + 128:oo + 256], ones[:], pt[:, 128:256], False, True, ["ones", ptk], [pok])
            if g == 0:
                cp("dve", OACC[:, cols], pO[:, oo:oo + 128], [pok], ["OACC"])
                cp("dve", DEN[:, cols], pO[:, oo + 128:oo + 256], [pok], ["DEN"])
            else:
                tt("dve", OACC[:, cols], OACC[:, cols], pO[:, oo:oo + 128], ALU.add, [pok, "OACC"], ["OACC"])
                tt("dve", DEN[:, cols], DEN[:, cols], pO[:, oo + 128:oo + 256], ALU.add, [pok, "DEN"], ["DEN"])


        pend = emit_scores(*blocks[0])
        for bi in range(len(blocks)):
            nxt = emit_scores(*blocks[bi + 1]) if bi + 1 < len(blocks) else None
            emit_pv(blocks[bi][0], blocks[bi][1], *pend)
            pend = nxt
            yield
        if g == 2:
            S.op("dve", lambda e: e.reciprocal(out=DEN[:], in_=DEN[:]), ["DEN"], ["DEN"])
            tt("dve", OT[:, j, :], OACC[:], DEN[:], ALU.mult, ["OACC", "DEN"], [("OT", j)])

    for _ in proj_gen(0):
        pass
    for hi in range(len(heads)):
        ga = attn_gen(hi)
        gp = proj_gen(hi + 1) if hi + 1 < len(heads) else iter(())
        a_done = p_done = False
        while not (a_done and p_done):
            if not a_done:
                try:
                    next(ga)
                except StopIteration:
                    a_done = True
            if not p_done:
                try:
                    next(gp)
                    next(gp)
                except StopIteration:
                    p_done = True
    S.barrier()
    A.off = mark2
    CT = A.alloc([128, 8, NT], BF16)
    mark2 = A.off

    checkpoint("attn")
    HT = 1024
    wring = Ring(A, "w", 3, [128, 8 * 128], BF16)
    sg_ring = Ring(A, "sg", 1, [128, BL], F32)
    sqf_ring = Ring(A, "sqf", 1, [128, BL], BF16)
    CONV = A.alloc([128, 8, HT], F32)
    ABF = A.alloc([128, 32 + HT], BF16)
    DG = A.alloc([128, 31, 128], BF16)
    st1 = A.alloc([128, HT], F32)
    st2 = A.alloc([128, HT], F32)
    set_pool([4, 5, 6, 7])
    for hf in range(2):
        t0 = hf * HT
        ps1 = [(PS[0], ("ps", 0)), (PS[1], ("ps", 1))]
        ps2 = [(PS[2], ("ps", 2)), (PS[3], ("ps", 3))]
        for c in range(8):
            wa, wak = load_w_chunk(wring, c * 128)
            wg_, wgk = load_w_chunk(wring, 1024 + c * 128)
            cp("act", ABF[:, 0:32], ahalo[:, c, :], [("ahalo", c)], ["aext_h"])
            for jt in range(31):
                act(DG[:, jt, :], ident[:], AF.Copy, ["ident", "wdw"], [("dg", jt)], scale=wdw[:, c, jt:jt + 1])
            for bb in range(2):
                b = hf * 2 + bb
                tsl = slice(t0 + bb * BL, t0 + (bb + 1) * BL)
                pa, pak = ps_next()
                pg, pgk = ps_next()
                for k in range(8):
                    mm(pa[:], wa[:, k, :], U[:, k, tsl], k == 0, k == 7, [wak, ("U", k, b)], [pak])
                for k in range(8):
                    mm(pg[:], wg_[:, k, :], U[:, k, tsl], k == 0, k == 7, [wgk, ("U", k, b)], [pgk])
                sg, sgk = sg_ring.next()
                act(sg[:], pg[:], AF.Sigmoid, [pgk], [sgk])
                tt("dve", ABF[:, 32 + bb * BL:32 + (bb + 1) * BL], sg[:], pa[:], ALU.mult, [sgk, pak], [("aext", bb)])
            cp("act", ahalo[:, c, :], ABF[:, HT:HT + 32], [("aext", 1)], [("ahalo", c)])
            for bb in range(2):
                pc, pck = ps_next()
                for jt in range(31):
                    o0 = 2 + jt + bb * BL
                    mm(pc[:], DG[:, jt, :], ABF[:, o0:o0 + BL], jt == 0, jt == 30,
                       [("dg", jt), "aext_h", ("aext", 0), ("aext", 1)], [pck])
                ts("dve", CONV[:, c, bb * BL:(bb + 1) * BL], pc[:], gains[:, B_DW, c:c + 1], None, ALU.add, None,
                   [pck, "gains"], [("conv", c, bb)])
            for bb in range(2):
                cb, cbk = sqf_ring.next()
                cp("act", cb[:], CONV[:, c, bb * BL:(bb + 1) * BL], [("conv", c, bb)], [cbk])
                mm(ps1[bb][0][:], ones[:], cb[:], c == 0, c == 7, [cbk, "ones"], [ps1[bb][1]])
                sq, sqk = sqf_ring.next()
                act(sq[:], CONV[:, c, bb * BL:(bb + 1) * BL], AF.Square, [("conv", c, bb)], [sqk])
                mm(ps2[bb][0][:], ones[:], sq[:], c == 0, c == 7, [sqk, "ones"], [ps2[bb][1]])
        for bb in range(2):
            bs = slice(bb * BL, (bb + 1) * BL)
            ts("dve", st1[:, bs], ps1[bb][0][:], 1.0 / D, None, ALU.mult, None, [ps1[bb][1]], [("st1", bb)])
            tt("dve", st2[:, bs], st1[:, bs], st1[:, bs], ALU.mult, [("st1", bb)], [("st2", bb)])
            stt("dve", st2[:, bs], ps2[bb][0][:], 1.0 / D, st2[:, bs], ALU.mult, ALU.subtract, [ps2[bb][1], ("st2", bb)], [("st2", bb)])
            act(st2[:, bs], st2[:, bs], AF.Sqrt, [("st2", bb), "sm"], [("st2", bb)], bias=epsc, scale=1.0)
            S.op("dve", lambda e, bs=bs: e.reciprocal(out=st2[:, bs], in_=st2[:, bs]), [("st2", bb)], [("st2", bb)])
        for c in range(8):
            ck = [("conv", c, 0), ("conv", c, 1)]
            tt("dve", CONV[:, c, :], CONV[:, c, :], st1[:], ALU.subtract, ck + [("st1", 0), ("st1", 1)], ck)
            tt("dve", CONV[:, c, :], CONV[:, c, :], st2[:], ALU.mult, ck + [("st2", 0), ("st2", 1)], ck)
            act(CT[:, c, t0:t0 + HT], CONV[:, c, :], AF.Silu, ck + ["gains"], [("CT", c, hf)],
                bias=gains[:, B_CLN, c:c + 1], scale=gains[:, G_CLN, c:c + 1])
    S.barrier()
    A.off = mark2
    set_pool(range(8))

    checkpoint("conv")
    wring = Ring(A, "w", 2, [128, 8 * 128], BF16)
    wring2 = Ring(A, "w2", 2, [128, 8 * 128], BF16)
    wring3 = Ring(A, "w3", 2, [128, 8 * 128], BF16)
    wring4 = Ring(A, "w4", 2, [128, 4 * 128], BF16)
    sg_ring = Ring(A, "sg", 4, [128, BL], F32)
    M = A.alloc([128, 8, NT], BF16)
    w_co_v = w_co.rearrange("(c p) n -> p c n", p=128)
    w_ao_v = w_ao.rearrange("(c p) n -> p c n", p=128)
    w_mo_v = w_mo.rearrange("(c p) n -> p c n", p=128)
    for c in range(8):
        wga, wgak = load_w_chunk(wring, GOFF + c * 128)
        wgb, wgbk = load_w_chunk(wring2, GOFF + 1024 + c * 128)
        wc, wck = wring3.next()
        wc = wc[:].rearrange("p (c n) -> p c n", c=8)
        pool_load(wc, w_co_v[:, :, c * 128:(c + 1) * 128], wck, "%s%d" % wck)
        wa, wak = wring4.next()
        wa = wa[:].rearrange("p (c n) -> p c n", c=4)
        pool_load(wa, w_ao_v[:, :, c * 128:(c + 1) * 128], wak, "%s%d" % wak)
        for b in range(NB):
            pga, pgak = ps_next()
            pyc, pyck = ps_next()
            pgb, pgbk = ps_next()
            pya, pyak = ps_next()
            for k in range(8):
                mm(pga[:], wga[:, k, :], U[:, k, blk(b)], k == 0, k == 7, [wgak, ("U", k, b)], [pgak])
            for k in range(8):
                mm(pyc[:], wc[:, k, :], CT[:, k, blk(b)], k == 0, k == 7, [wck, ("CT", k, b // 2)], [pyck])
            for k in range(8):
                mm(pgb[:], wgb[:, k, :], U[:, k, blk(b)], k == 0, k == 7, [wgbk, ("U", k, b)], [pgbk])
            for k in range(4):
                mm(pya[:], wa[:, k, :], OT[:, k, blk(b)], k == 0, k == 3, [wak, ("OT", k)], [pyak])
            sa, sak = sg_ring.next()
            sb, sbk = sg_ring.next()
            act(sa[:], pga[:], AF.Sigmoid, [pgak], [sak])
            act(sb[:], pgb[:], AF.Sigmoid, [pgbk], [sbk])
            tt("dve", sa[:], sa[:], pyc[:], ALU.mult, [sak, pyck], [sak])
            tt("dve", sb[:], sb[:], pya[:], ALU.mult, [sbk, pyak], [sbk])
            tt("dve", M[:, c, blk(b)], sa[:], sb[:], ALU.add, [sak, sbk], [("M", c, b)])
    for c in range(8):
        wm, wmk = wring.next()
        wm = wm[:].rearrange("p (c n) -> p c n", c=8)
        pool_load(wm, w_mo_v[:, :, c * 128:(c + 1) * 128], wmk, "%s%d" % wmk)
        for b in range(NB):
            ps, psk = ps_next()
            for k in range(8):
                mm(ps[:], wm[:, k, :], M[:, k, blk(b)], k == 0, k == 7, [wmk, ("M", k, b)], [psk])
            tt("dve", H[:, c, blk(b)], H[:, c, blk(b)], ps[:], ALU.add, [psk, ("H", c, b)], [("H", c, b)])
    S.barrier()
    A.off = mark

    checkpoint("mix")
    ffn(w2g, w2u, w2d, G_FFN2)
    checkpoint("ffn2")

    mark = A.off
    sq_ring = Ring(A, "sq", 2, [128, BL], BF16)
    rt_ring = Ring(A, "rt", 2, [128, BL], F32)
    rmsnorm_to_U(G_PLE, sq_ring, rt_ring)
    PB = A.alloc([128, 2, NT], BF16)
    pool_load(PB[:], pT.rearrange("(c p) t -> p c t", p=128), "PB", "pb")
    sg_ring = Ring(A, "sg", 3, [128, BL], F32)
    sqf_ring = Ring(A, "sqf", 2, [128, BL], BF16)
    rtf_ring = Ring(A, "rtf", 2, [128, BL], F32)
    w_pg_v = w_pg.rearrange("(c p) n -> p c n", p=128)
    w_pp_v = w_pp.rearrange("(c p) n -> p c n", p=128)
    WPG = A.alloc([128, 8, 8 * 128], BF16)
    WPP = A.alloc([128, 8, 2 * 128], BF16)
    for c in range(8):
        pool_load(WPG[:, c, :].rearrange("p (k n) -> p k n", k=8), w_pg_v[:, :, c * 128:(c + 1) * 128], ("wpg", c), "wpg%d" % c)
        pool_load(WPP[:, c, :].rearrange("p (k n) -> p k n", k=2), w_pp_v[:, :, c * 128:(c + 1) * 128], ("wpp", c), "wpp%d" % c)
    yv = yT.rearrange("(c p) t -> p c t", p=128)
    for b in range(NB):
        for c in range(8):
            wg_ = WPG[:, c, :].rearrange("p (k n) -> p k n", k=8)
            wp = WPP[:, c, :].rearrange("p (k n) -> p k n", k=2)
            pg, pgk = ps_next()
            pp, ppk = ps_next()
            for k in range(8):
                mm(pg[:], wg_[:, k, :], U[:, k, blk(b)], k == 0, k == 7, [("wpg", c), ("U", k, b)], [pgk])
            for k in range(2):
                mm(pp[:], wp[:, k, :], PB[:, k, blk(b)], k == 0, k == 1, [("wpp", c), "PB"], [ppk])
            sg, sgk = sg_ring.next()
            act(sg[:], pg[:], AF.Sigmoid, [pgk], [sgk])
            tt("dve", sg[:], sg[:], pp[:], ALU.mult, [sgk, ppk], [sgk])
            tt("dve", H[:, c, blk(b)], H[:, c, blk(b)], sg[:], ALU.add, [sgk, ("H", c, b)], [("H", c, b)])
        ps, psk = ps_next()
        for c in range(8):
            sq, sqk = sqf_ring.next()
            act(sq[:], H[:, c, blk(b)], AF.Square, [("H", c, b)], [sqk])
            mm(ps[:], ones[:], sq[:], c == 0, c == 7, [sqk, "ones"], [psk])
        rt, rtk = rtf_ring.next()
        act(rt[:], ps[:], AF.Sqrt, [psk, "sm"], [rtk], bias=epsc, scale=1.0 / D)
        S.op("dve", lambda e, rt=rt: e.reciprocal(out=rt[:], in_=rt[:]), [rtk], [rtk])
        for c in range(8):
            stt("dve", H[:, c, blk(b)], H[:, c, blk(b)], gains[:, G_FIN, c:c + 1], rt[:],
                ALU.mult, ALU.mult, [("H", c, b), rtk, "gains"], [("H", c, b)])
            S.op("sp", lambda e, c=c, b=b: e.dma_start(out=yv[:, c, blk(b)], in_=H[:, c, blk(b)]),
                 reads=[("H", c, b)], dma="yout")
    S.finish()
    S.emit()


_CACHE = {}


def _consts():
    ident = np.eye(128, dtype=np.float32)
    perm = np.zeros((128, 128), np.float32)
    for m in range(128):
        perm[(m + 64) % 128, m] = 1.0
    ones = np.ones((128, 128), np.float32)
    cst = np.stack([ident, perm, ones, ones], axis=1)
    k = np.arange(128)[:, None]
    q = np.arange(128)[None, :]
    mprev = np.where(k >= q, 0.0, NEGM).astype(np.float32)
    mcur = np.where(k <= q, 0.0, NEGM).astype(np.float32)
    mask2 = np.concatenate([mprev, mcur], axis=1)
    invf = (10000.0 ** (-np.arange(0, 128, 2, dtype=np.float32) / 128)).astype(np.float32)
    invf2 = np.concatenate([invf, invf])
    sgn = np.concatenate([-np.ones(64, np.float32), np.ones(64, np.float32)])
    sm = np.stack([invf2, sgn, np.full(128, EPS, np.float32), np.zeros(128, np.float32)], axis=1)
    return cst, mask2, sm.astype(np.float32)


def _pc(v):
    return np.ascontiguousarray(np.asarray(v, np.float32).reshape(8, 128).T)


def kernel(x, p, positions, g_ffn1, w_ffn1_gate, w_ffn1_up, w_ffn1_down, g_mix, w_in,
           w_dw, b_dw, g_conv_ln, b_conv_ln, w_conv_out, w_attn_out, w_mix_out,
           g_ffn2, w_ffn2_gate, w_ffn2_up, w_ffn2_down, g_ple, w_ple_gate, w_ple_proj,
           g_final):
    x = np.asarray(x, np.float32)
    p = np.asarray(p, np.float32)
    positions = np.asarray(positions, np.int32)
    if "nc" not in _CACHE:
        _CACHE["nc"] = build_program()
    nc = _CACHE["nc"]
    cst, mask2, sm = _consts()
    gains = np.stack([_pc(g_ffn1[0]), _pc(g_mix[0]), _pc(g_conv_ln[0]), _pc(b_conv_ln[0]), _pc(b_dw[0]),
                      _pc(g_ffn2[0]), _pc(g_ple[0]), _pc(g_final)], axis=1)
    wdw = np.ascontiguousarray(np.asarray(w_dw[0], np.float32).reshape(31, 8, 128).transpose(2, 1, 0))
    f32 = lambda a: np.ascontiguousarray(np.asarray(a, np.float32))
    shared = {
        "w1g": f32(w_ffn1_gate[0]), "w1u": f32(w_ffn1_up[0]), "w1d": f32(w_ffn1_down[0]),
        "w2g": f32(w_ffn2_gate[0]), "w2u": f32(w_ffn2_up[0]), "w2d": f32(w_ffn2_down[0]),
        "w_in": f32(w_in[0]), "w_co": f32(w_conv_out[0]), "w_ao": f32(w_attn_out[0]), "w_mo": f32(w_mix_out[0]),
        "w_pg": f32(w_ple_gate[0]), "w_pp": f32(w_ple_proj[0]),
        "gains": f32(gains), "wdw": f32(wdw), "cst": f32(cst), "sm": f32(sm),
    }
    in_maps = []
    for core in range(8):
        b, half = core // 2, core % 2
        t0 = half * NT
        xo = np.ascontiguousarray(x[b, t0:t0 + NT, :].T)
        if half == 1:
            xh = np.ascontiguousarray(x[b, 0:NT, :].T)
            ph = positions[b, 0:NT]
        else:
            xh = np.zeros((D, NT), np.float32)
            ph = np.zeros(NT, np.int32)
        pos = np.concatenate([ph, positions[b, t0:t0 + NT]]).astype(np.int32)
        pos = np.ascontiguousarray(np.broadcast_to(pos[None, :], (128, 2 * NT)))
        maskH = mask2.copy()
        if half == 0:
            maskH[:, 0:128] = NEGM
        msk = np.ascontiguousarray(np.stack([mask2, maskH], axis=1))
        m = dict(shared)
        m.update({"xo": xo, "xh": xh, "pT": np.ascontiguousarray(p[0, b, t0:t0 + NT, :].T), "pos": pos, "msk": msk})
        in_maps.append(m)
    import os
    if os.environ.get("KONE"):
        k1 = int(os.environ["KONE"])
        r1 = run_bass_kernel_spmd(nc, [in_maps[k1]], core_ids=[0], trace=bool(os.environ.get("KTRACE")))
        if os.environ.get("KTRACE"):
            print("EXEC_TIME_NS", r1.exec_time_ns)

        class _R:
            results = [r1.results[0] if c == k1 else {"yT": np.zeros((D, NT), np.float32)} for c in range(8)]
        res = _R()
    else:
        res = run_bass_kernel_spmd(nc, in_maps, core_ids=list(range(8)))
    out = np.empty((4, 4096, D), np.float32)
    for core in range(8):
        b, half = core // 2, core % 2
        out[b, half * NT:(half + 1) * NT, :] = res.results[core]["yT"].T
    return out
```
